# Optimizing a Trainium2 kernel written in Bass

```python
import math
import jax
import jax.numpy as jnp
from jax import lax
import numpy as np

D_MODEL = 1024
BATCH = 2
SEQ = 8192
DEPTH = 2

CTX_LEN = 256
GRID_W = 64
MIX_W = D_MODEL
FOURIER_W = MIX_W // 4
FOURIER_GROUPS = 4
FOURIER_GD = FOURIER_W // FOURIER_GROUPS
POOL_W = MIX_W // 4
POOL_WINDOWS = (2, 4, 8, 16)
POOL_GD = POOL_W // len(POOL_WINDOWS)
ATTN_W = MIX_W - FOURIER_W - POOL_W
QK_DIM = 64
V_DIM = 2 * QK_DIM
N_HEADS = ATTN_W // V_DIM
QK_W = N_HEADS * 2 * QK_DIM
Q_BLOCK = 128
ROPE_THETA = 10000.0
EPS = 1e-6

A_OFF = 0
B_OFF = A_OFF + FOURIER_W
Q_OFF = B_OFF + POOL_W
K_OFF = Q_OFF + QK_W
V_OFF = K_OFF + QK_W
G_OFF = V_OFF + ATTN_W
IN_W = G_OFF + MIX_W

kernel_name = "hybrid_fourier_pool_diffattn_dit"


def rms_norm(t, g):
    tf = t.astype(jnp.float32)
    y = tf * lax.rsqrt(jnp.mean(tf * tf, axis=-1, keepdims=True) + EPS)
    return (y * g.astype(jnp.float32)).astype(t.dtype)


def axial_rope_tables(rows):
    row = jnp.repeat(jnp.arange(rows), GRID_W).astype(jnp.float32)
    col = jnp.tile(jnp.arange(GRID_W), rows).astype(jnp.float32)
    half = QK_DIM // 2
    inv_freq = ROPE_THETA ** (-jnp.arange(0, half, 2, dtype=jnp.float32) / half)
    ang_r = row[:, None] * inv_freq[None, :]
    ang_c = col[:, None] * inv_freq[None, :]
    ang = jnp.concatenate([ang_r, ang_r, ang_c, ang_c], axis=-1)
    return jnp.cos(ang), jnp.sin(ang)


def apply_rope(t, cos, sin):
    tr = t.reshape(t.shape[:-1] + (2, 2, QK_DIM // 4))
    rot = jnp.stack([-tr[..., 1, :], tr[..., 0, :]], axis=-2).reshape(t.shape)
    return (t * cos + rot * sin).astype(t.dtype)


def centred_window_mean(u, w):
    L = u.shape[1]
    lo = w // 2
    hi = w - lo - 1
    cs = jnp.pad(jnp.cumsum(u.astype(jnp.float32), axis=1), ((0, 0), (1, 0), (0, 0)))
    t = jnp.arange(L)
    a = jnp.clip(t - lo, 0, L - 1)
    b = jnp.clip(t + hi, 0, L - 1)
    s = cs[:, b + 1] - cs[:, a]
    cnt = (b - a + 1).astype(jnp.float32)
    return (s / cnt[None, :, None]).astype(u.dtype)


def fourier_mix(a, w_fourier):
    B_, L = a.shape[:2]
    ag = a.reshape(B_, L, FOURIER_GROUPS, FOURIER_GD).astype(jnp.float32)
    f = jnp.fft.fftn(ag, axes=(1, 3), norm='ortho').real.astype(a.dtype)
    y = jnp.einsum('blgc,gcd->blgd', f, w_fourier)
    return y.reshape(B_, L, FOURIER_W)


def pool_mix(b, w_pool, pool_scale):
    B_, L = b.shape[:2]
    bg = b.reshape(B_, L, len(POOL_WINDOWS), POOL_GD)
    pooled = jnp.stack([centred_window_mean(bg[:, :, gi], w) for gi, w in enumerate(POOL_WINDOWS)], axis=2) - bg
    y = jnp.einsum('blgc,gcd->blgd', pooled, w_pool).reshape(B_, L, POOL_W)
    return y * pool_scale


def split_heads_qk(t):
    B_, L = t.shape[:2]
    return t.reshape(B_, L, N_HEADS, 2, QK_DIM).transpose(0, 2, 3, 1, 4)


def split_heads_v(t):
    B_, L = t.shape[:2]
    return t.reshape(B_, L, N_HEADS, V_DIM).transpose(0, 2, 1, 3)


def diff_attend(q, k, v, lam):
    s = jnp.einsum('bhmqd,bhmkd->bhmqk', q, k).astype(jnp.float32) * (QK_DIM ** -0.5)
    p = jax.nn.softmax(s, axis=-1)
    p = p[:, :, 0] - lam * p[:, :, 1]
    return jnp.einsum('bhqk,bhkd->bhqd', p.astype(v.dtype), v)


def attn_post(o, subln_g, lam_init):
    o = rms_norm(o, subln_g) * (1.0 - lam_init)
    B_, H, L, dv = o.shape
    return o.transpose(0, 2, 1, 3).reshape(B_, L, H * dv)


def mixer_out(a_in, b_in, attn, gate, w_fourier, w_pool, pool_scale, w_out):
    y = jnp.concatenate([fourier_mix(a_in, w_fourier), pool_mix(b_in, w_pool, pool_scale), attn], axis=-1)
    return (y * jax.nn.silu(gate)) @ w_out


def mixer_layer(layer_idx, x, ctx, c, c_ctx, norm_g, w_mod, b_mod, w_in, w_fourier, w_pool,
                pool_scale, qk_norm_g, lam_vecs, subln_g, w_out, rope_cos, rope_sin, update_ctx):
    shift, scale, gate = jnp.split(jax.nn.silu(c) @ w_mod + b_mod, 3, axis=-1)
    shift_c, scale_c, gate_c = jnp.split(jax.nn.silu(c_ctx) @ w_mod + b_mod, 3, axis=-1)
    h = rms_norm(x, norm_g) * (1 + scale[:, None]) + shift[:, None]
    hc = rms_norm(ctx, norm_g) * (1 + scale_c) + shift_c

    lam_init = 0.8 - 0.6 * math.exp(-0.3 * layer_idx)
    lv = lam_vecs.astype(jnp.float32)
    lam = jnp.exp(jnp.sum(lv[0] * lv[1])) - jnp.exp(jnp.sum(lv[2] * lv[3])) + lam_init

    p = h @ w_in
    a_in, b_in = p[..., A_OFF:B_OFF], p[..., B_OFF:Q_OFF]
    q, k, v, g = p[..., Q_OFF:K_OFF], p[..., K_OFF:V_OFF], p[..., V_OFF:G_OFF], p[..., G_OFF:]

    if update_ctx:
        pc = hc @ w_in
        kc, vc = pc[..., K_OFF:V_OFF], pc[..., V_OFF:G_OFF]
    else:
        pkv = hc @ w_in[:, K_OFF:G_OFF]
        kc, vc = pkv[..., :QK_W], pkv[..., QK_W:]

    q_lat = apply_rope(rms_norm(split_heads_qk(q), qk_norm_g[0]), rope_cos, rope_sin)
    k_lat = apply_rope(rms_norm(split_heads_qk(k), qk_norm_g[1]), rope_cos, rope_sin)
    k_ctx = rms_norm(split_heads_qk(kc), qk_norm_g[1])
    v_ctx = split_heads_v(vc)
    k_all = jnp.concatenate([k_ctx, k_lat], axis=3)
    v_all = jnp.concatenate([v_ctx, split_heads_v(v)], axis=2)

    B_, H, _, L, d = q_lat.shape
    nb = L // Q_BLOCK
    qb = jnp.moveaxis(q_lat.reshape(B_, H, 2, nb, Q_BLOCK, d), 3, 0)
    ob = lax.map(lambda qblk: diff_attend(qblk, k_all, v_all, lam), qb)
    o = jnp.moveaxis(ob, 0, 2).reshape(B_, H, L, V_DIM)
    attn = attn_post(o, subln_g, lam_init)

    x_new = x + gate[:, None] * mixer_out(a_in, b_in, attn, g, w_fourier, w_pool, pool_scale, w_out)

    if update_ctx:
        qc = rms_norm(split_heads_qk(pc[..., Q_OFF:K_OFF]), qk_norm_g[0])
        attn_c = attn_post(diff_attend(qc, k_ctx, v_ctx, lam), subln_g, lam_init)
        ctx = ctx + gate_c * mixer_out(pc[..., A_OFF:B_OFF], pc[..., B_OFF:Q_OFF], attn_c,
                                       pc[..., G_OFF:], w_fourier, w_pool, pool_scale, w_out)
    return x_new, ctx


def setup_inputs(seed: int = 0) -> dict:
    key = jax.random.key(seed)
    ks = jax.random.split(key, 16)
    f32 = jnp.float32
    nrm = lambda k, shape: jax.random.normal(k, shape, dtype=f32)
    return {
        'x': nrm(ks[0], (BATCH, SEQ, D_MODEL)),
        'c': nrm(ks[1], (BATCH, D_MODEL)),
        'ctx': nrm(ks[2], (BATCH, CTX_LEN, D_MODEL)),
        'c_ctx': nrm(ks[3], (D_MODEL,)),
        'norm_g': 1.0 + 0.02 * nrm(ks[4], (DEPTH, D_MODEL)),
        'w_mod': nrm(ks[5], (DEPTH, D_MODEL, 3 * D_MODEL)) * D_MODEL ** -0.5,
        'b_mod': 0.02 * nrm(ks[6], (DEPTH, 3 * D_MODEL)),
        'w_in': nrm(ks[7], (DEPTH, D_MODEL, IN_W)) * D_MODEL ** -0.5,
        'w_fourier': nrm(ks[8], (DEPTH, FOURIER_GROUPS, FOURIER_GD, FOURIER_GD)) * FOURIER_GD ** -0.5,
        'w_pool': nrm(ks[9], (DEPTH, len(POOL_WINDOWS), POOL_GD, POOL_GD)) * POOL_GD ** -0.5,
        'pool_scale': 1.0 + 0.1 * nrm(ks[10], (DEPTH, POOL_W)),
        'qk_norm_g': 1.0 + 0.02 * nrm(ks[11], (DEPTH, 2, QK_DIM)),
        'lam_vecs': 0.1 * nrm(ks[12], (DEPTH, 4, QK_DIM)),
        'subln_g': 1.0 + 0.02 * nrm(ks[13], (DEPTH, V_DIM)),
        'w_out': nrm(ks[14], (DEPTH, MIX_W, D_MODEL)) * MIX_W ** -0.5,
    }


def reference(x, c, ctx, c_ctx, norm_g, w_mod, b_mod, w_in, w_fourier, w_pool, pool_scale,
              qk_norm_g, lam_vecs, subln_g, w_out):
    rows = x.shape[1] // GRID_W
    rope_cos, rope_sin = axial_rope_tables(rows)
    for l in range(DEPTH):
        x, ctx = mixer_layer(l, x, ctx, c, c_ctx, norm_g[l], w_mod[l], b_mod[l], w_in[l],
                             w_fourier[l], w_pool[l], pool_scale[l], qk_norm_g[l], lam_vecs[l],
                             subln_g[l], w_out[l], rope_cos, rope_sin, l < DEPTH - 1)
    return x
```

```python
import math
from contextlib import ExitStack
import numpy as np
import ml_dtypes
import concourse.bass as bass
import concourse.mybir as mybir
from concourse.bass_utils import run_bass_kernel_spmd

F32 = mybir.dt.float32
BF16 = mybir.dt.bfloat16
AF = mybir.ActivationFunctionType
ALU = mybir.AluOpType
AX = mybir.AxisListType
NPBF = ml_dtypes.bfloat16

D = 1024
L = 8192
T = 2048
NT = 16
CT = 256
TT = T + CT
PR = 1792
EPS = 1e-6
R_U, R_VF, R_B = 1024, 1280, 1536
DEBUG = False


class Buf:
    __slots__ = ("w", "r")

    def __init__(self):
        self.w = None
        self.r = {}


def bufs(n):
    return [Buf() for _ in range(n)]


class SemObj:
    __slots__ = ("h", "cum", "qwaited")

    def __init__(self, h):
        self.h = h
        self.cum = 0
        self.qwaited = 0


class Eng:
    ROLL = 30000

    def __init__(self, kb, eng, name, same=True, ndma=0):
        self.kb, self.eng, self.name, self.same = kb, eng, name, same
        self.own = set()
        self.waited = {}
        self.nins = 0
        self._newsem()
        self.dpool = [SemObj(kb.newsem("%s_d%d" % (name, i))) for i in range(ndma)]
        self.di = 0

    def _newsem(self):
        self.cur = SemObj(self.kb.newsem("%s_s%d" % (self.name, len(self.own))))
        self.own.add(self.cur)
        self.cnt = 0

    def _wait(self, toks):
        need = {}
        for (s, v) in toks:
            if s in self.own and not self.same:
                continue
            if need.get(s, 0) < v:
                need[s] = v
        for s, v in need.items():
            if self.waited.get(s, 0) >= v:
                continue
            self.eng.wait_ge(s.h, v)
            self.waited[s] = v

    @staticmethod
    def _deps(r, w):
        toks = []
        for b in r:
            if b.w is not None:
                toks.append(b.w)
        for b in w:
            if b.w is not None:
                toks.append(b.w)
            toks.extend(b.r.items())
        return toks

    @staticmethod
    def _mark(tok, r, w):
        for b in r:
            b.r[tok[0]] = tok[1]
        for b in w:
            b.w = tok
            b.r = {}

    def op(self, fn, r=(), w=()):
        self._wait(self._deps(r, w))
        ins = fn()
        if self.cnt >= self.ROLL:
            self._newsem()
        self.cnt += 1
        ins.then_inc(self.cur.h, 1)
        self.nins += 1
        self._mark((self.cur, self.cnt), r, w)

    def dma(self, out, in_, r=(), w=(), **kw):
        self._wait(self._deps(r, w))
        slot = self.dpool[self.di % len(self.dpool)]
        self.di += 1
        if slot.cum > slot.qwaited and self.waited.get(slot, 0) < slot.cum:
            self.eng.wait_ge(slot.h, slot.cum)
            self.waited[slot] = slot.cum
        slot.qwaited = slot.cum
        self.eng.dma_start(out=out, in_=in_, **kw).then_inc(slot.h, 16)
        slot.cum += 16
        self.nins += 1
        self._mark((slot, slot.cum), r, w)

    def tok(self):
        return (self.cur, self.cnt)


class KB:
    def __init__(self):
        self.nc = bass.Bass("TRN2", target_bir_lowering=False)
        self.es = ExitStack()
        nc = self.nc
        self.PE = Eng(self, nc.tensor, "pe", same=False)
        self.ACT = Eng(self, nc.scalar, "act", ndma=8)
        self.DVE = Eng(self, nc.vector, "dve")
        self.POOL = Eng(self, nc.gpsimd, "pool", ndma=12)
        self.SP = Eng(self, nc.sync, "sp", ndma=24)
        self.engs = [self.PE, self.ACT, self.DVE, self.POOL, self.SP]
        self.inputs = {}
        self.pfx = ""

    def newsem(self, name):
        return self.es.enter_context(self.nc.semaphore(name))

    def din(self, name, shape, dt):
        self.inputs[name] = (tuple(shape), dt)
        return self.nc.dram_tensor(name, list(shape), dt, kind="ExternalInput").ap()

    def dout(self, name, shape, dt):
        return self.nc.dram_tensor(name, list(shape), dt, kind="ExternalOutput").ap()

    def sb(self, name, shape, dt, st=None):
        return (st or self.es).enter_context(self.nc.sbuf_tensor("s_" + self.pfx + name, list(shape), dt))

    def ps(self, name, shape, dt, st=None):
        return (st or self.es).enter_context(self.nc.psum_tensor("p_" + self.pfx + name, list(shape), dt))

    def barrier(self):
        toks = []
        for e in self.engs:
            if e.cnt > 0:
                toks.append(e.tok())
            for s in e.dpool:
                if s.cum > 0:
                    toks.append((s, s.cum))
        for e in self.engs:
            sv, e.same = e.same, False
            e._wait(toks)
            e.same = sv


def build_fused(nlayers=2):
    kb = KB()
    nc = kb.nc
    PE, ACT, DVE, POOL, SP = kb.PE, kb.ACT, kb.DVE, kb.POOL, kb.SP
    V, S_, G = nc.vector, nc.scalar, nc.gpsimd
    NL = 2

    x_in = kb.din("x", [T, D], F32)
    ctx_in = kb.din("ctx", [CT, D], F32)
    cT_d = kb.din("cT", [128, 8, 2], F32)
    wmod_a = kb.din("w_mod", [NL, D, 3 * D], F32)
    bmod_a = kb.din("bmod_rep", [NL, 128, 3 * D], F32)
    normg_a = kb.din("normg_rep", [NL, 128, D], F32)
    win_a = kb.din("w_in", [NL, D, 3 * D], F32)
    qkg_a = kb.din("qkg_rep", [NL, 128, 2, 64], F32)
    wf_a = kb.din("wf", [NL, 64, 4, 64], F32)
    wout_a = kb.din("w_out", [NL, D, D], F32)
    wp_a = kb.din("w_pool", [NL, 4, 64, 64], F32)
    pscale_a = kb.din("pscale", [NL, 128, 2], F32)
    lamv_a = kb.din("lamv_rep", [NL, 128, 4, 64], F32)
    subg_a = kb.din("subg_rep", [NL, 128, 128], F32)
    ident_d = kb.din("ident", [128, 128], BF16)
    rope_d = kb.din("rope", [128, 2, NT, 64], F32)
    ccs_d = kb.din("ccs", [64, 2, 64], F32)
    dft_d = kb.din("dft128", [128, 3, 128], BF16)
    wb_d = kb.din("wbt", [128, 128, 16], BF16)
    invw_d = kb.din("invw", [128, 2], F32)
    efix_d = kb.din("efix", [128, 2, 16], F32)
    sel_d = kb.din("sel", [128, 8], F32)
    dftc_d = kb.din("dftc", [128, 2, 2, CT], BF16)
    efixc_d = kb.din("efixc", [128, 2, 16], F32)
    x_out = kb.dout("xo", [T, D], F32)
    NSL = PR // 256
    payloc_t = [nc.dram_tensor("payloc%d" % l, [NSL, 256, T], BF16) for l in range(NL)]
    paygat_t = [nc.dram_tensor("paygat%d" % l, [NSL, 1024, T], BF16) for l in range(NL)]
    x1_d = nc.dram_tensor("x1s", [T, D], F32).ap()
    ctx1_d = nc.dram_tensor("ctx1s", [CT, D], F32).ap()
    b_x1 = bufs(NT)
    b_ctx1 = bufs(2)
    ccsem = SemObj(kb.newsem("ccsem"))
    if DEBUG:
        dbg_o = kb.dout("dbg", [128, 8, TT], BF16)

    ident = kb.sb("ident", [128, 128], BF16); b_ident = Buf()
    negh = kb.sb("negh", [128, 32], F32); b_negh = Buf()
    SP.dma(ident[:, :], ident_d[:, :], w=[b_ident])
    POOL.op(lambda: G.memset(negh[:, :], -0.5), w=[b_negh])

    def tcols(ti):
        return slice(ti * 128, (ti + 1) * 128)

    tiles = list(range(NT + 2))
    ntl = len(tiles)
    lat = list(range(NT))
    ctxt = [NT, NT + 1]

    for layer in range(nlayers):
        upd = (layer == 0)
        last = (layer == nlayers - 1)
        lam_init = 0.8 - 0.6 * math.exp(-0.3 * layer)
        kb.pfx = "L%d_" % layer
        x_d = x_in if layer == 0 else x1_d
        ctx_d = ctx_in if layer == 0 else ctx1_d
        x_o = x_out if last else x1_d
        wmod_d, bmod_d, normg_d, win_d = wmod_a[layer], bmod_a[layer], normg_a[layer], win_a[layer]
        qkg_d, wf_d, wout_d, wp_d = qkg_a[layer], wf_a[layer], wout_a[layer], wp_a[layer]
        pscale_d, lamv_d, subg_d = pscale_a[layer], lamv_a[layer], subg_a[layer]
        pay_o = payloc_t[layer].ap().rearrange("s r t -> (s r) t")
        pay_g = paygat_t[layer].ap()

        def gsrc(r_, row0, n, pay_g=pay_g):
            s_, off = row0 // 256, row0 % 256
            assert off + n <= 256
            return pay_g[s_, r_ * 256 + off:r_ * 256 + off + n, :]
        b_pay = {i_: [] for i_ in range(NSL)}
        b_gat = bufs(NSL)

        def paybuf(*slices, b_pay=b_pay):
            b = Buf()
            for s_ in slices:
                b_pay[s_].append(b)
            return [b]

        def gb(row0, b_gat=b_gat):
            return [b_gat[row0 // 256]]

        def issue_cc(slices, extra=(), layer=layer, b_pay=b_pay, b_gat=b_gat):
            deps = [b for s_ in slices for b in b_pay[s_]] + list(extra)
            POOL._wait(POOL._deps(deps, [b_gat[s_] for s_ in slices]))
            for s_ in slices:
                cins = G.collective_compute("AllGather", ALU.bypass, replica_groups=[[0, 1, 2, 3], [4, 5, 6, 7]],
                                            ins=[payloc_t[layer].ap()[s_, :, :].opt()], outs=[paygat_t[layer].ap()[s_, :, :].opt()], dma_qos="P3")
                cins.then_inc(ccsem.h)
                ccsem.cum += 1
                POOL._mark((ccsem, ccsem.cum), b_pay[s_], [b_gat[s_]])

        def xsrc(ti):
            if ti < NT:
                return x_d[ti * 128:(ti + 1) * 128, :]
            return ctx_d[(ti - NT) * 128:(ti - NT + 1) * 128, :]

        def xbuf(ti):
            if layer == 0:
                return []
            return [b_x1[ti]] if ti < NT else [b_ctx1[ti - NT]]

        lst = ExitStack()
        modl = kb.sb("modl", [128, 3 * D], F32, lst); b_modl = Buf()
        modc = kb.sb("modc", [128, 3 * D], F32, lst); b_modc = Buf()
        rstd = kb.sb("rstd", [128, 32], F32, lst); b_rstd = Buf()
        qkg = kb.sb("qkg", [128, 2, 64], F32, lst); b_qkg = Buf()
        bd = kb.sb("bd", [128, 2, 256], BF16, lst); b_bd = Buf()
        bdc = kb.sb("bdc", [128, 2, 256], BF16, lst)
        sgT = kb.sb("sgT", [128, 8, TT], BF16, lst); b_sg = bufs(8)
        qT = kb.sb("qT", [128, 4, TT], BF16, lst); b_qT = Buf()
        kTc = kb.sb("kTc", [128, 4, CT], BF16, lst); b_kTc = Buf()
        vc = kb.sb("vc", [128, 2, 4, 128], BF16, lst); b_vc = Buf()
        bhs = kb.sb("bhs", [128, 2, T + 16], BF16, lst); b_bhs = Buf()
        bTc = kb.sb("bTc", [128, 2, CT + 16], BF16, lst); b_bTc = Buf()
        uvc = kb.sb("uvc", [128, 2, 512], BF16, lst); b_uvc = Buf()
        SP.dma(qkg[:, :, :], qkg_d[:, :, :], w=[b_qkg])

        with ExitStack() as st:
            ptr = [kb.ps("ptr%d" % i, [128, 1024], BF16, st) for i in range(2)]; b_ptr = bufs(2)
            pp = [kb.ps("pp%d" % i, [128, 512], F32, st) for i in range(3)]; b_pp = bufs(3)
            ptq = kb.ps("ptq", [128, 1024], BF16, st); b_ptq = Buf()
            puv = kb.ps("puv", [128, 512], F32, st); b_puv = Buf()
            ppi = [0]

            def next_pp():
                i = ppi[0] % 3
                ppi[0] += 1
                return pp[i], b_pp[i]

            wst = [kb.sb("wst%d" % i, [128, 8, 512], BF16, st) for i in range(2)]; b_wst = bufs(2)
            cg = kb.sb("cg", [128, 2, 2, NT, 64], F32, st); b_cg = Buf()
            stM = ExitStack()
            cT = kb.sb("cT", [128, 8, 2], F32, stM); b_cT = Buf()
            scT = kb.sb("scT", [128, 8, 2], BF16, stM)
            scb = kb.sb("scb", [128, 2, 8, 128], BF16, stM); b_scb = Buf()
            wmb = [kb.sb("wmb%d" % i, [128, 8, 512], BF16, stM) for i in range(2)]; b_wmb = bufs(2)
            wm32 = [kb.sb("wm32_%d" % i, [128, 8, 512], F32, stM) for i in range(2)]; b_wm32 = bufs(2)
            bmod = kb.sb("bmod", [128, 3 * D], F32, stM); b_bmod = Buf()
            normg = kb.sb("normg", [128, D], F32, stM); b_normg = Buf()
            rope = kb.sb("rope", [128, 2, NT, 64], F32, stM); b_rope = Buf()
            wf = kb.sb("wf", [64, 4, 64], F32, stM); b_wf = Buf()
            ccs = kb.sb("ccs", [64, 2, 64], F32, stM); b_ccs = Buf()
            bdz = kb.sb("bdz", [128, 2, 256], F32, stM); b_bdz = Buf()

            SP.dma(cT[:, :, :], cT_d[:, :, :], w=[b_cT])
            SP.dma(bmod[:, :], bmod_d[:, :], w=[b_bmod])
            SP.dma(normg[:, :], normg_d[:, :], w=[b_normg])
            SP.dma(rope[:, :, :, :], rope_d[:, :, :, :], w=[b_rope])
            SP.dma(wf[:, :, :], wf_d[:, :, :], w=[b_wf])
            SP.dma(ccs[:, :, :], ccs_d[:, :, :], w=[b_ccs])

            ACT.op(lambda: S_.activation(out=scT[:, :, :], in_=cT[:, :, :], func=AF.Silu), r=[b_cT], w=[b_scb])
            for r_ in range(2):
                DVE.op(lambda: V.tensor_copy(out=scb[:, r_, :, :], in_=scT[:, :, r_].unsqueeze(2).to_broadcast([128, 8, 128])),
                       r=[b_scb], w=[b_scb])
            mods = [modl, modc]
            b_mods = [b_modl, b_modc]
            MW = 512
            SP.dma(wm32[0][:, :, :], wmod_d[:, 0:MW].rearrange("(k p) c -> p k c", p=128), w=[b_wm32[0]])
            for j in range(3 * D // MW):
                if j + 1 < 3 * D // MW:
                    SP.dma(wm32[(j + 1) % 2][:, :, :], wmod_d[:, (j + 1) * MW:(j + 2) * MW].rearrange("(k p) c -> p k c", p=128),
                           w=[b_wm32[(j + 1) % 2]])
                ACT.op(lambda: S_.copy(out=wmb[j % 2][:, 0:4, :], in_=wm32[j % 2][:, 0:4, :]), r=[b_wm32[j % 2]], w=[b_wmb[j % 2]])
                DVE.op(lambda: V.tensor_copy(out=wmb[j % 2][:, 4:8, :], in_=wm32[j % 2][:, 4:8, :]), r=[b_wm32[j % 2]], w=[b_wmb[j % 2]])
                for r_ in range(2):
                    p_, bp_ = next_pp()
                    for k in range(8):
                        PE.op(lambda: nc.tensor.matmul(p_[:, 0:MW], lhsT=scb[:, r_, k, :], rhs=wmb[j % 2][:, k, :], start=(k == 0), stop=(k == 7)),
                              r=[b_scb, b_wmb[j % 2]], w=[bp_])
                    DVE.op(lambda: V.tensor_tensor(out=mods[r_][:, j * MW:(j + 1) * MW], in0=p_[:, 0:MW], in1=bmod[:, j * MW:(j + 1) * MW], op=ALU.add),
                           r=[bp_, b_bmod], w=[b_mods[r_]])
            for r_ in range(2):
                DVE.op(lambda: V.scalar_tensor_tensor(out=mods[r_][:, D:2 * D], in0=mods[r_][:, D:2 * D], scalar=1.0, in1=normg[:, :],
                                                      op0=ALU.add, op1=ALU.mult), r=[b_normg], w=[b_mods[r_]])

            for w_ in range(2):
                DVE.op(lambda: V.tensor_tensor(out=cg[:, w_, 0, :, :], in0=rope[:, 0, :, :],
                                               in1=qkg[:, w_, :].unsqueeze(1).to_broadcast([128, NT, 64]), op=ALU.mult),
                       r=[b_rope, b_qkg], w=[b_cg])
                for j in range(2):
                    o_ = cg[:, w_, 1, :, :].rearrange("p t (a j i) -> p t a j i", a=2, j=2)[:, :, :, j, :]
                    i0 = rope[:, 1, :, :].rearrange("p t (a j i) -> p t a j i", a=2, j=2)[:, :, :, j, :]
                    i1 = qkg[:, w_, :].rearrange("p (a j i) -> p a j i", a=2, j=2)[:, :, 1 - j, :].unsqueeze(1).to_broadcast([128, NT, 2, 16])
                    DVE.op(lambda: V.tensor_tensor(out=o_, in0=i0, in1=i1, op=ALU.mult), r=[b_rope, b_qkg], w=[b_cg])

            fsc = 1.0 / math.sqrt(L * 64.0)
            DVE.op(lambda: V.memset(bdz[:, :, :], 0.0), w=[b_bdz])
            for g in range(4):
                j, g2 = g // 2, g % 2
                for uv in range(2):
                    p_, bp_ = next_pp()
                    PE.op(lambda: nc.tensor.matmul(p_[g2 * 64:(g2 + 1) * 64, 0:64], lhsT=ccs[:, uv, :], rhs=wf[:, g, :], start=True, stop=True),
                          r=[b_ccs, b_wf], w=[bp_])
                    DVE.op(lambda: V.tensor_copy(out=bdz[g2 * 64:(g2 + 1) * 64, j, g2 * 128 + uv * 64:g2 * 128 + (uv + 1) * 64],
                                                 in_=p_[g2 * 64:(g2 + 1) * 64, 0:64]), r=[bp_], w=[b_bdz])
            ACT.op(lambda: S_.activation(out=bd[:, :, :], in_=bdz[:, :, :], func=AF.Copy, scale=fsc), r=[b_bdz], w=[b_bd])
            if upd:
                ACT.op(lambda: S_.activation(out=bdc[:, :, :], in_=bdz[:, :, :], func=AF.Copy, scale=1.0 / math.sqrt(CT * 64.0)),
                       r=[b_bdz], w=[b_bd])

            stM.close()
            kb.barrier()
            hT = kb.sb("hT", [128, 8, TT], BF16, st); b_hT = bufs(NT + 2)
            xt = [kb.sb("xt%d" % i, [128, D], F32, st) for i in range(2)]; b_xt = bufs(2)
            ss = kb.sb("ss", [128, 32], F32, st); b_ss = Buf()
            hb = [kb.sb("hb%d" % i, [128, D], BF16, st) for i in range(2)]; b_hb = bufs(2)
            junk = hb[0]; b_junk = b_hb[0]
            scr = kb.sb("scr", [128, 2, 4, 512], F32, st)
            b_scr = [bufs(4) for _ in range(2)]
            b_sq, b_t1 = b_scr[0][0], b_scr[0][1]
            tmpf = scr[:, 0, 0:2, :].rearrange("p a b -> p (a b)")
            ssg = kb.sb("ssg", [128, 2, 32], F32, st); b_ssgs = bufs(2)
            qb = [kb.sb("qb%d" % i, [128, 512], BF16, st) for i in range(2)]; b_qb = bufs(2)
            vst = [kb.sb("vst%d" % i, [128, 512], BF16, st) for i in range(2)]; b_vst = bufs(2)
            kst = [kb.sb("kst%d" % i, [128, 4, 128], BF16, st) for i in range(2)]; b_kst = bufs(2)
            ust = [kb.sb("ust%d" % i, [128, 2, 4, 64], BF16, st) for i in range(2)]; b_ust = bufs(2)
            aT = kb.sb("aT", [128, 2, 512], BF16, st); b_aT = Buf()
            print("P1 sbuf remaining", nc.sbuf_bytes_remaining)
            xts = [xt[0][:, :], xt[1][:, :], scr[:, 1, 0:2, :].rearrange("p a b -> p (a b)"), scr[:, 1, 2:4, :].rearrange("p a b -> p (a b)")]
            b_xts = [[b_xt[0]], [b_xt[1]], [b_scr[1][0], b_scr[1][1]], [b_scr[1][2], b_scr[1][3]]]

            for n, ti in enumerate(tiles):
                SP.dma(xts[n % 4], xsrc(ti), r=xbuf(ti), w=b_xts[n % 4])
                ACT.op(lambda: S_.activation(out=junk[:, :], in_=xts[n % 4], func=AF.Square, accum_out=ss[:, n:n + 1]),
                       r=b_xts[n % 4], w=[b_junk, b_ss])
            DVE.op(lambda: V.tensor_scalar(out=ss[:, 0:ntl], in0=ss[:, 0:ntl], scalar1=1.0 / D, scalar2=EPS, op0=ALU.mult, op1=ALU.add),
                   r=[b_ss], w=[b_ss])
            POOL.op(lambda: G.tensor_tensor(out=rstd[:, 0:ntl], in0=ss[:, 0:ntl], in1=negh[:, 0:ntl], op=ALU.pow), r=[b_ss, b_negh], w=[b_rstd])

            for n, ti in enumerate(tiles):
                m_ = modl if ti < NT else modc
                bm_ = b_modl if ti < NT else b_modc
                xb = xts[n % 4]
                SP.dma(xb, xsrc(ti), r=xbuf(ti), w=b_xts[n % 4])
                DVE.op(lambda: V.scalar_tensor_tensor(out=tmpf, in0=xb, scalar=rstd[:, n:n + 1], in1=m_[:, D:2 * D],
                                                      op0=ALU.mult, op1=ALU.mult), r=b_xts[n % 4] + [b_rstd, bm_], w=[b_sq, b_t1])
                POOL.op(lambda: G.tensor_tensor(out=hb[n % 2][:, :], in0=tmpf, in1=m_[:, 0:D], op=ALU.add),
                        r=[b_sq, b_t1, bm_], w=[b_hb[n % 2]])
                for k in range(8):
                    PE.op(lambda: nc.tensor.transpose(out=ptr[n % 2][:, k * 128:(k + 1) * 128], in_=hb[n % 2][:, k * 128:(k + 1) * 128],
                                                      identity=ident[:, :]), r=[b_hb[n % 2], b_ident], w=[b_ptr[n % 2]])
                ACT.op(lambda: S_.copy(out=hT[:, :, tcols(ti)], in_=ptr[n % 2][:, :].rearrange("p (k t) -> p k t", k=8)),
                       r=[b_ptr[n % 2]], w=[b_hT[ti]])

            def load_w(cb, slot):
                POOL.dma(wst[slot][:, :, :], win_d[:, cb * 512:(cb + 1) * 512].rearrange("(k p) c -> p k c", p=128), w=[b_wst[slot]])

            def tok_major(cb, ti, slot):
                p_, bp_ = next_pp()
                for k in range(8):
                    PE.op(lambda: nc.tensor.matmul(p_[:, :], lhsT=hT[:, k, tcols(ti)], rhs=wst[slot][:, k, :], start=(k == 0), stop=(k == 7)),
                          r=[b_hT[ti], b_wst[slot]], w=[bp_])
                return p_, bp_

            qkn = [0]

            def qk_post(p_, bp_, ti, which, dest, bdest, after=None):
                n = qkn[0]
                qkn[0] += 1
                q_ = qb[n % 2]
                bq_ = b_qb[n % 2]
                sq, t1, uu, ww = scr[:, n % 2, 0, :], scr[:, n % 2, 1, :], scr[:, n % 2, 2, :], scr[:, n % 2, 3, :]
                b_sq, b_t1, b_uu, b_ww = b_scr[n % 2]
                sg_ = ssg[:, n % 2, :]
                b_ssg = b_ssgs[n % 2]
                ACT.op(lambda: S_.activation(out=sq, in_=p_[:, :], func=AF.Square), r=[bp_], w=[b_sq])
                DVE.op(lambda: V.tensor_reduce(out=sg_[:, 0:8], in_=sq.rearrange("p (g e) -> p g e", e=64), axis=AX.X, op=ALU.add),
                       r=[b_sq], w=[b_ssg])
                DVE.op(lambda: V.tensor_scalar(out=sg_[:, 8:16], in0=sg_[:, 0:8], scalar1=1.0 / 64, scalar2=EPS, op0=ALU.mult, op1=ALU.add),
                       r=[b_ssg], w=[b_ssg])
                ACT.op(lambda: S_.activation(out=sg_[:, 24:32], in_=sg_[:, 8:16], func=AF.Sqrt), r=[b_ssg], w=[b_ssg])
                DVE.op(lambda: V.reciprocal(out=sg_[:, 16:24], in_=sg_[:, 24:32]), r=[b_ssg], w=[b_ssg])
                t3 = t1.rearrange("p (g e) -> p g e", e=64)
                DVE.op(lambda: V.tensor_tensor(out=t3, in0=p_[:, :].rearrange("p (g e) -> p g e", e=64),
                                               in1=sg_[:, 16:24].unsqueeze(2).to_broadcast([128, 8, 64]), op=ALU.mult),
                       r=[bp_, b_ssg], w=[b_t1])
                if ti < NT:
                    POOL.op(lambda: G.tensor_tensor(out=uu.rearrange("p (g e) -> p g e", e=64), in0=t3,
                                                    in1=cg[:, which, 0, ti, :].unsqueeze(1).to_broadcast([128, 8, 64]), op=ALU.mult),
                            r=[b_t1, b_cg], w=[b_uu])
                    t5 = t1.rearrange("p (g a j i) -> p g a j i", a=2, j=2, i=16)
                    w5 = ww.rearrange("p (g a j i) -> p g a j i", a=2, j=2, i=16)
                    s4 = cg[:, which, 1, ti, :].rearrange("p (a j i) -> p a j i", a=2, j=2)
                    for j in range(2):
                        DVE.op(lambda: V.tensor_tensor(out=w5[:, :, :, j, :], in0=t5[:, :, :, 1 - j, :],
                                                       in1=s4[:, :, j, :].unsqueeze(1).to_broadcast([128, 8, 2, 16]), op=ALU.mult),
                               r=[b_t1, b_cg], w=[b_ww])
                    POOL.op(lambda: G.tensor_tensor(out=q_[:, :], in0=uu, in1=ww, op=ALU.add), r=[b_uu, b_ww], w=[bq_])
                else:
                    DVE.op(lambda: V.tensor_tensor(out=q_[:, :].rearrange("p (g e) -> p g e", e=64), in0=t3,
                                                   in1=qkg[:, which, :].unsqueeze(1).to_broadcast([128, 8, 64]), op=ALU.mult),
                           r=[b_t1, b_qkg], w=[bq_])

                def fin():
                    for h in range(4):
                        PE.op(lambda: nc.tensor.transpose(out=ptq[:, h * 128:(h + 1) * 128], in_=q_[:, h * 128:(h + 1) * 128], identity=ident[:, :]),
                              r=[bq_, b_ident], w=[b_ptq])
                    ACT.op(lambda: S_.copy(out=dest, in_=ptq[:, 0:512].rearrange("p (h t) -> p h t", h=4)), r=[b_ptq], w=[bdest])
                    if after is not None:
                        after()
                return fin

            cx = ctxt if upd else []
            sched = [(0, lat + cx), (2, lat + ctxt), (3, lat + ctxt), (1, lat + cx), (4, lat + cx), (5, lat + cx)]
            cc_of = {0: [4, 5], 3: [6, 0, 1]}
            cc_pending = []
            load_w(sched[0][0], 0)
            uvn = [0]
            for si, (cb, tl) in enumerate(sched):
                slot = si % 2
                if si + 1 < len(sched):
                    load_w(sched[si + 1][0], (si + 1) % 2)
                if si >= 1 and cc_pending:
                    issue_cc(cc_pending.pop(0))
                if cb in cc_of:
                    cc_pending.append(cc_of[cb])
                if cb in (0, 4, 5):
                    latg = [t for t in tl if t < NT]
                    groups = [latg[i:i + 4] for i in range(0, len(latg), 4)]
                    cg_ = [t for t in tl if t >= NT]
                    if cg_:
                        groups.append(cg_)
                    for grp in groups:
                        c0 = grp[0] * 128
                        ncol = len(grp) * 128
                        for cc in range(4):
                            p_, bp_ = next_pp()
                            for k in range(8):
                                PE.op(lambda: nc.tensor.matmul(p_[:, 0:ncol], lhsT=wst[slot][:, k, cc * 128:(cc + 1) * 128], rhs=hT[:, k, c0:c0 + ncol],
                                                               start=(k == 0), stop=(k == 7)), r=[b_hT[t] for t in grp] + [b_wst[slot]], w=[bp_])
                            if cb >= 4:
                                ch = (cb - 4) * 4 + cc
                                ACT.op(lambda: S_.activation(out=sgT[:, ch, c0:c0 + ncol], in_=p_[:, 0:ncol], func=AF.Silu), r=[bp_], w=[b_sg[ch]])
                            elif cc < 2:
                                ACT.op(lambda: S_.copy(out=aT[:, cc, 0:ncol], in_=p_[:, 0:ncol]), r=[bp_], w=[b_aT])
                            else:
                                if grp[0] < NT:
                                    ACT.op(lambda: S_.copy(out=bhs[:, cc - 2, 8 + c0:8 + c0 + ncol], in_=p_[:, 0:ncol]), r=[bp_], w=[b_bhs])
                                else:
                                    ACT.op(lambda: S_.copy(out=bTc[:, cc - 2, 8:8 + CT], in_=p_[:, 0:ncol]), r=[bp_], w=[b_bTc])
                        if cb == 0:
                            for gi, ti in enumerate(grp):
                                n = uvn[0]
                                uvn[0] += 1
                                isc = ti >= NT
                                for j in range(2):
                                    PE.op(lambda: nc.tensor.matmul(puv[:, j * 256:(j + 1) * 256], lhsT=aT[:, j, gi * 128:(gi + 1) * 128],
                                                                   rhs=(bdc if isc else bd)[:, j, :], start=True, stop=True),
                                          r=[b_aT, b_bd], w=[b_puv])
                                if isc:
                                    ACT.op(lambda: S_.copy(out=uvc[:, ti - NT, :], in_=puv[:, :]), r=[b_puv], w=[b_uvc])
                                else:
                                    u_ = ust[n % 2]
                                    for uv in range(2):
                                        src = puv[:, :].rearrange("p (g uv d) -> p g uv d", g=4, uv=2)[:, :, uv, :]
                                        if uv == 0:
                                            ACT.op(lambda: S_.copy(out=u_[:, uv, :, :], in_=src), r=[b_puv], w=[b_ust[n % 2]])
                                        else:
                                            DVE.op(lambda: V.tensor_copy(out=u_[:, uv, :, :], in_=src), r=[b_puv], w=[b_ust[n % 2]])
                                    for uv, r0 in ((0, R_U), (1, R_VF)):
                                        dst = pay_o[r0:r0 + 256, :].rearrange("(g r) (q d) -> g (r q) d", g=4, d=64)[:, ti * 128:(ti + 1) * 128, :]
                                        SP.dma(dst.rearrange("g t d -> t g d"), u_[:, uv, :, :], r=[b_ust[n % 2]], w=paybuf(r0 // 256))
                    if cb == 0:
                        SP.dma(pay_o[R_B:R_B + 256, :].rearrange("(j p) t -> p j t", p=128), bhs[:, :, 8:8 + T], r=[b_bhs], w=paybuf(R_B // 256))
                else:
                    qfin = None
                    for ti in tl:
                        p_, bp_ = tok_major(cb, ti, slot)
                        if cb == 3:
                            if ti < NT:
                                n = ti
                                ACT.op(lambda: S_.copy(out=vst[n % 2][:, :], in_=p_[:, :]), r=[bp_], w=[b_vst[n % 2]])
                                dst = pay_o[0:1024, :].rearrange("(h r) t -> h r t", h=4)[:, 128:256, :].rearrange("h r (q c) -> h (r q) c", c=128)
                                dst = dst[:, ti * 128:(ti + 1) * 128, :].rearrange("h t c -> t h c")
                                SP.dma(dst, vst[n % 2][:, :].rearrange("p (h c) -> p h c", h=4), r=[b_vst[n % 2]], w=paybuf(0, 1, 2, 3))
                            else:
                                ACT.op(lambda: S_.copy(out=vc[:, ti - NT, :, :], in_=p_[:, :].rearrange("p (h e) -> p h e", h=4)), r=[bp_], w=[b_vc])
                        elif cb == 1:
                            nf = qk_post(p_, bp_, ti, 0, qT[:, :, tcols(ti)], b_qT)
                        else:
                            if ti < NT:
                                def kdma(ti=ti):
                                    SP.dma(pay_o[0:1024, tcols(ti)].rearrange("(h r) t -> r h t", h=4)[0:128, :, :], kst[ti % 2][:, :, :],
                                           r=[b_kst[ti % 2]], w=paybuf(0, 1, 2, 3))
                                nf = qk_post(p_, bp_, ti, 1, kst[ti % 2][:, :, :], b_kst[ti % 2], after=kdma)
                            else:
                                nf = qk_post(p_, bp_, ti, 1, kTc[:, :, (ti - NT) * 128:(ti - NT + 1) * 128], b_kTc)
                        if cb in (1, 2):
                            if qfin is not None:
                                qfin()
                            qfin = nf
                    if qfin is not None:
                        qfin()
            while cc_pending:
                issue_cc(cc_pending.pop(0))
            kb.barrier()

        lam = kb.sb("lam", [128, 8], F32, lst); b_lam = Buf()
        wo = kb.sb("wo", [128, 8, D], BF16, lst); b_wo = Buf()
        subg = kb.sb("subg", [128, 128], F32, lst); b_subg = Buf()

        with ExitStack() as st:
            lv = kb.sb("lv", [128, 4, 64], F32, st); b_lv = Buf()
            lp = kb.sb("lp", [128, 2, 64], F32, st)
            SP.dma(lv[:, :, :], lamv_d[:, :, :], w=[b_lv])
            SP.dma(subg[:, :], subg_d[:, :], w=[b_subg])
            DVE.op(lambda: V.tensor_tensor(out=lp[:, :, :], in0=lv[:, :, :].rearrange("p (a b) e -> p a b e", b=2)[:, :, 0, :],
                                           in1=lv[:, :, :].rearrange("p (a b) e -> p a b e", b=2)[:, :, 1, :], op=ALU.mult), r=[b_lv], w=[b_lv])
            DVE.op(lambda: V.tensor_reduce(out=lam[:, 0:2], in_=lp[:, :, :], axis=AX.X, op=ALU.add), r=[b_lv], w=[b_lam])
            ACT.op(lambda: S_.activation(out=lam[:, 2:4], in_=lam[:, 0:2], func=AF.Exp), r=[b_lam], w=[b_lam])
            DVE.op(lambda: V.tensor_tensor(out=lam[:, 4:5], in0=lam[:, 2:3], in1=lam[:, 3:4], op=ALU.subtract), r=[b_lam], w=[b_lam])
            DVE.op(lambda: V.tensor_scalar(out=lam[:, 5:6], in0=lam[:, 4:5], scalar1=lam_init, scalar2=-1.0, op0=ALU.add, op1=ALU.mult),
                   r=[b_lam], w=[b_lam])
            DVE.op(lambda: V.tensor_scalar(out=subg[:, :], in0=subg[:, :], scalar1=(1.0 - lam_init), scalar2=None, op0=ALU.mult),
                   r=[b_subg], w=[b_subg])
            kb.barrier()

        with ExitStack() as st:
            pA = [kb.ps("pA%d" % i, [128, 512], F32, st) for i in range(2)]; b_pA = bufs(2)
            pY = kb.ps("pY", [128, 2048], F32, st); b_pY = Buf()
            zu = [kb.sb("zu%d" % i, [128, 64, 64], BF16, st) for i in range(2)]; b_zu = bufs(2)
            zv = [kb.sb("zv%d" % i, [128, 64, 64], BF16, st) for i in range(2)]; b_zv = bufs(2)
            sS = kb.sb("sS", [128, 64, 128], BF16, st); b_sS = Buf()
            dft = kb.sb("dft", [128, 3, 128], BF16, st); b_dft = Buf()
            wbt = kb.sb("wbt", [128, 128, 16], BF16, st); b_wbt = Buf()
            ACT.dma(dft[:, :, :], dft_d[:, :, :], w=[b_dft])
            ACT.dma(wbt[:, :, :], wb_d[:, :, :], w=[b_wbt])

            def load_z(g, slot):
                for r_ in range(4):
                    for (z_, bz_, r0) in ((zu[slot], b_zu[slot], R_U), (zv[slot], b_zv[slot], R_VF)):
                        src = gsrc(r_, r0 + g * 64, 64).rearrange("r (q d) -> (r q) d", d=64)
                        SP.dma(z_[32 * r_:32 * (r_ + 1), :, :], src.rearrange("(a b) d -> a b d", b=64), r=gb(r0), w=[bz_])

            load_z(0, 0)
            ppl = [kb.ps("ppl%d" % i, [128, 512], F32, st) for i in range(2)]; b_ppl = bufs(2)
            s2 = kb.sb("s2", [128, T + 16], F32, st); b_s2 = Buf()
            s4 = kb.sb("s4", [128, T + 16], F32, st); b_s4 = Buf()
            s8 = kb.sb("s8", [128, T + 16], F32, st); b_s8 = Buf()
            pmTs = kb.sb("pmT", [128, 2, T + CT], BF16, st); b_pmTs = bufs(4)
            pool_mm = []
            bdp = kb.sb("bdp", [128, 2, 128], BF16, st); b_bdp = Buf()
            pscale = kb.sb("pscale", [128, 2], F32, st); b_psc = Buf()
            invw = kb.sb("invw", [128, 2], F32, st); b_invw = Buf()
            efix = kb.sb("efix", [128, 2, 16], F32, st); b_efix = Buf()
            sel = kb.sb("sel", [128, 8], F32, st); b_sel = Buf()
            cand = kb.sb("cand", [128, 2, 4, 2, 8], BF16, st); b_cand = Buf()
            ACT.dma(pscale[:, :], pscale_d[:, :], w=[b_psc])
            ACT.dma(invw[:, :], invw_d[:, :], w=[b_invw])
            ACT.dma(efix[:, :, :], efix_d[:, :, :], w=[b_efix])
            ACT.dma(sel[:, :], sel_d[:, :], w=[b_sel])
            for r_ in range(4):
                for fl, c0 in ((0, 0), (1, T - 8)):
                    ACT.dma(cand[:, :, r_, fl, :], gsrc(r_, R_B, 256)[:, c0:c0 + 8].rearrange("(j p) t -> p j t", p=128),
                           r=gb(R_B), w=[b_cand])
            for side, fl, dsl in ((0, 1, slice(0, 8)), (1, 0, slice(T + 8, T + 16))):
                for r_ in range(4):
                    sc_ = sel[:, side * 4 + r_:side * 4 + r_ + 1]
                    if r_ == 0:
                        DVE.op(lambda: V.tensor_scalar(out=bhs[:, :, dsl], in0=cand[:, :, r_, fl, :], scalar1=sc_, scalar2=None, op0=ALU.mult),
                               r=[b_cand, b_sel], w=[b_bhs])
                    else:
                        DVE.op(lambda: V.scalar_tensor_tensor(out=bhs[:, :, dsl], in0=cand[:, :, r_, fl, :], scalar=sc_, in1=bhs[:, :, dsl],
                                                              op0=ALU.mult, op1=ALU.add), r=[b_cand, b_sel], w=[b_bhs])
            DVE.op(lambda: V.memset(bdp[:, :, :], 0.0), w=[b_bdp])
            for g in range(4):
                POOL.dma(bdp[(g % 2) * 64:(g % 2 + 1) * 64, g // 2, (g % 2) * 64:(g % 2 + 1) * 64], wp_d[g, :, :], w=[b_bdp])
            if upd:
                efixc = kb.sb("efixc", [128, 2, 16], F32, st); b_efixc = Buf()
                ACT.dma(efixc[:, :, :], efixc_d[:, :, :], w=[b_efixc])
                DVE.op(lambda: V.memset(bTc[:, :, 0:8], 0.0), w=[b_bTc])
                DVE.op(lambda: V.memset(bTc[:, :, 8 + CT:16 + CT], 0.0), w=[b_bTc])

            pln = [0]

            def pool_seq(bsrc, bb, W, fix, bfix, col0):
                E = W + 16
                for j in range(2):
                    b_ = bsrc[:, j, :]
                    pmT = pmTs[:, j, col0:col0 + W]
                    b_pmT = b_pmTs[j * 2 + (1 if col0 else 0)]
                    POOL.op(lambda: G.tensor_tensor(out=s2[:, 1:E], in0=b_[:, 0:E - 1], in1=b_[:, 1:E], op=ALU.add), r=[bb], w=[b_s2])
                    POOL.op(lambda: G.tensor_tensor(out=s4[:, 2:E - 1], in0=s2[:, 1:E - 2], in1=s2[:, 3:E], op=ALU.add), r=[b_s2], w=[b_s4])
                    if j == 0:
                        lv0, lv1 = s2, s4
                        bl0, bl1 = b_s2, b_s4
                    else:
                        POOL.op(lambda: G.tensor_tensor(out=s8[:, 4:E - 3], in0=s4[:, 2:E - 5], in1=s4[:, 6:E - 1], op=ALU.add), r=[b_s4], w=[b_s8])
                        POOL.op(lambda: G.tensor_tensor(out=s2[64:128, 8:E - 7], in0=s8[64:128, 4:E - 11], in1=s8[64:128, 12:E - 3], op=ALU.add),
                                r=[b_s8], w=[b_s2])
                        lv0, lv1 = s8, s2
                        bl0, bl1 = b_s8, b_s2
                    for half, lv_, bl_ in ((0, lv0, bl0), (1, lv1, bl1)):
                        ps_ = slice(half * 64, (half + 1) * 64)
                        POOL.op(lambda: G.tensor_tensor(out=lv_[ps_, 8:16], in0=lv_[ps_, 8:16], in1=fix[ps_, j, 0:8], op=ALU.mult), r=[bfix], w=[bl_])
                        POOL.op(lambda: G.tensor_tensor(out=lv_[ps_, W:W + 8], in0=lv_[ps_, W:W + 8], in1=fix[ps_, j, 8:16], op=ALU.mult), r=[bfix], w=[bl_])
                        DVE.op(lambda: V.scalar_tensor_tensor(out=pmT[ps_, 0:W], in0=lv_[ps_, 8:8 + W], scalar=invw[ps_, j:j + 1], in1=b_[ps_, 8:8 + W],
                                                              op0=ALU.mult, op1=ALU.subtract), r=[bl_, b_invw, bb], w=[b_pmT])
                    def mm(j=j, pmT=pmT, b_pmT=b_pmT):
                        for c0 in range(0, W, 512):
                            nn = min(512, W - c0)
                            i = pln[0] % 2
                            pln[0] += 1
                            PE.op(lambda: nc.tensor.matmul(ppl[i][:, 0:nn], lhsT=bdp[:, j, :], rhs=pmT[:, c0:c0 + nn], start=True, stop=True),
                                  r=[b_bdp, b_pmT], w=[b_ppl[i]])
                            dv = sgT[:, 2 + j, col0 + c0:col0 + c0 + nn]
                            DVE.op(lambda: V.scalar_tensor_tensor(out=dv, in0=ppl[i][:, 0:nn], scalar=pscale[:, j:j + 1], in1=dv, op0=ALU.mult, op1=ALU.mult),
                                   r=[b_ppl[i], b_psc], w=[b_sg[2 + j]])
                    pool_mm.append(mm)

            pool_seq(bhs, b_bhs, T, efix, b_efix, 0)
            if upd:
                pool_seq(bTc, b_bTc, CT, efixc, b_efixc, T)


            an = 0
            for g in range(4):
                slot = g % 2
                if g + 1 < 4:
                    load_z(g + 1, (g + 1) % 2)
                for c4 in range(16):
                    pa_, bpa_ = pA[an % 2], b_pA[an % 2]
                    an += 1
                    for ci in range(4):
                        ch = c4 * 4 + ci
                        osl = slice(ci * 128, (ci + 1) * 128)
                        PE.op(lambda: nc.tensor.matmul(pa_[0:64, osl], lhsT=zu[slot][:, :, ch], rhs=dft[:, 0, :], start=True, stop=False),
                              r=[b_zu[slot], b_dft], w=[bpa_])
                        PE.op(lambda: nc.tensor.matmul(pa_[0:64, osl], lhsT=zv[slot][:, :, ch], rhs=dft[:, 2, :], start=False, stop=True),
                              r=[b_zv[slot], b_dft], w=[bpa_])
                        PE.op(lambda: nc.tensor.matmul(pa_[64:128, osl], lhsT=zu[slot][:, :, ch], rhs=dft[:, 1, :], start=True, stop=False),
                              r=[b_zu[slot], b_dft], w=[bpa_])
                        PE.op(lambda: nc.tensor.matmul(pa_[64:128, osl], lhsT=zv[slot][:, :, ch], rhs=dft[:, 0, :], start=False, stop=True),
                              r=[b_zv[slot], b_dft], w=[bpa_])
                    dst = sS[:, c4 * 4:(c4 + 1) * 4, :]
                    src = pa_[:, :].rearrange("p (c k) -> p c k", c=4)
                    ACT.op(lambda: S_.copy(out=dst, in_=src), r=[bpa_], w=[b_sS])
                g2 = g % 2
                for k2 in range(128):
                    PE.op(lambda: nc.tensor.matmul(pY[g2 * 64:(g2 + 1) * 64, k2 * 16:(k2 + 1) * 16], lhsT=sS[:, :, k2], rhs=wbt[:, k2, :],
                                                   start=True, stop=True), r=[b_sS, b_wbt], w=[b_pY])
                if g == 2:
                    while pool_mm:
                        pool_mm.pop(0)()
                if g2 == 1:
                    j = g // 2
                    dstv = sgT[:, j, 0:T].rearrange("p (a b) -> p a b", b=128)
                    DVE.op(lambda: V.tensor_tensor(out=dstv, in0=pY[:, :].rearrange("p (b a) -> p a b", a=16), in1=dstv, op=ALU.mult),
                           r=[b_pY], w=[b_sg[j]])
            kb.barrier()

        if upd:
            with ExitStack() as st:
                pYc = kb.ps("pYc", [128, 512], F32, st); b_pYc = Buf()
                dftc = kb.sb("dftc", [128, 2, 2, CT], BF16, st); b_dftc = Buf()
                SP.dma(dftc[:, :, :, :], dftc_d[:, :, :, :], w=[b_dftc])
                for j in range(2):
                    for g2 in range(2):
                        n = 0
                        for lt in range(2):
                            for uv in range(2):
                                c0 = j * 256 + g2 * 128 + uv * 64
                                PE.op(lambda: nc.tensor.matmul(pYc[g2 * 64:(g2 + 1) * 64, 0:CT], lhsT=uvc[:, lt, c0:c0 + 64], rhs=dftc[:, uv, lt, :],
                                                               start=(n == 0), stop=(n == 3)), r=[b_uvc, b_dftc], w=[b_pYc])
                                n += 1
                    DVE.op(lambda: V.tensor_tensor(out=sgT[:, j, T:TT], in0=pYc[:, 0:CT], in1=sgT[:, j, T:TT], op=ALU.mult), r=[b_pYc], w=[b_sg[j]])
                kb.barrier()

        with ExitStack() as st:
            pS = [kb.ps("pS%d" % i, [128, 1024], F32, st) for i in range(2)]; b_pS = bufs(2)
            pO = kb.ps("pO", [128, 1536], F32, st); b_pO = Buf()
            pTq = kb.ps("pTq", [128, 1024], BF16, st); b_pTq = Buf()
            NKC = 2 + 64
            kTh = [kb.sb("kTh%d" % i, [128, NKC * 128], BF16, st) for i in range(2)]; b_kTh = bufs(2)
            vh = [kb.sb("vh%d" % i, [128, NKC, 130], BF16, st) for i in range(2)]; b_vh = bufs(2)
            pT = [kb.sb("pT%d" % i, [128, 1024], BF16, st) for i in range(3)]; b_pT = bufs(3)
            oacc = kb.sb("oacc", [128, 8, 130], F32, st); b_oacc = Buf()
            rz = kb.sb("rz", [128, 16], F32, st); b_rz = Buf()
            od = kb.sb("od", [128, 4, 128], F32, st); b_od = Buf()
            osq = kb.sb("osq", [128, 4, 128], F32, st); b_osq = Buf()
            t2 = kb.sb("t2", [128, 128], F32, st); b_t2 = Buf()
            yb = kb.sb("yb", [128, 4, 128], BF16, st); b_yb = Buf()
            for i in range(2):
                DVE.op(lambda: V.memset(vh[i][:, :, 128:130], 1.0), w=[b_vh[i]])

            def load_kv(h, slot):
                POOL.op(lambda: G.tensor_copy(out=kTh[slot][:, 0:CT], in_=kTc[:, h, :]), r=[b_kTc], w=[b_kTh[slot]])
                POOL.op(lambda: G.tensor_copy(out=vh[slot][:, 0:2, 0:128], in_=vc[:, :, h, :]), r=[b_vc], w=[b_vh[slot]])
                for r_ in range(4):
                    SP.dma(kTh[slot][:, CT + r_ * T:CT + (r_ + 1) * T], gsrc(r_, h * 256, 128),
                           r=gb(h * 256), w=[b_kTh[slot]])
                    src = gsrc(r_, h * 256 + 128, 128).rearrange("r (q c) -> (r q) c", c=128)
                    SP.dma(vh[slot][:, 2 + 16 * r_:2 + 16 * (r_ + 1), 0:128], src.rearrange("(a p) e -> p a e", p=128),
                           r=gb(h * 256), w=[b_vh[slot]])

            deferred = []

            def attend(h, slot, q0, nq, chunks):
                nqs = nq // 128
                nch = len(chunks)
                first_in_bank = set()
                seen_banks = set()
                for m in range(2):
                    for qs in range(nqs):
                        a = m * 4 + qs
                        if a // 3 not in seen_banks:
                            seen_banks.add(a // 3)
                            first_in_bank.add(a)

                def qk(i):
                    c = chunks[i]
                    for m in range(2):
                        PE.op(lambda: nc.tensor.matmul(pS[i % 2][:, m * 512:m * 512 + nq], lhsT=kTh[slot][m * 64:(m + 1) * 64, c * 128:(c + 1) * 128],
                                                       rhs=qT[m * 64:(m + 1) * 64, h, q0:q0 + nq], start=True, stop=True),
                              r=[b_kTh[slot], b_qT], w=[b_pS[i % 2]])

                def ex(i):
                    if nq == 512:
                        ACT.op(lambda: S_.activation(out=pT[i % 3][:, :], in_=pS[i % 2][:, :], func=AF.Exp, scale=0.125), r=[b_pS[i % 2]], w=[b_pT[i % 3]])
                    else:
                        ACT.op(lambda: S_.activation(out=pT[i % 3][:, :].rearrange("p (m q) -> p m q", m=2)[:, :, 0:nq],
                                                     in_=pS[i % 2][:, :].rearrange("p (m q) -> p m q", m=2)[:, :, 0:nq], func=AF.Exp, scale=0.125),
                               r=[b_pS[i % 2]], w=[b_pT[i % 3]])

                def pv(i):
                    c = chunks[i]
                    for m in range(2):
                        for qs in range(nqs):
                            a = m * 4 + qs
                            co = (a // 3) * 512 + (a % 3) * 130
                            PE.op(lambda: nc.tensor.matmul(pO[:, co:co + 130], lhsT=pT[i % 3][:, m * 512 + qs * 128:m * 512 + (qs + 1) * 128],
                                                           rhs=vh[slot][:, c, :], start=(i == 0 and a in first_in_bank),
                                                           stop=(i == nch - 1), skip_group_check=True),
                                  r=[b_pT[i % 3], b_vh[slot]], w=[b_pO])

                qk(0)
                if nch > 1:
                    qk(1)
                for i in range(nch):
                    ex(i)
                    if i + 2 < nch:
                        qk(i + 2)
                    pv(i)
                    if i == 8 and deferred:
                        deferred.pop(0)()
                while deferred:
                    deferred.pop(0)()

                DVE.op(lambda: V.tensor_copy(out=oacc[:, 0:3, :], in_=pO[:, 0:390].rearrange("p (a e) -> p a e", e=130)), r=[b_pO], w=[b_oacc])
                DVE.op(lambda: V.tensor_copy(out=oacc[:, 3:6, :], in_=pO[:, 512:902].rearrange("p (a e) -> p a e", e=130)), r=[b_pO], w=[b_oacc])
                DVE.op(lambda: V.tensor_copy(out=oacc[:, 6:8, :], in_=pO[:, 1024:1284].rearrange("p (a e) -> p a e", e=130)), r=[b_pO], w=[b_oacc])
                DVE.op(lambda: V.reciprocal(out=rz[:, 0:8], in_=oacc[:, :, 128]), r=[b_oacc], w=[b_rz])
                DVE.op(lambda: V.tensor_scalar(out=rz[:, 8:12], in0=rz[:, 4:8], scalar1=lam[:, 5:6], scalar2=None, op0=ALU.mult), r=[b_rz, b_lam], w=[b_rz])
                for qs in range(nqs):
                    DVE.op(lambda: V.tensor_scalar(out=t2[:, :], in0=oacc[:, 4 + qs, 0:128], scalar1=rz[:, 8 + qs:9 + qs], scalar2=None, op0=ALU.mult),
                           r=[b_oacc, b_rz], w=[b_t2])
                    DVE.op(lambda: V.scalar_tensor_tensor(out=od[:, qs, :], in0=oacc[:, qs, 0:128], scalar=rz[:, qs:qs + 1], in1=t2[:, :],
                                                          op0=ALU.mult, op1=ALU.add), r=[b_oacc, b_rz, b_t2], w=[b_od])
                DVE.op(lambda: V.tensor_tensor(out=osq[:, 0:nqs, :], in0=od[:, 0:nqs, :], in1=od[:, 0:nqs, :], op=ALU.mult), r=[b_od], w=[b_osq])
                DVE.op(lambda: V.tensor_reduce(out=rz[:, 12:12 + nqs], in_=osq[:, 0:nqs, :], axis=AX.X, op=ALU.add), r=[b_osq], w=[b_rz])
                DVE.op(lambda: V.tensor_scalar(out=rz[:, 12:12 + nqs], in0=rz[:, 12:12 + nqs], scalar1=1.0 / 128, scalar2=EPS, op0=ALU.mult, op1=ALU.add),
                       r=[b_rz], w=[b_rz])
                POOL.op(lambda: G.tensor_tensor(out=rz[:, 12:12 + nqs], in0=rz[:, 12:12 + nqs], in1=negh[:, 0:nqs], op=ALU.pow), r=[b_rz, b_negh], w=[b_rz])
                for qs in range(nqs):
                    DVE.op(lambda: V.scalar_tensor_tensor(out=yb[:, qs, :], in0=od[:, qs, :], scalar=rz[:, 12 + qs:13 + qs], in1=subg[:, :],
                                                          op0=ALU.mult, op1=ALU.mult), r=[b_od, b_rz, b_subg], w=[b_yb])

                def fin():
                    for qs in range(nqs):
                        PE.op(lambda: nc.tensor.transpose(out=pTq[:, qs * 128:(qs + 1) * 128], in_=yb[:, qs, :], identity=ident[:, :]),
                              r=[b_yb, b_ident], w=[b_pTq])
                    dv = sgT[:, 4 + h, q0:q0 + nq]
                    DVE.op(lambda: V.tensor_tensor(out=dv, in0=pTq[:, 0:nq], in1=dv, op=ALU.mult), r=[b_pTq], w=[b_sg[4 + h]])
                deferred.append(fin)

            load_kv(0, 0)
            for h in range(4):
                slot = h % 2
                if h + 1 < 4:
                    load_kv(h + 1, (h + 1) % 2)
                if h == 2:
                    POOL.dma(wo[:, :, :], wout_d[:, :].rearrange("(k p) c -> p k c", p=128), w=[b_wo])
                for qb_ in range(4):
                    attend(h, slot, qb_ * 512, 512, list(range(NKC)))
                    if h == 0 and qb_ == 1:
                        issue_cc([2, 3], extra=[b_kTh[0], b_vh[0], b_kTh[1], b_vh[1]])
                if upd:
                    attend(h, slot, T, CT, [0, 1])
            while deferred:
                deferred.pop(0)()
            kb.barrier()

        if DEBUG and layer == DEBUG - 1:
            SP.dma(dbg_o[:, :, :], sgT[:, :, :], r=b_sg)
            kb.barrier()
        with ExitStack() as st:
            NB_ = 3
            po = [kb.ps("po%d" % i, [128, 1024], F32, st) for i in range(NB_)]; b_po = bufs(NB_)
            xr = [kb.sb("xr%d" % i, [128, D], F32, st) for i in range(NB_)]; b_xr = bufs(NB_)
            to = [kb.sb("to%d" % i, [128, D], F32, st) for i in range(NB_)]; b_to = bufs(NB_)
            otiles = list(range(NT)) + ([NT, NT + 1] if upd else [])
            for n, ti in enumerate(otiles):
                i = n % NB_
                SP.dma(xr[i][:, :], xsrc(ti), r=xbuf(ti), w=[b_xr[i]])
                for hf in range(2):
                    for k in range(8):
                        PE.op(lambda: nc.tensor.matmul(po[i][:, hf * 512:(hf + 1) * 512], lhsT=sgT[:, k, tcols(ti)], rhs=wo[:, k, hf * 512:(hf + 1) * 512],
                                                       start=(k == 0), stop=(k == 7)), r=b_sg + [b_wo], w=[b_po[i]])
                m_ = modl if ti < NT else modc
                bm_ = b_modl if ti < NT else b_modc
                DVE.op(lambda: V.tensor_tensor(out=to[i][:, :], in0=po[i][:, :], in1=m_[:, 2 * D:3 * D], op=ALU.mult), r=[b_po[i], bm_], w=[b_to[i]])
                DVE.op(lambda: V.tensor_tensor(out=to[i][:, :], in0=to[i][:, :], in1=xr[i][:, :], op=ALU.add), r=[b_xr[i]], w=[b_to[i]])
                if ti < NT:
                    SP.dma(x_o[ti * 128:(ti + 1) * 128, :], to[i][:, :], r=[b_to[i]], w=([] if last else [b_x1[ti]]))
                else:
                    SP.dma(ctx1_d[(ti - NT) * 128:(ti - NT + 1) * 128, :], to[i][:, :], r=[b_to[i]], w=[b_ctx1[ti - NT]])
            kb.barrier()
        lst.close()
    return kb


def _rope_tables(s):
    pos = np.arange(T) + T * s
    row = (pos // 64).astype(np.float32)
    col = (pos % 64).astype(np.float32)
    half = 32
    inv = (np.float32(10000.0) ** (-np.arange(0, half, 2, dtype=np.float32) / np.float32(half))).astype(np.float32)
    ar = row[:, None] * inv[None, :]
    ac = col[:, None] * inv[None, :]
    ang = np.concatenate([ar, ar, ac, ac], -1).astype(np.float32)
    cos, sin = np.cos(ang), np.sin(ang)
    sgn = np.tile(np.concatenate([-np.ones(16), np.ones(16)]), 2).astype(np.float32)
    out = np.stack([cos, sin * sgn], 0).reshape(2, NT, 128, 64).transpose(2, 0, 1, 3)
    return np.ascontiguousarray(out.astype(np.float32))


def _consts(s):
    c = {}
    c["ident"] = np.eye(128, dtype=np.float32).astype(NPBF)
    c["rope"] = _rope_tables(s)
    k = np.arange(64)
    ang = 2 * np.pi * np.outer(k, k) / 64.0
    c["ccs"] = np.ascontiguousarray(np.stack([np.cos(ang), np.sin(ang)], 1).astype(np.float32))
    k = np.arange(128)
    ang = 2 * np.pi * np.outer(k, k) / 128.0
    c["dft128"] = np.ascontiguousarray(np.stack([np.cos(ang), np.sin(ang), -np.sin(ang)], 1).astype(np.float32).astype(NPBF))
    l1 = np.arange(64)[:, None, None]
    k2 = np.arange(128)[None, :, None]
    k1 = np.arange(16)[None, None, :]
    lp = 128 * (16 * s + k1) + k2
    ang = 2 * np.pi * ((l1 * lp) % L) / L
    c["wbt"] = np.ascontiguousarray(np.concatenate([np.cos(ang), -np.sin(ang)], 0).astype(np.float32).astype(NPBF))
    l = np.arange(CT)
    ang = 2 * np.pi * (np.outer(l, l) % CT) / CT
    cs = np.stack([np.cos(ang), -np.sin(ang)], 0).reshape(2, 2, 128, CT).transpose(2, 0, 1, 3)
    c["dftc"] = np.ascontiguousarray(cs.astype(np.float32).astype(NPBF))
    wins = [2, 4, 8, 16]
    invw = np.zeros((128, 2), np.float32)
    for g, w in enumerate(wins):
        invw[(g % 2) * 64:(g % 2 + 1) * 64, g // 2] = 1.0 / w

    def edgefix(first, last, Lseq):
        f = np.ones((128, 2, 16), np.float32)
        for g, w in enumerate(wins):
            lo, hi = w // 2, w - w // 2 - 1
            for i in range(8):
                if first:
                    t = i
                    cnt = min(t + hi, Lseq - 1) - max(t - lo, 0) + 1
                    f[(g % 2) * 64:(g % 2 + 1) * 64, g // 2, i] = w / cnt
                if last:
                    t = Lseq - 8 + i
                    cnt = min(t + hi, Lseq - 1) - max(t - lo, 0) + 1
                    f[(g % 2) * 64:(g % 2 + 1) * 64, g // 2, 8 + i] = w / cnt
        return f
    c["invw"] = invw
    c["efix"] = edgefix(s == 0, s == 3, L)
    c["efixc"] = edgefix(True, True, CT)
    sel = np.zeros((128, 8), np.float32)
    if s > 0:
        sel[:, s - 1] = 1.0
    if s < 3:
        sel[:, 4 + s + 1] = 1.0
    c["sel"] = sel
    return c


_PROGS = {}


def _prog(nlayers=2):
    if nlayers not in _PROGS:
        _PROGS[nlayers] = build_fused(nlayers)
    return _PROGS[nlayers]


def _run(kb, maps):
    need = kb.inputs
    ins = []
    for m in maps:
        d = {}
        for name, (shape, dt) in need.items():
            a = np.ascontiguousarray(m[name])
            assert tuple(a.shape) == tuple(shape), (name, a.shape, shape)
            d[name] = a
        ins.append(d)
    res = run_bass_kernel_spmd(kb.nc, ins, core_ids=list(range(8)))
    return res.results


def _maps(x, c, ctx, c_ctx, norm_g, w_mod, b_mod, w_in, w_fourier, w_pool, pool_scale, qk_norm_g, lam_vecs, subln_g, w_out):
    f = lambda a: np.ascontiguousarray(np.asarray(a, dtype=np.float32))
    x, c, ctx, c_ctx = f(x), f(c), f(ctx), f(c_ctx)
    rep = lambda v: np.ascontiguousarray(np.broadcast_to(np.asarray(v, np.float32)[:, None], (v.shape[0], 128) + v.shape[1:]))
    shared = {
        "w_mod": f(w_mod), "bmod_rep": rep(f(b_mod)), "normg_rep": rep(f(norm_g)), "w_in": f(w_in),
        "qkg_rep": rep(f(qk_norm_g)), "wf": np.ascontiguousarray(f(w_fourier).transpose(0, 2, 1, 3)),
        "w_out": f(w_out), "w_pool": f(w_pool),
        "pscale": np.ascontiguousarray(f(pool_scale).reshape(2, 2, 128).transpose(0, 2, 1)),
        "lamv_rep": rep(f(lam_vecs)), "subg_rep": rep(f(subln_g)),
    }
    consts = [_consts(s) for s in range(4)]
    maps = []
    for b in range(2):
        for s in range(4):
            m = dict(consts[s])
            m.update(shared)
            m["x"] = x[b, s * T:(s + 1) * T]
            m["ctx"] = ctx[b]
            m["cT"] = np.ascontiguousarray(np.stack([c[b], c_ctx], 0).reshape(2, 8, 128).transpose(2, 1, 0))
            maps.append(m)
    return maps


def kernel(x, c, ctx, c_ctx, norm_g, w_mod, b_mod, w_in, w_fourier, w_pool, pool_scale, qk_norm_g, lam_vecs, subln_g, w_out):
    maps = _maps(x, c, ctx, c_ctx, norm_g, w_mod, b_mod, w_in, w_fourier, w_pool, pool_scale, qk_norm_g, lam_vecs, subln_g, w_out)
    res = _run(_prog(2), maps)
    out = np.stack([np.concatenate([res[b * 4 + s]["xo"] for s in range(4)], axis=0) for b in range(2)], 0)
    return out.astype(np.float32)
```

```python
import math
from contextlib import ExitStack
import numpy as np
import ml_dtypes
import concourse.bass as bass
import concourse.mybir as mybir
from concourse.bass_utils import run_bass_kernel_spmd

F32 = mybir.dt.float32
BF16 = mybir.dt.bfloat16
AF = mybir.ActivationFunctionType
ALU = mybir.AluOpType
AX = mybir.AxisListType
NPBF = ml_dtypes.bfloat16

D = 1024
L = 8192
T = 2048
NT = 16
CT = 256
TT = T + CT
PR = 1792
EPS = 1e-6
R_U, R_VF, R_B = 1024, 1280, 1536
DEBUG = False


class Buf:
    __slots__ = ("w", "r")

    def __init__(self):
        self.w = None
        self.r = {}


def bufs(n):
    return [Buf() for _ in range(n)]


class SemObj:
    __slots__ = ("h", "cum", "qwaited")

    def __init__(self, h):
        self.h = h
        self.cum = 0
        self.qwaited = 0


class Eng:
    ROLL = 30000

    def __init__(self, kb, eng, name, same=True, ndma=0):
        self.kb, self.eng, self.name, self.same = kb, eng, name, same
        self.own = set()
        self.waited = {}
        self.nins = 0
        self._newsem()
        self.dpool = [SemObj(kb.newsem("%s_d%d" % (name, i))) for i in range(ndma)]
        self.di = 0

    def _newsem(self):
        self.cur = SemObj(self.kb.newsem("%s_s%d" % (self.name, len(self.own))))
        self.own.add(self.cur)
        self.cnt = 0

    def _wait(self, toks):
        need = {}
        for (s, v) in toks:
            if s in self.own and not self.same:
                continue
            if need.get(s, 0) < v:
                need[s] = v
        for s, v in need.items():
            if self.waited.get(s, 0) >= v:
                continue
            self.eng.wait_ge(s.h, v)
            self.waited[s] = v

    @staticmethod
    def _deps(r, w):
        toks = []
        for b in r:
            if b.w is not None:
                toks.append(b.w)
        for b in w:
            if b.w is not None:
                toks.append(b.w)
            toks.extend(b.r.items())
        return toks

    @staticmethod
    def _mark(tok, r, w):
        for b in r:
            b.r[tok[0]] = tok[1]
        for b in w:
            b.w = tok
            b.r = {}

    def op(self, fn, r=(), w=()):
        self._wait(self._deps(r, w))
        ins = fn()
        if self.cnt >= self.ROLL:
            self._newsem()
        self.cnt += 1
        ins.then_inc(self.cur.h, 1)
        self.nins += 1
        self._mark((self.cur, self.cnt), r, w)

    def dma(self, out, in_, r=(), w=(), **kw):
        self._wait(self._deps(r, w))
        slot = self.dpool[self.di % len(self.dpool)]
        self.di += 1
        if slot.cum > slot.qwaited and self.waited.get(slot, 0) < slot.cum:
            self.eng.wait_ge(slot.h, slot.cum)
            self.waited[slot] = slot.cum
        slot.qwaited = slot.cum
        self.eng.dma_start(out=out, in_=in_, **kw).then_inc(slot.h, 16)
        slot.cum += 16
        self.nins += 1
        self._mark((slot, slot.cum), r, w)

    def tok(self):
        return (self.cur, self.cnt)


class KB:
    def __init__(self):
        self.nc = bass.Bass("TRN2", target_bir_lowering=False)
        self.es = ExitStack()
        nc = self.nc
        self.PE = Eng(self, nc.tensor, "pe", same=False)
        self.ACT = Eng(self, nc.scalar, "act", ndma=8)
        self.DVE = Eng(self, nc.vector, "dve")
        self.POOL = Eng(self, nc.gpsimd, "pool", ndma=12)
        self.SP = Eng(self, nc.sync, "sp", ndma=24)
        self.engs = [self.PE, self.ACT, self.DVE, self.POOL, self.SP]
        self.inputs = {}
        self.pfx = ""

    def newsem(self, name):
        return self.es.enter_context(self.nc.semaphore(name))

    def din(self, name, shape, dt):
        self.inputs[name] = (tuple(shape), dt)
        return self.nc.dram_tensor(name, list(shape), dt, kind="ExternalInput").ap()

    def dout(self, name, shape, dt):
        return self.nc.dram_tensor(name, list(shape), dt, kind="ExternalOutput").ap()

    def sb(self, name, shape, dt, st=None):
        return (st or self.es).enter_context(self.nc.sbuf_tensor("s_" + self.pfx + name, list(shape), dt))

    def ps(self, name, shape, dt, st=None):
        return (st or self.es).enter_context(self.nc.psum_tensor("p_" + self.pfx + name, list(shape), dt))

    def barrier(self):
        toks = []
        for e in self.engs:
            if e.cnt > 0:
                toks.append(e.tok())
            for s in e.dpool:
                if s.cum > 0:
                    toks.append((s, s.cum))
        for e in self.engs:
            sv, e.same = e.same, False
            e._wait(toks)
            e.same = sv


def build_fused(nlayers=2):
    kb = KB()
    nc = kb.nc
    PE, ACT, DVE, POOL, SP = kb.PE, kb.ACT, kb.DVE, kb.POOL, kb.SP
    V, S_, G = nc.vector, nc.scalar, nc.gpsimd
    NL = 2

    x_in = kb.din("x", [T, D], F32)
    ctx_in = kb.din("ctx", [CT, D], F32)
    cT_d = kb.din("cT", [128, 8, 2], F32)
    wmod_a = kb.din("w_mod", [NL, D, 3 * D], F32)
    bmod_a = kb.din("bmod_rep", [NL, 128, 3 * D], F32)
    normg_a = kb.din("normg_rep", [NL, 128, D], F32)
    win_a = kb.din("w_in", [NL, D, 3 * D], F32)
    qkg_a = kb.din("qkg_rep", [NL, 128, 2, 64], F32)
    wf_a = kb.din("wf", [NL, 64, 4, 64], F32)
    wout_a = kb.din("w_out", [NL, D, D], F32)
    wp_a = kb.din("w_pool", [NL, 4, 64, 64], F32)
    pscale_a = kb.din("pscale", [NL, 128, 2], F32)
    lamv_a = kb.din("lamv_rep", [NL, 128, 4, 64], F32)
    subg_a = kb.din("subg_rep", [NL, 128, 128], F32)
    ident_d = kb.din("ident", [128, 128], BF16)
    rope_d = kb.din("rope", [128, 2, NT, 64], F32)
    ccs_d = kb.din("ccs", [64, 2, 64], F32)
    dft_d = kb.din("dft128", [128, 3, 128], BF16)
    wb_d = kb.din("wbt", [128, 128, 16], BF16)
    invw_d = kb.din("invw", [128, 2], F32)
    efix_d = kb.din("efix", [128, 2, 16], F32)
    sel_d = kb.din("sel", [128, 8], F32)
    dftc_d = kb.din("dftc", [128, 2, 2, CT], BF16)
    efixc_d = kb.din("efixc", [128, 2, 16], F32)
    x_out = kb.dout("xo", [T, D], F32)
    NSL = PR // 256
    payloc_t = [nc.dram_tensor("payloc%d" % l, [NSL, 256, T], BF16) for l in range(NL)]
    paygat_t = [nc.dram_tensor("paygat%d" % l, [NSL, 1024, T], BF16) for l in range(NL)]
    x1_d = nc.dram_tensor("x1s", [T, D], F32).ap()
    ctx1_d = nc.dram_tensor("ctx1s", [CT, D], F32).ap()
    b_x1 = bufs(NT)
    b_ctx1 = bufs(2)
    ccsem = SemObj(kb.newsem("ccsem"))
    if DEBUG:
        dbg_o = kb.dout("dbg", [128, 8, TT], BF16)

    ident = kb.sb("ident", [128, 128], BF16); b_ident = Buf()
    negh = kb.sb("negh", [128, 32], F32); b_negh = Buf()
    SP.dma(ident[:, :], ident_d[:, :], w=[b_ident])
    POOL.op(lambda: G.memset(negh[:, :], -0.5), w=[b_negh])

    def tcols(ti):
        return slice(ti * 128, (ti + 1) * 128)

    tiles = list(range(NT + 2))
    ntl = len(tiles)
    lat = list(range(NT))
    ctxt = [NT, NT + 1]

    for layer in range(nlayers):
        upd = (layer == 0)
        last = (layer == nlayers - 1)
        lam_init = 0.8 - 0.6 * math.exp(-0.3 * layer)
        kb.pfx = "L%d_" % layer
        x_d = x_in if layer == 0 else x1_d
        ctx_d = ctx_in if layer == 0 else ctx1_d
        x_o = x_out if last else x1_d
        wmod_d, bmod_d, normg_d, win_d = wmod_a[layer], bmod_a[layer], normg_a[layer], win_a[layer]
        qkg_d, wf_d, wout_d, wp_d = qkg_a[layer], wf_a[layer], wout_a[layer], wp_a[layer]
        pscale_d, lamv_d, subg_d = pscale_a[layer], lamv_a[layer], subg_a[layer]
        pay_o = payloc_t[layer].ap().rearrange("s r t -> (s r) t")
        pay_g = paygat_t[layer].ap()

        def gsrc(r_, row0, n, pay_g=pay_g):
            s_, off = row0 // 256, row0 % 256
            assert off + n <= 256
            return pay_g[s_, r_ * 256 + off:r_ * 256 + off + n, :]
        b_pay = {i_: [] for i_ in range(NSL)}
        b_gat = bufs(NSL)

        def paybuf(*slices, b_pay=b_pay):
            b = Buf()
            for s_ in slices:
                b_pay[s_].append(b)
            return [b]

        def gb(row0, b_gat=b_gat):
            return [b_gat[row0 // 256]]

        def issue_cc(slices, extra=(), layer=layer, b_pay=b_pay, b_gat=b_gat):
            deps = [b for s_ in slices for b in b_pay[s_]] + list(extra)
            POOL._wait(POOL._deps(deps, [b_gat[s_] for s_ in slices]))
            for s_ in slices:
                cins = G.collective_compute("AllGather", ALU.bypass, replica_groups=[[0, 1, 2, 3], [4, 5, 6, 7]],
                                            ins=[payloc_t[layer].ap()[s_, :, :].opt()], outs=[paygat_t[layer].ap()[s_, :, :].opt()], dma_qos="P3")
                cins.then_inc(ccsem.h)
                ccsem.cum += 1
                POOL._mark((ccsem, ccsem.cum), b_pay[s_], [b_gat[s_]])

        def xsrc(ti):
            if ti < NT:
                return x_d[ti * 128:(ti + 1) * 128, :]
            return ctx_d[(ti - NT) * 128:(ti - NT + 1) * 128, :]

        def xbuf(ti):
            if layer == 0:
                return []
            return [b_x1[ti]] if ti < NT else [b_ctx1[ti - NT]]

        lst = ExitStack()
        modl = kb.sb("modl", [128, 3 * D], F32, lst); b_modl = Buf()
        modc = kb.sb("modc", [128, 3 * D], F32, lst); b_modc = Buf()
        rstd = kb.sb("rstd", [128, 32], F32, lst); b_rstd = Buf()
        qkg = kb.sb("qkg", [128, 2, 64], F32, lst); b_qkg = Buf()
        bd = kb.sb("bd", [128, 2, 256], BF16, lst); b_bd = Buf()
        bdc = kb.sb("bdc", [128, 2, 256], BF16, lst)
        sgT = kb.sb("sgT", [128, 8, TT], BF16, lst); b_sg = bufs(8)
        qT = kb.sb("qT", [128, 4, TT], BF16, lst); b_qT = Buf()
        kTc = kb.sb("kTc", [128, 4, CT], BF16, lst); b_kTc = Buf()
        vc = kb.sb("vc", [128, 2, 4, 128], BF16, lst); b_vc = Buf()
        bhs = kb.sb("bhs", [128, 2, T + 16], BF16, lst); b_bhs = Buf()
        bTc = kb.sb("bTc", [128, 2, CT + 16], BF16, lst); b_bTc = Buf()
        uvc = kb.sb("uvc", [128, 2, 512], BF16, lst); b_uvc = Buf()
        SP.dma(qkg[:, :, :], qkg_d[:, :, :], w=[b_qkg])

        with ExitStack() as st:
            ptr = [kb.ps("ptr%d" % i, [128, 1024], BF16, st) for i in range(2)]; b_ptr = bufs(2)
            pp = [kb.ps("pp%d" % i, [128, 512], F32, st) for i in range(3)]; b_pp = bufs(3)
            ptq = kb.ps("ptq", [128, 1024], BF16, st); b_ptq = Buf()
            puv = kb.ps("puv", [128, 512], F32, st); b_puv = Buf()
            ppi = [0]

            def next_pp():
                i = ppi[0] % 3
                ppi[0] += 1
                return pp[i], b_pp[i]

            wst = [kb.sb("wst%d" % i, [128, 8, 512], BF16, st) for i in range(2)]; b_wst = bufs(2)
            cg = kb.sb("cg", [128, 2, 2, NT, 64], F32, st); b_cg = Buf()
            stM = ExitStack()
            cT = kb.sb("cT", [128, 8, 2], F32, stM); b_cT = Buf()
            scT = kb.sb("scT", [128, 8, 2], BF16, stM)
            scb = kb.sb("scb", [128, 2, 8, 128], BF16, stM); b_scb = Buf()
            wmb = [kb.sb("wmb%d" % i, [128, 8, 512], BF16, stM) for i in range(2)]; b_wmb = bufs(2)
            wm32 = [kb.sb("wm32_%d" % i, [128, 8, 512], F32, stM) for i in range(2)]; b_wm32 = bufs(2)
            bmod = kb.sb("bmod", [128, 3 * D], F32, stM); b_bmod = Buf()
            normg = kb.sb("normg", [128, D], F32, stM); b_normg = Buf()
            rope = kb.sb("rope", [128, 2, NT, 64], F32, stM); b_rope = Buf()
            wf = kb.sb("wf", [64, 4, 64], F32, stM); b_wf = Buf()
            ccs = kb.sb("ccs", [64, 2, 64], F32, stM); b_ccs = Buf()
            bdz = kb.sb("bdz", [128, 2, 256], F32, stM); b_bdz = Buf()

            SP.dma(cT[:, :, :], cT_d[:, :, :], w=[b_cT])
            SP.dma(bmod[:, :], bmod_d[:, :], w=[b_bmod])
            SP.dma(normg[:, :], normg_d[:, :], w=[b_normg])
            SP.dma(rope[:, :, :, :], rope_d[:, :, :, :], w=[b_rope])
            SP.dma(wf[:, :, :], wf_d[:, :, :], w=[b_wf])
            SP.dma(ccs[:, :, :], ccs_d[:, :, :], w=[b_ccs])

            ACT.op(lambda: S_.activation(out=scT[:, :, :], in_=cT[:, :, :], func=AF.Silu), r=[b_cT], w=[b_scb])
            for r_ in range(2):
                DVE.op(lambda: V.tensor_copy(out=scb[:, r_, :, :], in_=scT[:, :, r_].unsqueeze(2).to_broadcast([128, 8, 128])),
                       r=[b_scb], w=[b_scb])
            mods = [modl, modc]
            b_mods = [b_modl, b_modc]
            MW = 512
            SP.dma(wm32[0][:, :, :], wmod_d[:, 0:MW].rearrange("(k p) c -> p k c", p=128), w=[b_wm32[0]])
            for j in range(3 * D // MW):
                if j + 1 < 3 * D // MW:
                    SP.dma(wm32[(j + 1) % 2][:, :, :], wmod_d[:, (j + 1) * MW:(j + 2) * MW].rearrange("(k p) c -> p k c", p=128),
                           w=[b_wm32[(j + 1) % 2]])
                ACT.op(lambda: S_.copy(out=wmb[j % 2][:, 0:4, :], in_=wm32[j % 2][:, 0:4, :]), r=[b_wm32[j % 2]], w=[b_wmb[j % 2]])
                DVE.op(lambda: V.tensor_copy(out=wmb[j % 2][:, 4:8, :], in_=wm32[j % 2][:, 4:8, :]), r=[b_wm32[j % 2]], w=[b_wmb[j % 2]])
                for r_ in range(2):
                    p_, bp_ = next_pp()
                    for k in range(8):
                        PE.op(lambda: nc.tensor.matmul(p_[:, 0:MW], lhsT=scb[:, r_, k, :], rhs=wmb[j % 2][:, k, :], start=(k == 0), stop=(k == 7)),
                              r=[b_scb, b_wmb[j % 2]], w=[bp_])
                    DVE.op(lambda: V.tensor_tensor(out=mods[r_][:, j * MW:(j + 1) * MW], in0=p_[:, 0:MW], in1=bmod[:, j * MW:(j + 1) * MW], op=ALU.add),
                           r=[bp_, b_bmod], w=[b_mods[r_]])
            for r_ in range(2):
                DVE.op(lambda: V.scalar_tensor_tensor(out=mods[r_][:, D:2 * D], in0=mods[r_][:, D:2 * D], scalar=1.0, in1=normg[:, :],
                                                      op0=ALU.add, op1=ALU.mult), r=[b_normg], w=[b_mods[r_]])

            for w_ in range(2):
                DVE.op(lambda: V.tensor_tensor(out=cg[:, w_, 0, :, :], in0=rope[:, 0, :, :],
                                               in1=qkg[:, w_, :].unsqueeze(1).to_broadcast([128, NT, 64]), op=ALU.mult),
                       r=[b_rope, b_qkg], w=[b_cg])
                for j in range(2):
                    o_ = cg[:, w_, 1, :, :].rearrange("p t (a j i) -> p t a j i", a=2, j=2)[:, :, :, j, :]
                    i0 = rope[:, 1, :, :].rearrange("p t (a j i) -> p t a j i", a=2, j=2)[:, :, :, j, :]
                    i1 = qkg[:, w_, :].rearrange("p (a j i) -> p a j i", a=2, j=2)[:, :, 1 - j, :].unsqueeze(1).to_broadcast([128, NT, 2, 16])
                    DVE.op(lambda: V.tensor_tensor(out=o_, in0=i0, in1=i1, op=ALU.mult), r=[b_rope, b_qkg], w=[b_cg])

            fsc = 1.0 / math.sqrt(L * 64.0)
            DVE.op(lambda: V.memset(bdz[:, :, :], 0.0), w=[b_bdz])
            for g in range(4):
                j, g2 = g // 2, g % 2
                for uv in range(2):
                    p_, bp_ = next_pp()
                    PE.op(lambda: nc.tensor.matmul(p_[g2 * 64:(g2 + 1) * 64, 0:64], lhsT=ccs[:, uv, :], rhs=wf[:, g, :], start=True, stop=True),
                          r=[b_ccs, b_wf], w=[bp_])
                    DVE.op(lambda: V.tensor_copy(out=bdz[g2 * 64:(g2 + 1) * 64, j, g2 * 128 + uv * 64:g2 * 128 + (uv + 1) * 64],
                                                 in_=p_[g2 * 64:(g2 + 1) * 64, 0:64]), r=[bp_], w=[b_bdz])
            ACT.op(lambda: S_.activation(out=bd[:, :, :], in_=bdz[:, :, :], func=AF.Copy, scale=fsc), r=[b_bdz], w=[b_bd])
            if upd:
                ACT.op(lambda: S_.activation(out=bdc[:, :, :], in_=bdz[:, :, :], func=AF.Copy, scale=1.0 / math.sqrt(CT * 64.0)),
                       r=[b_bdz], w=[b_bd])

            stM.close()
            kb.barrier()
            hT = kb.sb("hT", [128, 8, TT], BF16, st); b_hT = bufs(NT + 2)
            xt = [kb.sb("xt%d" % i, [128, D], F32, st) for i in range(2)]; b_xt = bufs(2)
            ss = kb.sb("ss", [128, 32], F32, st); b_ss = Buf()
            hb = [kb.sb("hb%d" % i, [128, D], BF16, st) for i in range(2)]; b_hb = bufs(2)
            junk = hb[0]; b_junk = b_hb[0]
            scr = kb.sb("scr", [128, 2, 4, 512], F32, st)
            b_scr = [bufs(4) for _ in range(2)]
            b_sq, b_t1 = b_scr[0][0], b_scr[0][1]
            tmpf = scr[:, 0, 0:2, :].rearrange("p a b -> p (a b)")
            ssg = kb.sb("ssg", [128, 2, 32], F32, st); b_ssgs = bufs(2)
            qb = [kb.sb("qb%d" % i, [128, 512], BF16, st) for i in range(2)]; b_qb = bufs(2)
            vst = [kb.sb("vst%d" % i, [128, 512], BF16, st) for i in range(2)]; b_vst = bufs(2)
            kst = [kb.sb("kst%d" % i, [128, 4, 128], BF16, st) for i in range(2)]; b_kst = bufs(2)
            ust = [kb.sb("ust%d" % i, [128, 2, 4, 64], BF16, st) for i in range(2)]; b_ust = bufs(2)
            aT = kb.sb("aT", [128, 2, 512], BF16, st); b_aT = Buf()
            print("P1 sbuf remaining", nc.sbuf_bytes_remaining)
            xts = [xt[0][:, :], xt[1][:, :], scr[:, 1, 0:2, :].rearrange("p a b -> p (a b)"), scr[:, 1, 2:4, :].rearrange("p a b -> p (a b)")]
            b_xts = [[b_xt[0]], [b_xt[1]], [b_scr[1][0], b_scr[1][1]], [b_scr[1][2], b_scr[1][3]]]

            for n, ti in enumerate(tiles):
                SP.dma(xts[n % 4], xsrc(ti), r=xbuf(ti), w=b_xts[n % 4])
                ACT.op(lambda: S_.activation(out=junk[:, :], in_=xts[n % 4], func=AF.Square, accum_out=ss[:, n:n + 1]),
                       r=b_xts[n % 4], w=[b_junk, b_ss])
            DVE.op(lambda: V.tensor_scalar(out=ss[:, 0:ntl], in0=ss[:, 0:ntl], scalar1=1.0 / D, scalar2=EPS, op0=ALU.mult, op1=ALU.add),
                   r=[b_ss], w=[b_ss])
            POOL.op(lambda: G.tensor_tensor(out=rstd[:, 0:ntl], in0=ss[:, 0:ntl], in1=negh[:, 0:ntl], op=ALU.pow), r=[b_ss, b_negh], w=[b_rstd])

            for n, ti in enumerate(tiles):
                m_ = modl if ti < NT else modc
                bm_ = b_modl if ti < NT else b_modc
                xb = xts[n % 4]
                SP.dma(xb, xsrc(ti), r=xbuf(ti), w=b_xts[n % 4])
                tf = scr[:, 0, 2 * (n % 2):2 * (n % 2) + 2, :].rearrange("p a b -> p (a b)")
                btf = [b_scr[0][2 * (n % 2)], b_scr[0][2 * (n % 2) + 1]]
                DVE.op(lambda: V.scalar_tensor_tensor(out=tf, in0=xb, scalar=rstd[:, n:n + 1], in1=m_[:, D:2 * D],
                                                      op0=ALU.mult, op1=ALU.mult), r=b_xts[n % 4] + [b_rstd, bm_], w=btf)
                POOL.op(lambda: G.tensor_tensor(out=hb[n % 2][:, :], in0=tf, in1=m_[:, 0:D], op=ALU.add),
                        r=btf + [bm_], w=[b_hb[n % 2]])
                for k in range(8):
                    PE.op(lambda: nc.tensor.transpose(out=ptr[n % 2][:, k * 128:(k + 1) * 128], in_=hb[n % 2][:, k * 128:(k + 1) * 128],
                                                      identity=ident[:, :]), r=[b_hb[n % 2], b_ident], w=[b_ptr[n % 2]])
                ACT.op(lambda: S_.copy(out=hT[:, :, tcols(ti)], in_=ptr[n % 2][:, :].rearrange("p (k t) -> p k t", k=8)),
                       r=[b_ptr[n % 2]], w=[b_hT[ti]])

            def load_w(cb, slot):
                POOL.dma(wst[slot][:, :, :], win_d[:, cb * 512:(cb + 1) * 512].rearrange("(k p) c -> p k c", p=128), w=[b_wst[slot]])

            def tok_major(cb, ti, slot):
                p_, bp_ = next_pp()
                for k in range(8):
                    PE.op(lambda: nc.tensor.matmul(p_[:, :], lhsT=hT[:, k, tcols(ti)], rhs=wst[slot][:, k, :], start=(k == 0), stop=(k == 7)),
                          r=[b_hT[ti], b_wst[slot]], w=[bp_])
                return p_, bp_

            qkn = [0]

            def qk_post(p_, bp_, ti, which, dest, bdest, after=None):
                n = qkn[0]
                qkn[0] += 1
                q_ = qb[n % 2]
                bq_ = b_qb[n % 2]
                sq, t1, uu, ww = scr[:, n % 2, 0, :], scr[:, n % 2, 1, :], scr[:, n % 2, 2, :], scr[:, n % 2, 3, :]
                b_sq, b_t1, b_uu, b_ww = b_scr[n % 2]
                sg_ = ssg[:, n % 2, :]
                b_ssg = b_ssgs[n % 2]
                ACT.op(lambda: S_.activation(out=sq, in_=p_[:, :], func=AF.Square), r=[bp_], w=[b_sq])
                DVE.op(lambda: V.tensor_reduce(out=sg_[:, 0:8], in_=sq.rearrange("p (g e) -> p g e", e=64), axis=AX.X, op=ALU.add),
                       r=[b_sq], w=[b_ssg])
                DVE.op(lambda: V.tensor_scalar(out=sg_[:, 8:16], in0=sg_[:, 0:8], scalar1=1.0 / 64, scalar2=EPS, op0=ALU.mult, op1=ALU.add),
                       r=[b_ssg], w=[b_ssg])
                ACT.op(lambda: S_.activation(out=sg_[:, 24:32], in_=sg_[:, 8:16], func=AF.Sqrt), r=[b_ssg], w=[b_ssg])
                DVE.op(lambda: V.reciprocal(out=sg_[:, 16:24], in_=sg_[:, 24:32]), r=[b_ssg], w=[b_ssg])
                t3 = t1.rearrange("p (g e) -> p g e", e=64)
                DVE.op(lambda: V.tensor_tensor(out=t3, in0=p_[:, :].rearrange("p (g e) -> p g e", e=64),
                                               in1=sg_[:, 16:24].unsqueeze(2).to_broadcast([128, 8, 64]), op=ALU.mult),
                       r=[bp_, b_ssg], w=[b_t1])
                if ti < NT:
                    POOL.op(lambda: G.tensor_tensor(out=uu.rearrange("p (g e) -> p g e", e=64), in0=t3,
                                                    in1=cg[:, which, 0, ti, :].unsqueeze(1).to_broadcast([128, 8, 64]), op=ALU.mult),
                            r=[b_t1, b_cg], w=[b_uu])
                    t5 = t1.rearrange("p (g a j i) -> p g a j i", a=2, j=2, i=16)
                    w5 = ww.rearrange("p (g a j i) -> p g a j i", a=2, j=2, i=16)
                    s4 = cg[:, which, 1, ti, :].rearrange("p (a j i) -> p a j i", a=2, j=2)
                    for j in range(2):
                        DVE.op(lambda: V.tensor_tensor(out=w5[:, :, :, j, :], in0=t5[:, :, :, 1 - j, :],
                                                       in1=s4[:, :, j, :].unsqueeze(1).to_broadcast([128, 8, 2, 16]), op=ALU.mult),
                               r=[b_t1, b_cg], w=[b_ww])
                    DVE.op(lambda: V.tensor_tensor(out=q_[:, :], in0=uu, in1=ww, op=ALU.add), r=[b_uu, b_ww], w=[bq_])
                else:
                    DVE.op(lambda: V.tensor_tensor(out=q_[:, :].rearrange("p (g e) -> p g e", e=64), in0=t3,
                                                   in1=qkg[:, which, :].unsqueeze(1).to_broadcast([128, 8, 64]), op=ALU.mult),
                           r=[b_t1, b_qkg], w=[bq_])

                def fin():
                    for h in range(4):
                        PE.op(lambda: nc.tensor.transpose(out=ptq[:, h * 128:(h + 1) * 128], in_=q_[:, h * 128:(h + 1) * 128], identity=ident[:, :]),
                              r=[bq_, b_ident], w=[b_ptq])
                    ACT.op(lambda: S_.copy(out=dest, in_=ptq[:, 0:512].rearrange("p (h t) -> p h t", h=4)), r=[b_ptq], w=[bdest])
                    if after is not None:
                        after()
                return fin

            cx = ctxt if upd else []
            sched = [(0, lat + cx), (2, lat + ctxt), (3, lat + ctxt), (1, lat + cx), (4, lat + cx), (5, lat + cx)]
            cc_of = {0: [4, 5], 3: [6, 0, 1]}
            cc_pending = []
            load_w(sched[0][0], 0)
            uvn = [0]
            for si, (cb, tl) in enumerate(sched):
                slot = si % 2
                if si + 1 < len(sched):
                    load_w(sched[si + 1][0], (si + 1) % 2)
                if si >= 1 and cc_pending:
                    issue_cc(cc_pending.pop(0))
                if cb in cc_of:
                    cc_pending.append(cc_of[cb])
                if cb in (0, 4, 5):
                    latg = [t for t in tl if t < NT]
                    groups = [latg[i:i + 4] for i in range(0, len(latg), 4)]
                    cg_ = [t for t in tl if t >= NT]
                    if cg_:
                        groups.append(cg_)
                    for grp in groups:
                        c0 = grp[0] * 128
                        ncol = len(grp) * 128
                        for cc in range(4):
                            p_, bp_ = next_pp()
                            for k in range(8):
                                PE.op(lambda: nc.tensor.matmul(p_[:, 0:ncol], lhsT=wst[slot][:, k, cc * 128:(cc + 1) * 128], rhs=hT[:, k, c0:c0 + ncol],
                                                               start=(k == 0), stop=(k == 7)), r=[b_hT[t] for t in grp] + [b_wst[slot]], w=[bp_])
                            if cb >= 4:
                                ch = (cb - 4) * 4 + cc
                                ACT.op(lambda: S_.activation(out=sgT[:, ch, c0:c0 + ncol], in_=p_[:, 0:ncol], func=AF.Silu), r=[bp_], w=[b_sg[ch]])
                            elif cc < 2:
                                ACT.op(lambda: S_.copy(out=aT[:, cc, 0:ncol], in_=p_[:, 0:ncol]), r=[bp_], w=[b_aT])
                            else:
                                if grp[0] < NT:
                                    ACT.op(lambda: S_.copy(out=bhs[:, cc - 2, 8 + c0:8 + c0 + ncol], in_=p_[:, 0:ncol]), r=[bp_], w=[b_bhs])
                                else:
                                    ACT.op(lambda: S_.copy(out=bTc[:, cc - 2, 8:8 + CT], in_=p_[:, 0:ncol]), r=[bp_], w=[b_bTc])
                        if cb == 0:
                            for gi, ti in enumerate(grp):
                                n = uvn[0]
                                uvn[0] += 1
                                isc = ti >= NT
                                for j in range(2):
                                    PE.op(lambda: nc.tensor.matmul(puv[:, j * 256:(j + 1) * 256], lhsT=aT[:, j, gi * 128:(gi + 1) * 128],
                                                                   rhs=(bdc if isc else bd)[:, j, :], start=True, stop=True),
                                          r=[b_aT, b_bd], w=[b_puv])
                                if isc:
                                    ACT.op(lambda: S_.copy(out=uvc[:, ti - NT, :], in_=puv[:, :]), r=[b_puv], w=[b_uvc])
                                else:
                                    u_ = ust[n % 2]
                                    for uv in range(2):
                                        src = puv[:, :].rearrange("p (g uv d) -> p g uv d", g=4, uv=2)[:, :, uv, :]
                                        if uv == 0:
                                            ACT.op(lambda: S_.copy(out=u_[:, uv, :, :], in_=src), r=[b_puv], w=[b_ust[n % 2]])
                                        else:
                                            DVE.op(lambda: V.tensor_copy(out=u_[:, uv, :, :], in_=src), r=[b_puv], w=[b_ust[n % 2]])
                                    for uv, r0 in ((0, R_U), (1, R_VF)):
                                        dst = pay_o[r0:r0 + 256, :].rearrange("(g r) (q d) -> g (r q) d", g=4, d=64)[:, ti * 128:(ti + 1) * 128, :]
                                        SP.dma(dst.rearrange("g t d -> t g d"), u_[:, uv, :, :], r=[b_ust[n % 2]], w=paybuf(r0 // 256))
                    if cb == 0:
                        SP.dma(pay_o[R_B:R_B + 256, :].rearrange("(j p) t -> p j t", p=128), bhs[:, :, 8:8 + T], r=[b_bhs], w=paybuf(R_B // 256))
                else:
                    qfin = None
                    for ti in tl:
                        p_, bp_ = tok_major(cb, ti, slot)
                        if cb == 3:
                            if ti < NT:
                                n = ti
                                ACT.op(lambda: S_.copy(out=vst[n % 2][:, :], in_=p_[:, :]), r=[bp_], w=[b_vst[n % 2]])
                                dst = pay_o[0:1024, :].rearrange("(h r) t -> h r t", h=4)[:, 128:256, :].rearrange("h r (q c) -> h (r q) c", c=128)
                                dst = dst[:, ti * 128:(ti + 1) * 128, :].rearrange("h t c -> t h c")
                                SP.dma(dst, vst[n % 2][:, :].rearrange("p (h c) -> p h c", h=4), r=[b_vst[n % 2]], w=paybuf(0, 1, 2, 3))
                            else:
                                ACT.op(lambda: S_.copy(out=vc[:, ti - NT, :, :], in_=p_[:, :].rearrange("p (h e) -> p h e", h=4)), r=[bp_], w=[b_vc])
                        elif cb == 1:
                            nf = qk_post(p_, bp_, ti, 0, qT[:, :, tcols(ti)], b_qT)
                        else:
                            if ti < NT:
                                def kdma(ti=ti):
                                    SP.dma(pay_o[0:1024, tcols(ti)].rearrange("(h r) t -> r h t", h=4)[0:128, :, :], kst[ti % 2][:, :, :],
                                           r=[b_kst[ti % 2]], w=paybuf(0, 1, 2, 3))
                                nf = qk_post(p_, bp_, ti, 1, kst[ti % 2][:, :, :], b_kst[ti % 2], after=kdma)
                            else:
                                nf = qk_post(p_, bp_, ti, 1, kTc[:, :, (ti - NT) * 128:(ti - NT + 1) * 128], b_kTc)
                        if cb in (1, 2):
                            if qfin is not None:
                                qfin()
                            qfin = nf
                    if qfin is not None:
                        qfin()
            while cc_pending:
                issue_cc(cc_pending.pop(0))
            kb.barrier()

        lam = kb.sb("lam", [128, 8], F32, lst); b_lam = Buf()
        wo = kb.sb("wo", [128, 8, D], BF16, lst); b_wo = Buf()
        subg = kb.sb("subg", [128, 128], F32, lst); b_subg = Buf()

        with ExitStack() as st:
            lv = kb.sb("lv", [128, 4, 64], F32, st); b_lv = Buf()
            lp = kb.sb("lp", [128, 2, 64], F32, st)
            SP.dma(lv[:, :, :], lamv_d[:, :, :], w=[b_lv])
            SP.dma(subg[:, :], subg_d[:, :], w=[b_subg])
            DVE.op(lambda: V.tensor_tensor(out=lp[:, :, :], in0=lv[:, :, :].rearrange("p (a b) e -> p a b e", b=2)[:, :, 0, :],
                                           in1=lv[:, :, :].rearrange("p (a b) e -> p a b e", b=2)[:, :, 1, :], op=ALU.mult), r=[b_lv], w=[b_lv])
            DVE.op(lambda: V.tensor_reduce(out=lam[:, 0:2], in_=lp[:, :, :], axis=AX.X, op=ALU.add), r=[b_lv], w=[b_lam])
            ACT.op(lambda: S_.activation(out=lam[:, 2:4], in_=lam[:, 0:2], func=AF.Exp), r=[b_lam], w=[b_lam])
            DVE.op(lambda: V.tensor_tensor(out=lam[:, 4:5], in0=lam[:, 2:3], in1=lam[:, 3:4], op=ALU.subtract), r=[b_lam], w=[b_lam])
            DVE.op(lambda: V.tensor_scalar(out=lam[:, 5:6], in0=lam[:, 4:5], scalar1=lam_init, scalar2=-1.0, op0=ALU.add, op1=ALU.mult),
                   r=[b_lam], w=[b_lam])
            DVE.op(lambda: V.tensor_scalar(out=subg[:, :], in0=subg[:, :], scalar1=(1.0 - lam_init), scalar2=None, op0=ALU.mult),
                   r=[b_subg], w=[b_subg])
            kb.barrier()

        with ExitStack() as st:
            pA = [kb.ps("pA%d" % i, [128, 512], F32, st) for i in range(2)]; b_pA = bufs(2)
            pY = kb.ps("pY", [128, 2048], F32, st); b_pY = Buf()
            zu = [kb.sb("zu%d" % i, [128, 64, 64], BF16, st) for i in range(2)]; b_zu = bufs(2)
            zv = [kb.sb("zv%d" % i, [128, 64, 64], BF16, st) for i in range(2)]; b_zv = bufs(2)
            sS = kb.sb("sS", [128, 64, 128], BF16, st); b_sS = Buf()
            dft = kb.sb("dft", [128, 3, 128], BF16, st); b_dft = Buf()
            wbt = kb.sb("wbt", [128, 128, 16], BF16, st); b_wbt = Buf()
            ACT.dma(dft[:, :, :], dft_d[:, :, :], w=[b_dft])
            ACT.dma(wbt[:, :, :], wb_d[:, :, :], w=[b_wbt])

            def load_z(g, slot):
                for r_ in range(4):
                    for (z_, bz_, r0) in ((zu[slot], b_zu[slot], R_U), (zv[slot], b_zv[slot], R_VF)):
                        src = gsrc(r_, r0 + g * 64, 64).rearrange("r (q d) -> (r q) d", d=64)
                        SP.dma(z_[32 * r_:32 * (r_ + 1), :, :], src.rearrange("(a b) d -> a b d", b=64), r=gb(r0), w=[bz_])

            load_z(0, 0)
            ppl = [kb.ps("ppl%d" % i, [128, 512], F32, st) for i in range(2)]; b_ppl = bufs(2)
            s2 = kb.sb("s2", [128, T + 16], F32, st); b_s2 = Buf()
            s4 = kb.sb("s4", [128, T + 16], F32, st); b_s4 = Buf()
            s8 = kb.sb("s8", [128, T + 16], F32, st); b_s8 = Buf()
            pmTs = kb.sb("pmT", [128, 2, T + CT], BF16, st); b_pmTs = bufs(4)
            pool_mm = []
            bdp = kb.sb("bdp", [128, 2, 128], BF16, st); b_bdp = Buf()
            pscale = kb.sb("pscale", [128, 2], F32, st); b_psc = Buf()
            invw = kb.sb("invw", [128, 2], F32, st); b_invw = Buf()
            efix = kb.sb("efix", [128, 2, 16], F32, st); b_efix = Buf()
            sel = kb.sb("sel", [128, 8], F32, st); b_sel = Buf()
            cand = kb.sb("cand", [128, 2, 4, 2, 8], BF16, st); b_cand = Buf()
            ACT.dma(pscale[:, :], pscale_d[:, :], w=[b_psc])
            ACT.dma(invw[:, :], invw_d[:, :], w=[b_invw])
            ACT.dma(efix[:, :, :], efix_d[:, :, :], w=[b_efix])
            ACT.dma(sel[:, :], sel_d[:, :], w=[b_sel])
            for r_ in range(4):
                for fl, c0 in ((0, 0), (1, T - 8)):
                    ACT.dma(cand[:, :, r_, fl, :], gsrc(r_, R_B, 256)[:, c0:c0 + 8].rearrange("(j p) t -> p j t", p=128),
                           r=gb(R_B), w=[b_cand])
            for side, fl, dsl in ((0, 1, slice(0, 8)), (1, 0, slice(T + 8, T + 16))):
                for r_ in range(4):
                    sc_ = sel[:, side * 4 + r_:side * 4 + r_ + 1]
                    if r_ == 0:
                        DVE.op(lambda: V.tensor_scalar(out=bhs[:, :, dsl], in0=cand[:, :, r_, fl, :], scalar1=sc_, scalar2=None, op0=ALU.mult),
                               r=[b_cand, b_sel], w=[b_bhs])
                    else:
                        DVE.op(lambda: V.scalar_tensor_tensor(out=bhs[:, :, dsl], in0=cand[:, :, r_, fl, :], scalar=sc_, in1=bhs[:, :, dsl],
                                                              op0=ALU.mult, op1=ALU.add), r=[b_cand, b_sel], w=[b_bhs])
            DVE.op(lambda: V.memset(bdp[:, :, :], 0.0), w=[b_bdp])
            for g in range(4):
                POOL.dma(bdp[(g % 2) * 64:(g % 2 + 1) * 64, g // 2, (g % 2) * 64:(g % 2 + 1) * 64], wp_d[g, :, :], w=[b_bdp])
            if upd:
                efixc = kb.sb("efixc", [128, 2, 16], F32, st); b_efixc = Buf()
                ACT.dma(efixc[:, :, :], efixc_d[:, :, :], w=[b_efixc])
                DVE.op(lambda: V.memset(bTc[:, :, 0:8], 0.0), w=[b_bTc])
                DVE.op(lambda: V.memset(bTc[:, :, 8 + CT:16 + CT], 0.0), w=[b_bTc])

            pln = [0]

            def pool_seq(bsrc, bb, W, fix, bfix, col0):
                E = W + 16
                for j in range(2):
                    b_ = bsrc[:, j, :]
                    pmT = pmTs[:, j, col0:col0 + W]
                    b_pmT = b_pmTs[j * 2 + (1 if col0 else 0)]
                    POOL.op(lambda: G.tensor_tensor(out=s2[:, 1:E], in0=b_[:, 0:E - 1], in1=b_[:, 1:E], op=ALU.add), r=[bb], w=[b_s2])
                    POOL.op(lambda: G.tensor_tensor(out=s4[:, 2:E - 1], in0=s2[:, 1:E - 2], in1=s2[:, 3:E], op=ALU.add), r=[b_s2], w=[b_s4])
                    if j == 0:
                        lv0, lv1 = s2, s4
                        bl0, bl1 = b_s2, b_s4
                    else:
                        POOL.op(lambda: G.tensor_tensor(out=s8[:, 4:E - 3], in0=s4[:, 2:E - 5], in1=s4[:, 6:E - 1], op=ALU.add), r=[b_s4], w=[b_s8])
                        POOL.op(lambda: G.tensor_tensor(out=s2[64:128, 8:E - 7], in0=s8[64:128, 4:E - 11], in1=s8[64:128, 12:E - 3], op=ALU.add),
                                r=[b_s8], w=[b_s2])
                        lv0, lv1 = s8, s2
                        bl0, bl1 = b_s8, b_s2
                    for half, lv_, bl_ in ((0, lv0, bl0), (1, lv1, bl1)):
                        ps_ = slice(half * 64, (half + 1) * 64)
                        POOL.op(lambda: G.tensor_tensor(out=lv_[ps_, 8:16], in0=lv_[ps_, 8:16], in1=fix[ps_, j, 0:8], op=ALU.mult), r=[bfix], w=[bl_])
                        POOL.op(lambda: G.tensor_tensor(out=lv_[ps_, W:W + 8], in0=lv_[ps_, W:W + 8], in1=fix[ps_, j, 8:16], op=ALU.mult), r=[bfix], w=[bl_])
                        DVE.op(lambda: V.scalar_tensor_tensor(out=pmT[ps_, 0:W], in0=lv_[ps_, 8:8 + W], scalar=invw[ps_, j:j + 1], in1=b_[ps_, 8:8 + W],
                                                              op0=ALU.mult, op1=ALU.subtract), r=[bl_, b_invw, bb], w=[b_pmT])
                    def mm(j=j, pmT=pmT, b_pmT=b_pmT):
                        for c0 in range(0, W, 512):
                            nn = min(512, W - c0)
                            i = pln[0] % 2
                            pln[0] += 1
                            PE.op(lambda: nc.tensor.matmul(ppl[i][:, 0:nn], lhsT=bdp[:, j, :], rhs=pmT[:, c0:c0 + nn], start=True, stop=True),
                                  r=[b_bdp, b_pmT], w=[b_ppl[i]])
                            dv = sgT[:, 2 + j, col0 + c0:col0 + c0 + nn]
                            DVE.op(lambda: V.scalar_tensor_tensor(out=dv, in0=ppl[i][:, 0:nn], scalar=pscale[:, j:j + 1], in1=dv, op0=ALU.mult, op1=ALU.mult),
                                   r=[b_ppl[i], b_psc], w=[b_sg[2 + j]])
                    pool_mm.append(mm)

            pool_seq(bhs, b_bhs, T, efix, b_efix, 0)
            if upd:
                pool_seq(bTc, b_bTc, CT, efixc, b_efixc, T)


            an = 0
            for g in range(4):
                slot = g % 2
                if g + 1 < 4:
                    load_z(g + 1, (g + 1) % 2)
                for c4 in range(16):
                    pa_, bpa_ = pA[an % 2], b_pA[an % 2]
                    an += 1
                    for ci in range(4):
                        ch = c4 * 4 + ci
                        osl = slice(ci * 128, (ci + 1) * 128)
                        PE.op(lambda: nc.tensor.matmul(pa_[0:64, osl], lhsT=zu[slot][:, :, ch], rhs=dft[:, 0, :], start=True, stop=False),
                              r=[b_zu[slot], b_dft], w=[bpa_])
                        PE.op(lambda: nc.tensor.matmul(pa_[0:64, osl], lhsT=zv[slot][:, :, ch], rhs=dft[:, 2, :], start=False, stop=True),
                              r=[b_zv[slot], b_dft], w=[bpa_])
                        PE.op(lambda: nc.tensor.matmul(pa_[64:128, osl], lhsT=zu[slot][:, :, ch], rhs=dft[:, 1, :], start=True, stop=False),
                              r=[b_zu[slot], b_dft], w=[bpa_])
                        PE.op(lambda: nc.tensor.matmul(pa_[64:128, osl], lhsT=zv[slot][:, :, ch], rhs=dft[:, 0, :], start=False, stop=True),
                              r=[b_zv[slot], b_dft], w=[bpa_])
                    dst = sS[:, c4 * 4:(c4 + 1) * 4, :]
                    src = pa_[:, :].rearrange("p (c k) -> p c k", c=4)
                    ACT.op(lambda: S_.copy(out=dst, in_=src), r=[bpa_], w=[b_sS])
                g2 = g % 2
                for k2 in range(128):
                    PE.op(lambda: nc.tensor.matmul(pY[g2 * 64:(g2 + 1) * 64, k2 * 16:(k2 + 1) * 16], lhsT=sS[:, :, k2], rhs=wbt[:, k2, :],
                                                   start=True, stop=True), r=[b_sS, b_wbt], w=[b_pY])
                if g == 2:
                    while pool_mm:
                        pool_mm.pop(0)()
                if g2 == 1:
                    j = g // 2
                    dstv = sgT[:, j, 0:T].rearrange("p (a b) -> p a b", b=128)
                    DVE.op(lambda: V.tensor_tensor(out=dstv, in0=pY[:, :].rearrange("p (b a) -> p a b", a=16), in1=dstv, op=ALU.mult),
                           r=[b_pY], w=[b_sg[j]])
            kb.barrier()

        if upd:
            with ExitStack() as st:
                pYc = kb.ps("pYc", [128, 512], F32, st); b_pYc = Buf()
                dftc = kb.sb("dftc", [128, 2, 2, CT], BF16, st); b_dftc = Buf()
                SP.dma(dftc[:, :, :, :], dftc_d[:, :, :, :], w=[b_dftc])
                for j in range(2):
                    for g2 in range(2):
                        n = 0
                        for lt in range(2):
                            for uv in range(2):
                                c0 = j * 256 + g2 * 128 + uv * 64
                                PE.op(lambda: nc.tensor.matmul(pYc[g2 * 64:(g2 + 1) * 64, 0:CT], lhsT=uvc[:, lt, c0:c0 + 64], rhs=dftc[:, uv, lt, :],
                                                               start=(n == 0), stop=(n == 3)), r=[b_uvc, b_dftc], w=[b_pYc])
                                n += 1
                    DVE.op(lambda: V.tensor_tensor(out=sgT[:, j, T:TT], in0=pYc[:, 0:CT], in1=sgT[:, j, T:TT], op=ALU.mult), r=[b_pYc], w=[b_sg[j]])
                kb.barrier()

        with ExitStack() as st:
            pS = [kb.ps("pS%d" % i, [128, 1024], F32, st) for i in range(2)]; b_pS = bufs(2)
            pO = kb.ps("pO", [128, 1536], F32, st); b_pO = Buf()
            pTq = kb.ps("pTq", [128, 1024], BF16, st); b_pTq = Buf()
            NKC = 2 + 64
            kTh = [kb.sb("kTh%d" % i, [128, NKC * 128], BF16, st) for i in range(2)]; b_kTh = bufs(2)
            vh = [kb.sb("vh%d" % i, [128, NKC, 130], BF16, st) for i in range(2)]; b_vh = bufs(2)
            pT = [kb.sb("pT%d" % i, [128, 1024], BF16, st) for i in range(3)]; b_pT = bufs(3)
            oacc = kb.sb("oacc", [128, 8, 130], F32, st); b_oacc = Buf()
            rz = kb.sb("rz", [128, 16], F32, st); b_rz = Buf()
            od = kb.sb("od", [128, 4, 128], F32, st); b_od = Buf()
            osq = kb.sb("osq", [128, 4, 128], F32, st); b_osq = Buf()
            t2 = kb.sb("t2", [128, 128], F32, st); b_t2 = Buf()
            yb = kb.sb("yb", [128, 4, 128], BF16, st); b_yb = Buf()
            for i in range(2):
                DVE.op(lambda: V.memset(vh[i][:, :, 128:130], 1.0), w=[b_vh[i]])

            def load_kv(h, slot):
                POOL.op(lambda: G.tensor_copy(out=kTh[slot][:, 0:CT], in_=kTc[:, h, :]), r=[b_kTc], w=[b_kTh[slot]])
                POOL.op(lambda: G.tensor_copy(out=vh[slot][:, 0:2, 0:128], in_=vc[:, :, h, :]), r=[b_vc], w=[b_vh[slot]])
                for r_ in range(4):
                    SP.dma(kTh[slot][:, CT + r_ * T:CT + (r_ + 1) * T], gsrc(r_, h * 256, 128),
                           r=gb(h * 256), w=[b_kTh[slot]])
                    src = gsrc(r_, h * 256 + 128, 128).rearrange("r (q c) -> (r q) c", c=128)
                    SP.dma(vh[slot][:, 2 + 16 * r_:2 + 16 * (r_ + 1), 0:128], src.rearrange("(a p) e -> p a e", p=128),
                           r=gb(h * 256), w=[b_vh[slot]])

            deferred = []

            def attend(h, slot, q0, nq, chunks):
                nqs = nq // 128
                nch = len(chunks)
                first_in_bank = set()
                seen_banks = set()
                for m in range(2):
                    for qs in range(nqs):
                        a = m * 4 + qs
                        if a // 3 not in seen_banks:
                            seen_banks.add(a // 3)
                            first_in_bank.add(a)

                def qk(i):
                    c = chunks[i]
                    for m in range(2):
                        PE.op(lambda: nc.tensor.matmul(pS[i % 2][:, m * 512:m * 512 + nq], lhsT=kTh[slot][m * 64:(m + 1) * 64, c * 128:(c + 1) * 128],
                                                       rhs=qT[m * 64:(m + 1) * 64, h, q0:q0 + nq], start=True, stop=True),
                              r=[b_kTh[slot], b_qT], w=[b_pS[i % 2]])

                def ex(i):
                    if nq == 512:
                        ACT.op(lambda: S_.activation(out=pT[i % 3][:, :], in_=pS[i % 2][:, :], func=AF.Exp, scale=0.125), r=[b_pS[i % 2]], w=[b_pT[i % 3]])
                    else:
                        ACT.op(lambda: S_.activation(out=pT[i % 3][:, :].rearrange("p (m q) -> p m q", m=2)[:, :, 0:nq],
                                                     in_=pS[i % 2][:, :].rearrange("p (m q) -> p m q", m=2)[:, :, 0:nq], func=AF.Exp, scale=0.125),
                               r=[b_pS[i % 2]], w=[b_pT[i % 3]])

                def pv(i):
                    c = chunks[i]
                    for m in range(2):
                        for qs in range(nqs):
                            a = m * 4 + qs
                            co = (a // 3) * 512 + (a % 3) * 130
                            PE.op(lambda: nc.tensor.matmul(pO[:, co:co + 130], lhsT=pT[i % 3][:, m * 512 + qs * 128:m * 512 + (qs + 1) * 128],
                                                           rhs=vh[slot][:, c, :], start=(i == 0 and a in first_in_bank),
                                                           stop=(i == nch - 1), skip_group_check=True),
                                  r=[b_pT[i % 3], b_vh[slot]], w=[b_pO])

                qk(0)
                if nch > 1:
                    qk(1)
                for i in range(nch):
                    ex(i)
                    if i + 2 < nch:
                        qk(i + 2)
                    pv(i)
                    if i == 8 and deferred:
                        deferred.pop(0)()
                while deferred:
                    deferred.pop(0)()

                DVE.op(lambda: V.tensor_copy(out=oacc[:, 0:3, :], in_=pO[:, 0:390].rearrange("p (a e) -> p a e", e=130)), r=[b_pO], w=[b_oacc])
                DVE.op(lambda: V.tensor_copy(out=oacc[:, 3:6, :], in_=pO[:, 512:902].rearrange("p (a e) -> p a e", e=130)), r=[b_pO], w=[b_oacc])
                DVE.op(lambda: V.tensor_copy(out=oacc[:, 6:8, :], in_=pO[:, 1024:1284].rearrange("p (a e) -> p a e", e=130)), r=[b_pO], w=[b_oacc])
                DVE.op(lambda: V.reciprocal(out=rz[:, 0:8], in_=oacc[:, :, 128]), r=[b_oacc], w=[b_rz])
                DVE.op(lambda: V.tensor_scalar(out=rz[:, 8:12], in0=rz[:, 4:8], scalar1=lam[:, 5:6], scalar2=None, op0=ALU.mult), r=[b_rz, b_lam], w=[b_rz])
                for qs in range(nqs):
                    DVE.op(lambda: V.tensor_scalar(out=t2[:, :], in0=oacc[:, 4 + qs, 0:128], scalar1=rz[:, 8 + qs:9 + qs], scalar2=None, op0=ALU.mult),
                           r=[b_oacc, b_rz], w=[b_t2])
                    DVE.op(lambda: V.scalar_tensor_tensor(out=od[:, qs, :], in0=oacc[:, qs, 0:128], scalar=rz[:, qs:qs + 1], in1=t2[:, :],
                                                          op0=ALU.mult, op1=ALU.add), r=[b_oacc, b_rz, b_t2], w=[b_od])
                DVE.op(lambda: V.tensor_tensor(out=osq[:, 0:nqs, :], in0=od[:, 0:nqs, :], in1=od[:, 0:nqs, :], op=ALU.mult), r=[b_od], w=[b_osq])
                DVE.op(lambda: V.tensor_reduce(out=rz[:, 12:12 + nqs], in_=osq[:, 0:nqs, :], axis=AX.X, op=ALU.add), r=[b_osq], w=[b_rz])
                DVE.op(lambda: V.tensor_scalar(out=rz[:, 12:12 + nqs], in0=rz[:, 12:12 + nqs], scalar1=1.0 / 128, scalar2=EPS, op0=ALU.mult, op1=ALU.add),
                       r=[b_rz], w=[b_rz])
                POOL.op(lambda: G.tensor_tensor(out=rz[:, 12:12 + nqs], in0=rz[:, 12:12 + nqs], in1=negh[:, 0:nqs], op=ALU.pow), r=[b_rz, b_negh], w=[b_rz])
                for qs in range(nqs):
                    DVE.op(lambda: V.scalar_tensor_tensor(out=yb[:, qs, :], in0=od[:, qs, :], scalar=rz[:, 12 + qs:13 + qs], in1=subg[:, :],
                                                          op0=ALU.mult, op1=ALU.mult), r=[b_od, b_rz, b_subg], w=[b_yb])

                def fin():
                    for qs in range(nqs):
                        PE.op(lambda: nc.tensor.transpose(out=pTq[:, qs * 128:(qs + 1) * 128], in_=yb[:, qs, :], identity=ident[:, :]),
                              r=[b_yb, b_ident], w=[b_pTq])
                    dv = sgT[:, 4 + h, q0:q0 + nq]
                    DVE.op(lambda: V.tensor_tensor(out=dv, in0=pTq[:, 0:nq], in1=dv, op=ALU.mult), r=[b_pTq], w=[b_sg[4 + h]])
                deferred.append(fin)

            load_kv(0, 0)
            for h in range(4):
                slot = h % 2
                if h + 1 < 4:
                    load_kv(h + 1, (h + 1) % 2)
                if h == 2:
                    POOL.dma(wo[:, :, :], wout_d[:, :].rearrange("(k p) c -> p k c", p=128), w=[b_wo])
                for qb_ in range(4):
                    attend(h, slot, qb_ * 512, 512, list(range(NKC)))
                    if h == 0 and qb_ == 1:
                        issue_cc([2, 3], extra=[b_kTh[0], b_vh[0], b_kTh[1], b_vh[1]])
                if upd:
                    attend(h, slot, T, CT, [0, 1])
            while deferred:
                deferred.pop(0)()
            kb.barrier()

        if DEBUG and layer == DEBUG - 1:
            SP.dma(dbg_o[:, :, :], sgT[:, :, :], r=b_sg)
            kb.barrier()
        with ExitStack() as st:
            NB_ = 3
            po = [kb.ps("po%d" % i, [128, 1024], F32, st) for i in range(NB_)]; b_po = bufs(NB_)
            xr = [kb.sb("xr%d" % i, [128, D], F32, st) for i in range(NB_)]; b_xr = bufs(NB_)
            to = [kb.sb("to%d" % i, [128, D], F32, st) for i in range(NB_)]; b_to = bufs(NB_)
            otiles = list(range(NT)) + ([NT, NT + 1] if upd else [])
            for n, ti in enumerate(otiles):
                i = n % NB_
                SP.dma(xr[i][:, :], xsrc(ti), r=xbuf(ti), w=[b_xr[i]])
                for hf in range(2):
                    for k in range(8):
                        PE.op(lambda: nc.tensor.matmul(po[i][:, hf * 512:(hf + 1) * 512], lhsT=sgT[:, k, tcols(ti)], rhs=wo[:, k, hf * 512:(hf + 1) * 512],
                                                       start=(k == 0), stop=(k == 7)), r=b_sg + [b_wo], w=[b_po[i]])
                m_ = modl if ti < NT else modc
                bm_ = b_modl if ti < NT else b_modc
                DVE.op(lambda: V.tensor_tensor(out=to[i][:, :], in0=po[i][:, :], in1=m_[:, 2 * D:3 * D], op=ALU.mult), r=[b_po[i], bm_], w=[b_to[i]])
                DVE.op(lambda: V.tensor_tensor(out=to[i][:, :], in0=to[i][:, :], in1=xr[i][:, :], op=ALU.add), r=[b_xr[i]], w=[b_to[i]])
                if ti < NT:
                    SP.dma(x_o[ti * 128:(ti + 1) * 128, :], to[i][:, :], r=[b_to[i]], w=([] if last else [b_x1[ti]]))
                else:
                    SP.dma(ctx1_d[(ti - NT) * 128:(ti - NT + 1) * 128, :], to[i][:, :], r=[b_to[i]], w=[b_ctx1[ti - NT]])
            kb.barrier()
        lst.close()
    return kb


def _rope_tables(s):
    pos = np.arange(T) + T * s
    row = (pos // 64).astype(np.float32)
    col = (pos % 64).astype(np.float32)
    half = 32
    inv = (np.float32(10000.0) ** (-np.arange(0, half, 2, dtype=np.float32) / np.float32(half))).astype(np.float32)
    ar = row[:, None] * inv[None, :]
    ac = col[:, None] * inv[None, :]
    ang = np.concatenate([ar, ar, ac, ac], -1).astype(np.float32)
    cos, sin = np.cos(ang), np.sin(ang)
    sgn = np.tile(np.concatenate([-np.ones(16), np.ones(16)]), 2).astype(np.float32)
    out = np.stack([cos, sin * sgn], 0).reshape(2, NT, 128, 64).transpose(2, 0, 1, 3)
    return np.ascontiguousarray(out.astype(np.float32))


def _consts(s):
    c = {}
    c["ident"] = np.eye(128, dtype=np.float32).astype(NPBF)
    c["rope"] = _rope_tables(s)
    k = np.arange(64)
    ang = 2 * np.pi * np.outer(k, k) / 64.0
    c["ccs"] = np.ascontiguousarray(np.stack([np.cos(ang), np.sin(ang)], 1).astype(np.float32))
    k = np.arange(128)
    ang = 2 * np.pi * np.outer(k, k) / 128.0
    c["dft128"] = np.ascontiguousarray(np.stack([np.cos(ang), np.sin(ang), -np.sin(ang)], 1).astype(np.float32).astype(NPBF))
    l1 = np.arange(64)[:, None, None]
    k2 = np.arange(128)[None, :, None]
    k1 = np.arange(16)[None, None, :]
    lp = 128 * (16 * s + k1) + k2
    ang = 2 * np.pi * ((l1 * lp) % L) / L
    c["wbt"] = np.ascontiguousarray(np.concatenate([np.cos(ang), -np.sin(ang)], 0).astype(np.float32).astype(NPBF))
    l = np.arange(CT)
    ang = 2 * np.pi * (np.outer(l, l) % CT) / CT
    cs = np.stack([np.cos(ang), -np.sin(ang)], 0).reshape(2, 2, 128, CT).transpose(2, 0, 1, 3)
    c["dftc"] = np.ascontiguousarray(cs.astype(np.float32).astype(NPBF))
    wins = [2, 4, 8, 16]
    invw = np.zeros((128, 2), np.float32)
    for g, w in enumerate(wins):
        invw[(g % 2) * 64:(g % 2 + 1) * 64, g // 2] = 1.0 / w

    def edgefix(first, last, Lseq):
        f = np.ones((128, 2, 16), np.float32)
        for g, w in enumerate(wins):
            lo, hi = w // 2, w - w // 2 - 1
            for i in range(8):
                if first:
                    t = i
                    cnt = min(t + hi, Lseq - 1) - max(t - lo, 0) + 1
                    f[(g % 2) * 64:(g % 2 + 1) * 64, g // 2, i] = w / cnt
                if last:
                    t = Lseq - 8 + i
                    cnt = min(t + hi, Lseq - 1) - max(t - lo, 0) + 1
                    f[(g % 2) * 64:(g % 2 + 1) * 64, g // 2, 8 + i] = w / cnt
        return f
    c["invw"] = invw
    c["efix"] = edgefix(s == 0, s == 3, L)
    c["efixc"] = edgefix(True, True, CT)
    sel = np.zeros((128, 8), np.float32)
    if s > 0:
        sel[:, s - 1] = 1.0
    if s < 3:
        sel[:, 4 + s + 1] = 1.0
    c["sel"] = sel
    return c


_PROGS = {}


def _prog(nlayers=2):
    if nlayers not in _PROGS:
        _PROGS[nlayers] = build_fused(nlayers)
    return _PROGS[nlayers]


def _run(kb, maps):
    need = kb.inputs
    ins = []
    for m in maps:
        d = {}
        for name, (shape, dt) in need.items():
            a = np.ascontiguousarray(m[name])
            assert tuple(a.shape) == tuple(shape), (name, a.shape, shape)
            d[name] = a
        ins.append(d)
    res = run_bass_kernel_spmd(kb.nc, ins, core_ids=list(range(8)))
    return res.results


def _maps(x, c, ctx, c_ctx, norm_g, w_mod, b_mod, w_in, w_fourier, w_pool, pool_scale, qk_norm_g, lam_vecs, subln_g, w_out):
    f = lambda a: np.ascontiguousarray(np.asarray(a, dtype=np.float32))
    x, c, ctx, c_ctx = f(x), f(c), f(ctx), f(c_ctx)
    rep = lambda v: np.ascontiguousarray(np.broadcast_to(np.asarray(v, np.float32)[:, None], (v.shape[0], 128) + v.shape[1:]))
    shared = {
        "w_mod": f(w_mod), "bmod_rep": rep(f(b_mod)), "normg_rep": rep(f(norm_g)), "w_in": f(w_in),
        "qkg_rep": rep(f(qk_norm_g)), "wf": np.ascontiguousarray(f(w_fourier).transpose(0, 2, 1, 3)),
        "w_out": f(w_out), "w_pool": f(w_pool),
        "pscale": np.ascontiguousarray(f(pool_scale).reshape(2, 2, 128).transpose(0, 2, 1)),
        "lamv_rep": rep(f(lam_vecs)), "subg_rep": rep(f(subln_g)),
    }
    consts = [_consts(s) for s in range(4)]
    maps = []
    for b in range(2):
        for s in range(4):
            m = dict(consts[s])
            m.update(shared)
            m["x"] = x[b, s * T:(s + 1) * T]
            m["ctx"] = ctx[b]
            m["cT"] = np.ascontiguousarray(np.stack([c[b], c_ctx], 0).reshape(2, 8, 128).transpose(2, 1, 0))
            maps.append(m)
    return maps


def kernel(x, c, ctx, c_ctx, norm_g, w_mod, b_mod, w_in, w_fourier, w_pool, pool_scale, qk_norm_g, lam_vecs, subln_g, w_out):
    maps = _maps(x, c, ctx, c_ctx, norm_g, w_mod, b_mod, w_in, w_fourier, w_pool, pool_scale, qk_norm_g, lam_vecs, subln_g, w_out)
    res = _run(_prog(2), maps)
    out = np.stack([np.concatenate([res[b * 4 + s]["xo"] for s in range(4)], axis=0) for b in range(2)], 0)
    return out.astype(np.float32)
```

```python
import math
from contextlib import ExitStack
import numpy as np
import ml_dtypes
import concourse.bass as bass
import concourse.mybir as mybir
from concourse.bass_utils import run_bass_kernel_spmd

F32 = mybir.dt.float32
BF16 = mybir.dt.bfloat16
AF = mybir.ActivationFunctionType
ALU = mybir.AluOpType
AX = mybir.AxisListType
NPBF = ml_dtypes.bfloat16

D = 1024
L = 8192
T = 2048
NT = 16
CT = 256
TT = T + CT
PR = 1792
EPS = 1e-6
R_U, R_VF, R_B = 1024, 1280, 1536
DEBUG = False


class Buf:
    __slots__ = ("w", "r")

    def __init__(self):
        self.w = None
        self.r = {}


def bufs(n):
    return [Buf() for _ in range(n)]


class SemObj:
    __slots__ = ("h", "cum", "qwaited")

    def __init__(self, h):
        self.h = h
        self.cum = 0
        self.qwaited = 0


class Eng:
    ROLL = 30000

    def __init__(self, kb, eng, name, same=True, ndma=0):
        self.kb, self.eng, self.name, self.same = kb, eng, name, same
        self.own = set()
        self.waited = {}
        self.nins = 0
        self._newsem()
        self.dpool = [SemObj(kb.newsem("%s_d%d" % (name, i))) for i in range(ndma)]
        self.di = 0

    def _newsem(self):
        self.cur = SemObj(self.kb.newsem("%s_s%d" % (self.name, len(self.own))))
        self.own.add(self.cur)
        self.cnt = 0

    def _wait(self, toks):
        need = {}
        for (s, v) in toks:
            if s in self.own and not self.same:
                continue
            if need.get(s, 0) < v:
                need[s] = v
        for s, v in need.items():
            if self.waited.get(s, 0) >= v:
                continue
            self.eng.wait_ge(s.h, v)
            self.waited[s] = v

    @staticmethod
    def _deps(r, w):
        toks = []
        for b in r:
            if b.w is not None:
                toks.append(b.w)
        for b in w:
            if b.w is not None:
                toks.append(b.w)
            toks.extend(b.r.items())
        return toks

    @staticmethod
    def _mark(tok, r, w):
        for b in r:
            b.r[tok[0]] = tok[1]
        for b in w:
            b.w = tok
            b.r = {}

    def op(self, fn, r=(), w=()):
        self._wait(self._deps(r, w))
        ins = fn()
        if self.cnt >= self.ROLL:
            self._newsem()
        self.cnt += 1
        ins.then_inc(self.cur.h, 1)
        self.nins += 1
        self._mark((self.cur, self.cnt), r, w)

    def dma(self, out, in_, r=(), w=(), **kw):
        self._wait(self._deps(r, w))
        slot = self.dpool[self.di % len(self.dpool)]
        self.di += 1
        if slot.cum > slot.qwaited and self.waited.get(slot, 0) < slot.cum:
            self.eng.wait_ge(slot.h, slot.cum)
            self.waited[slot] = slot.cum
        slot.qwaited = slot.cum
        self.eng.dma_start(out=out, in_=in_, **kw).then_inc(slot.h, 16)
        slot.cum += 16
        self.nins += 1
        self._mark((slot, slot.cum), r, w)

    def tok(self):
        return (self.cur, self.cnt)


class KB:
    def __init__(self):
        self.nc = bass.Bass("TRN2", target_bir_lowering=False)
        self.es = ExitStack()
        nc = self.nc
        self.PE = Eng(self, nc.tensor, "pe", same=False)
        self.ACT = Eng(self, nc.scalar, "act", ndma=8)
        self.DVE = Eng(self, nc.vector, "dve")
        self.POOL = Eng(self, nc.gpsimd, "pool", ndma=12)
        self.SP = Eng(self, nc.sync, "sp", ndma=24)
        self.engs = [self.PE, self.ACT, self.DVE, self.POOL, self.SP]
        self.inputs = {}
        self.pfx = ""

    def newsem(self, name):
        return self.es.enter_context(self.nc.semaphore(name))

    def din(self, name, shape, dt):
        self.inputs[name] = (tuple(shape), dt)
        return self.nc.dram_tensor(name, list(shape), dt, kind="ExternalInput").ap()

    def dout(self, name, shape, dt):
        return self.nc.dram_tensor(name, list(shape), dt, kind="ExternalOutput").ap()

    def sb(self, name, shape, dt, st=None):
        return (st or self.es).enter_context(self.nc.sbuf_tensor("s_" + self.pfx + name, list(shape), dt))

    def ps(self, name, shape, dt, st=None):
        return (st or self.es).enter_context(self.nc.psum_tensor("p_" + self.pfx + name, list(shape), dt))

    def barrier(self):
        toks = []
        for e in self.engs:
            if e.cnt > 0:
                toks.append(e.tok())
            for s in e.dpool:
                if s.cum > 0:
                    toks.append((s, s.cum))
        for e in self.engs:
            sv, e.same = e.same, False
            e._wait(toks)
            e.same = sv


def build_fused(nlayers=2):
    kb = KB()
    nc = kb.nc
    PE, ACT, DVE, POOL, SP = kb.PE, kb.ACT, kb.DVE, kb.POOL, kb.SP
    V, S_, G = nc.vector, nc.scalar, nc.gpsimd
    NL = 2

    x_in = kb.din("x", [T, D], F32)
    ctx_in = kb.din("ctx", [CT, D], F32)
    cT_d = kb.din("cT", [128, 8, 2], F32)
    wmod_a = kb.din("w_mod", [NL, D, 3 * D], F32)
    bmod_a = kb.din("bmod_rep", [NL, 128, 3 * D], F32)
    normg_a = kb.din("normg_rep", [NL, 128, D], F32)
    win_a = kb.din("w_in", [NL, D, 3 * D], F32)
    qkg_a = kb.din("qkg_rep", [NL, 128, 2, 64], F32)
    wf_a = kb.din("wf", [NL, 64, 4, 64], F32)
    wout_a = kb.din("w_out", [NL, D, D], F32)
    wp_a = kb.din("w_pool", [NL, 4, 64, 64], F32)
    pscale_a = kb.din("pscale", [NL, 128, 2], F32)
    lamv_a = kb.din("lamv_rep", [NL, 128, 4, 64], F32)
    subg_a = kb.din("subg_rep", [NL, 128, 128], F32)
    ident_d = kb.din("ident", [128, 128], BF16)
    rope_d = kb.din("rope", [128, 2, NT, 64], F32)
    ccs_d = kb.din("ccs", [64, 2, 64], F32)
    dft_d = kb.din("dft128", [128, 3, 128], BF16)
    wb_d = kb.din("wbt", [128, 128, 16], BF16)
    invw_d = kb.din("invw", [128, 2], F32)
    efix_d = kb.din("efix", [128, 2, 16], F32)
    sel_d = kb.din("sel", [128, 8], F32)
    dftc_d = kb.din("dftc", [128, 2, 2, CT], BF16)
    efixc_d = kb.din("efixc", [128, 2, 16], F32)
    x_out = kb.dout("xo", [T, D], F32)
    NSL = PR // 256
    payloc_t = [nc.dram_tensor("payloc%d" % l, [NSL, 256, T], BF16) for l in range(NL)]
    paygat_t = [nc.dram_tensor("paygat%d" % l, [NSL, 1024, T], BF16) for l in range(NL)]
    x1_d = nc.dram_tensor("x1s", [T, D], F32).ap()
    ctx1_d = nc.dram_tensor("ctx1s", [CT, D], F32).ap()
    b_x1 = bufs(NT)
    b_ctx1 = bufs(2)
    ccsem = SemObj(kb.newsem("ccsem"))
    if DEBUG:
        dbg_o = kb.dout("dbg", [128, 8, TT], BF16)

    ident = kb.sb("ident", [128, 128], BF16); b_ident = Buf()
    negh = kb.sb("negh", [128, 32], F32); b_negh = Buf()
    SP.dma(ident[:, :], ident_d[:, :], w=[b_ident])
    POOL.op(lambda: G.memset(negh[:, :], -0.5), w=[b_negh])

    def tcols(ti):
        return slice(ti * 128, (ti + 1) * 128)

    tiles = list(range(NT + 2))
    ntl = len(tiles)
    lat = list(range(NT))
    ctxt = [NT, NT + 1]

    for layer in range(nlayers):
        upd = (layer == 0)
        last = (layer == nlayers - 1)
        lam_init = 0.8 - 0.6 * math.exp(-0.3 * layer)
        kb.pfx = "L%d_" % layer
        x_d = x_in if layer == 0 else x1_d
        ctx_d = ctx_in if layer == 0 else ctx1_d
        x_o = x_out if last else x1_d
        wmod_d, bmod_d, normg_d, win_d = wmod_a[layer], bmod_a[layer], normg_a[layer], win_a[layer]
        qkg_d, wf_d, wout_d, wp_d = qkg_a[layer], wf_a[layer], wout_a[layer], wp_a[layer]
        pscale_d, lamv_d, subg_d = pscale_a[layer], lamv_a[layer], subg_a[layer]
        pay_o = payloc_t[layer].ap().rearrange("s r t -> (s r) t")
        pay_g = paygat_t[layer].ap()

        def gsrc(r_, row0, n, pay_g=pay_g):
            s_, off = row0 // 256, row0 % 256
            assert off + n <= 256
            return pay_g[s_, r_ * 256 + off:r_ * 256 + off + n, :]
        b_pay = {i_: [] for i_ in range(NSL)}
        b_gat = bufs(NSL)

        def paybuf(*slices, b_pay=b_pay):
            b = Buf()
            for s_ in slices:
                b_pay[s_].append(b)
            return [b]

        def gb(row0, b_gat=b_gat):
            return [b_gat[row0 // 256]]

        def issue_cc(slices, extra=(), layer=layer, b_pay=b_pay, b_gat=b_gat):
            deps = [b for s_ in slices for b in b_pay[s_]] + list(extra)
            POOL._wait(POOL._deps(deps, [b_gat[s_] for s_ in slices]))
            for s_ in slices:
                cins = G.collective_compute("AllGather", ALU.bypass, replica_groups=[[0, 1, 2, 3], [4, 5, 6, 7]],
                                            ins=[payloc_t[layer].ap()[s_, :, :].opt()], outs=[paygat_t[layer].ap()[s_, :, :].opt()], dma_qos="P3")
                cins.then_inc(ccsem.h)
                ccsem.cum += 1
                POOL._mark((ccsem, ccsem.cum), b_pay[s_], [b_gat[s_]])

        def xsrc(ti):
            if ti < NT:
                return x_d[ti * 128:(ti + 1) * 128, :]
            return ctx_d[(ti - NT) * 128:(ti - NT + 1) * 128, :]

        def xbuf(ti):
            if layer == 0:
                return []
            return [b_x1[ti]] if ti < NT else [b_ctx1[ti - NT]]

        lst = ExitStack()
        modl = kb.sb("modl", [128, 3 * D], F32, lst); b_modl = Buf()
        modc = kb.sb("modc", [128, 3 * D], F32, lst); b_modc = Buf()
        rstd = kb.sb("rstd", [128, 32], F32, lst); b_rstd = Buf()
        qkg = kb.sb("qkg", [128, 2, 64], F32, lst); b_qkg = Buf()
        bd = kb.sb("bd", [128, 2, 256], BF16, lst); b_bd = Buf()
        bdc = kb.sb("bdc", [128, 2, 256], BF16, lst)
        sgT = kb.sb("sgT", [128, 8, TT], BF16, lst); b_sg = bufs(8)
        qT = kb.sb("qT", [128, 4, TT], BF16, lst); b_qT = Buf()
        kTc = kb.sb("kTc", [128, 4, CT], BF16, lst); b_kTc = Buf()
        vc = kb.sb("vc", [128, 2, 4, 128], BF16, lst); b_vc = Buf()
        bhs = kb.sb("bhs", [128, 2, T + 16], BF16, lst); b_bhs = Buf()
        bTc = kb.sb("bTc", [128, 2, CT + 16], BF16, lst); b_bTc = Buf()
        uvc = kb.sb("uvc", [128, 2, 512], BF16, lst); b_uvc = Buf()
        SP.dma(qkg[:, :, :], qkg_d[:, :, :], w=[b_qkg])

        with ExitStack() as st:
            ptr = [kb.ps("ptr%d" % i, [128, 1024], BF16, st) for i in range(2)]; b_ptr = bufs(2)
            pp = [kb.ps("pp%d" % i, [128, 512], F32, st) for i in range(3)]; b_pp = bufs(3)
            ptq = kb.ps("ptq", [128, 1024], BF16, st); b_ptq = Buf()
            puv = kb.ps("puv", [128, 512], F32, st); b_puv = Buf()
            ppi = [0]

            def next_pp():
                i = ppi[0] % 3
                ppi[0] += 1
                return pp[i], b_pp[i]

            wst = [kb.sb("wst%d" % i, [128, 8, 512], BF16, st) for i in range(2)]; b_wst = bufs(2)
            cg = kb.sb("cg", [128, 2, 2, NT, 64], F32, st); b_cg = Buf()
            stM = ExitStack()
            cT = kb.sb("cT", [128, 8, 2], F32, stM); b_cT = Buf()
            scT = kb.sb("scT", [128, 8, 2], BF16, stM)
            scb = kb.sb("scb", [128, 2, 8, 128], BF16, stM); b_scb = Buf()
            wmb = [kb.sb("wmb%d" % i, [128, 8, 512], BF16, stM) for i in range(2)]; b_wmb = bufs(2)
            wm32 = [kb.sb("wm32_%d" % i, [128, 8, 512], F32, stM) for i in range(2)]; b_wm32 = bufs(2)
            bmod = kb.sb("bmod", [128, 3 * D], F32, stM); b_bmod = Buf()
            normg = kb.sb("normg", [128, D], F32, stM); b_normg = Buf()
            rope = kb.sb("rope", [128, 2, NT, 64], F32, stM); b_rope = Buf()
            wf = kb.sb("wf", [64, 4, 64], F32, stM); b_wf = Buf()
            ccs = kb.sb("ccs", [64, 2, 64], F32, stM); b_ccs = Buf()
            bdz = kb.sb("bdz", [128, 2, 256], F32, stM); b_bdz = Buf()

            SP.dma(cT[:, :, :], cT_d[:, :, :], w=[b_cT])
            SP.dma(bmod[:, :], bmod_d[:, :], w=[b_bmod])
            SP.dma(normg[:, :], normg_d[:, :], w=[b_normg])
            SP.dma(rope[:, :, :, :], rope_d[:, :, :, :], w=[b_rope])
            SP.dma(wf[:, :, :], wf_d[:, :, :], w=[b_wf])
            SP.dma(ccs[:, :, :], ccs_d[:, :, :], w=[b_ccs])

            ACT.op(lambda: S_.activation(out=scT[:, :, :], in_=cT[:, :, :], func=AF.Silu), r=[b_cT], w=[b_scb])
            for r_ in range(2):
                DVE.op(lambda: V.tensor_copy(out=scb[:, r_, :, :], in_=scT[:, :, r_].unsqueeze(2).to_broadcast([128, 8, 128])),
                       r=[b_scb], w=[b_scb])
            mods = [modl, modc]
            b_mods = [b_modl, b_modc]
            MW = 512
            SP.dma(wm32[0][:, :, :], wmod_d[:, 0:MW].rearrange("(k p) c -> p k c", p=128), w=[b_wm32[0]])
            for j in range(3 * D // MW):
                if j + 1 < 3 * D // MW:
                    SP.dma(wm32[(j + 1) % 2][:, :, :], wmod_d[:, (j + 1) * MW:(j + 2) * MW].rearrange("(k p) c -> p k c", p=128),
                           w=[b_wm32[(j + 1) % 2]])
                ACT.op(lambda: S_.copy(out=wmb[j % 2][:, 0:4, :], in_=wm32[j % 2][:, 0:4, :]), r=[b_wm32[j % 2]], w=[b_wmb[j % 2]])
                DVE.op(lambda: V.tensor_copy(out=wmb[j % 2][:, 4:8, :], in_=wm32[j % 2][:, 4:8, :]), r=[b_wm32[j % 2]], w=[b_wmb[j % 2]])
                for r_ in range(2):
                    p_, bp_ = next_pp()
                    for k in range(8):
                        PE.op(lambda: nc.tensor.matmul(p_[:, 0:MW], lhsT=scb[:, r_, k, :], rhs=wmb[j % 2][:, k, :], start=(k == 0), stop=(k == 7)),
                              r=[b_scb, b_wmb[j % 2]], w=[bp_])
                    DVE.op(lambda: V.tensor_tensor(out=mods[r_][:, j * MW:(j + 1) * MW], in0=p_[:, 0:MW], in1=bmod[:, j * MW:(j + 1) * MW], op=ALU.add),
                           r=[bp_, b_bmod], w=[b_mods[r_]])
            for r_ in range(2):
                DVE.op(lambda: V.scalar_tensor_tensor(out=mods[r_][:, D:2 * D], in0=mods[r_][:, D:2 * D], scalar=1.0, in1=normg[:, :],
                                                      op0=ALU.add, op1=ALU.mult), r=[b_normg], w=[b_mods[r_]])

            for w_ in range(2):
                DVE.op(lambda: V.tensor_tensor(out=cg[:, w_, 0, :, :], in0=rope[:, 0, :, :],
                                               in1=qkg[:, w_, :].unsqueeze(1).to_broadcast([128, NT, 64]), op=ALU.mult),
                       r=[b_rope, b_qkg], w=[b_cg])
                for j in range(2):
                    o_ = cg[:, w_, 1, :, :].rearrange("p t (a j i) -> p t a j i", a=2, j=2)[:, :, :, j, :]
                    i0 = rope[:, 1, :, :].rearrange("p t (a j i) -> p t a j i", a=2, j=2)[:, :, :, j, :]
                    i1 = qkg[:, w_, :].rearrange("p (a j i) -> p a j i", a=2, j=2)[:, :, 1 - j, :].unsqueeze(1).to_broadcast([128, NT, 2, 16])
                    DVE.op(lambda: V.tensor_tensor(out=o_, in0=i0, in1=i1, op=ALU.mult), r=[b_rope, b_qkg], w=[b_cg])

            fsc = 1.0 / math.sqrt(L * 64.0)
            DVE.op(lambda: V.memset(bdz[:, :, :], 0.0), w=[b_bdz])
            for g in range(4):
                j, g2 = g // 2, g % 2
                for uv in range(2):
                    p_, bp_ = next_pp()
                    PE.op(lambda: nc.tensor.matmul(p_[g2 * 64:(g2 + 1) * 64, 0:64], lhsT=ccs[:, uv, :], rhs=wf[:, g, :], start=True, stop=True),
                          r=[b_ccs, b_wf], w=[bp_])
                    DVE.op(lambda: V.tensor_copy(out=bdz[g2 * 64:(g2 + 1) * 64, j, g2 * 128 + uv * 64:g2 * 128 + (uv + 1) * 64],
                                                 in_=p_[g2 * 64:(g2 + 1) * 64, 0:64]), r=[bp_], w=[b_bdz])
            ACT.op(lambda: S_.activation(out=bd[:, :, :], in_=bdz[:, :, :], func=AF.Copy, scale=fsc), r=[b_bdz], w=[b_bd])
            if upd:
                ACT.op(lambda: S_.activation(out=bdc[:, :, :], in_=bdz[:, :, :], func=AF.Copy, scale=1.0 / math.sqrt(CT * 64.0)),
                       r=[b_bdz], w=[b_bd])

            stM.close()
            kb.barrier()
            hT = kb.sb("hT", [128, 8, TT], BF16, st); b_hT = bufs(NT + 2)
            xt = [kb.sb("xt%d" % i, [128, D], F32, st) for i in range(2)]; b_xt = bufs(2)
            ss = kb.sb("ss", [128, 32], F32, st); b_ss = Buf()
            hb = [kb.sb("hb%d" % i, [128, D], BF16, st) for i in range(2)]; b_hb = bufs(2)
            junk = hb[0]; b_junk = b_hb[0]
            scr = kb.sb("scr", [128, 2, 4, 512], F32, st)
            b_scr = [bufs(4) for _ in range(2)]
            b_sq, b_t1 = b_scr[0][0], b_scr[0][1]
            tmpf = scr[:, 0, 0:2, :].rearrange("p a b -> p (a b)")
            ssg = kb.sb("ssg", [128, 2, 32], F32, st); b_ssgs = bufs(2)
            qb = [kb.sb("qb%d" % i, [128, 512], BF16, st) for i in range(2)]; b_qb = bufs(2)
            vst = [kb.sb("vst%d" % i, [128, 512], BF16, st) for i in range(2)]; b_vst = bufs(2)
            kst = [kb.sb("kst%d" % i, [128, 4, 128], BF16, st) for i in range(2)]; b_kst = bufs(2)
            ust = [kb.sb("ust%d" % i, [128, 2, 4, 64], BF16, st) for i in range(2)]; b_ust = bufs(2)
            aT = kb.sb("aT", [128, 2, 512], BF16, st); b_aT = Buf()
            print("P1 sbuf remaining", nc.sbuf_bytes_remaining)
            xts = [xt[0][:, :], xt[1][:, :], scr[:, 1, 0:2, :].rearrange("p a b -> p (a b)"), scr[:, 1, 2:4, :].rearrange("p a b -> p (a b)")]
            b_xts = [[b_xt[0]], [b_xt[1]], [b_scr[1][0], b_scr[1][1]], [b_scr[1][2], b_scr[1][3]]]

            for n, ti in enumerate(tiles):
                SP.dma(xts[n % 4], xsrc(ti), r=xbuf(ti), w=b_xts[n % 4])
                ACT.op(lambda: S_.activation(out=junk[:, :], in_=xts[n % 4], func=AF.Square, accum_out=ss[:, n:n + 1]),
                       r=b_xts[n % 4], w=[b_junk, b_ss])
            DVE.op(lambda: V.tensor_scalar(out=ss[:, 0:ntl], in0=ss[:, 0:ntl], scalar1=1.0 / D, scalar2=EPS, op0=ALU.mult, op1=ALU.add),
                   r=[b_ss], w=[b_ss])
            POOL.op(lambda: G.tensor_tensor(out=rstd[:, 0:ntl], in0=ss[:, 0:ntl], in1=negh[:, 0:ntl], op=ALU.pow), r=[b_ss, b_negh], w=[b_rstd])

            for n, ti in enumerate(tiles):
                m_ = modl if ti < NT else modc
                bm_ = b_modl if ti < NT else b_modc
                xb = xts[n % 4]
                SP.dma(xb, xsrc(ti), r=xbuf(ti), w=b_xts[n % 4])
                tf = scr[:, 0, 2 * (n % 2):2 * (n % 2) + 2, :].rearrange("p a b -> p (a b)")
                btf = [b_scr[0][2 * (n % 2)], b_scr[0][2 * (n % 2) + 1]]
                DVE.op(lambda: V.scalar_tensor_tensor(out=tf, in0=xb, scalar=rstd[:, n:n + 1], in1=m_[:, D:2 * D],
                                                      op0=ALU.mult, op1=ALU.mult), r=b_xts[n % 4] + [b_rstd, bm_], w=btf)
                POOL.op(lambda: G.tensor_tensor(out=hb[n % 2][:, :], in0=tf, in1=m_[:, 0:D], op=ALU.add),
                        r=btf + [bm_], w=[b_hb[n % 2]])
                for k in range(8):
                    PE.op(lambda: nc.tensor.transpose(out=ptr[n % 2][:, k * 128:(k + 1) * 128], in_=hb[n % 2][:, k * 128:(k + 1) * 128],
                                                      identity=ident[:, :]), r=[b_hb[n % 2], b_ident], w=[b_ptr[n % 2]])
                ACT.op(lambda: S_.copy(out=hT[:, :, tcols(ti)], in_=ptr[n % 2][:, :].rearrange("p (k t) -> p k t", k=8)),
                       r=[b_ptr[n % 2]], w=[b_hT[ti]])

            def load_w(cb, slot):
                POOL.dma(wst[slot][:, :, :], win_d[:, cb * 512:(cb + 1) * 512].rearrange("(k p) c -> p k c", p=128), w=[b_wst[slot]])

            def tok_major(cb, ti, slot):
                p_, bp_ = next_pp()
                for k in range(8):
                    PE.op(lambda: nc.tensor.matmul(p_[:, :], lhsT=hT[:, k, tcols(ti)], rhs=wst[slot][:, k, :], start=(k == 0), stop=(k == 7)),
                          r=[b_hT[ti], b_wst[slot]], w=[bp_])
                return p_, bp_

            qkn = [0]

            def qk_post(p_, bp_, ti, which, dest, bdest, after=None):
                n = qkn[0]
                qkn[0] += 1
                q_ = qb[n % 2]
                bq_ = b_qb[n % 2]
                sq, t1, uu, ww = scr[:, n % 2, 0, :], scr[:, n % 2, 1, :], scr[:, n % 2, 2, :], scr[:, n % 2, 3, :]
                b_sq, b_t1, b_uu, b_ww = b_scr[n % 2]
                sg_ = ssg[:, n % 2, :]
                b_ssg = b_ssgs[n % 2]
                ACT.op(lambda: S_.activation(out=sq, in_=p_[:, :], func=AF.Square), r=[bp_], w=[b_sq])
                DVE.op(lambda: V.tensor_reduce(out=sg_[:, 0:8], in_=sq.rearrange("p (g e) -> p g e", e=64), axis=AX.X, op=ALU.add),
                       r=[b_sq], w=[b_ssg])
                DVE.op(lambda: V.tensor_scalar(out=sg_[:, 8:16], in0=sg_[:, 0:8], scalar1=1.0 / 64, scalar2=EPS, op0=ALU.mult, op1=ALU.add),
                       r=[b_ssg], w=[b_ssg])
                ACT.op(lambda: S_.activation(out=sg_[:, 24:32], in_=sg_[:, 8:16], func=AF.Sqrt), r=[b_ssg], w=[b_ssg])
                DVE.op(lambda: V.reciprocal(out=sg_[:, 16:24], in_=sg_[:, 24:32]), r=[b_ssg], w=[b_ssg])
                t3 = t1.rearrange("p (g e) -> p g e", e=64)
                DVE.op(lambda: V.tensor_tensor(out=t3, in0=p_[:, :].rearrange("p (g e) -> p g e", e=64),
                                               in1=sg_[:, 16:24].unsqueeze(2).to_broadcast([128, 8, 64]), op=ALU.mult),
                       r=[bp_, b_ssg], w=[b_t1])
                if ti < NT:
                    POOL.op(lambda: G.tensor_tensor(out=uu.rearrange("p (g e) -> p g e", e=64), in0=t3,
                                                    in1=cg[:, which, 0, ti, :].unsqueeze(1).to_broadcast([128, 8, 64]), op=ALU.mult),
                            r=[b_t1, b_cg], w=[b_uu])
                    t5 = t1.rearrange("p (g a j i) -> p g a j i", a=2, j=2, i=16)
                    w5 = ww.rearrange("p (g a j i) -> p g a j i", a=2, j=2, i=16)
                    s4 = cg[:, which, 1, ti, :].rearrange("p (a j i) -> p a j i", a=2, j=2)
                    for j in range(2):
                        DVE.op(lambda: V.tensor_tensor(out=w5[:, :, :, j, :], in0=t5[:, :, :, 1 - j, :],
                                                       in1=s4[:, :, j, :].unsqueeze(1).to_broadcast([128, 8, 2, 16]), op=ALU.mult),
                               r=[b_t1, b_cg], w=[b_ww])
                    DVE.op(lambda: V.tensor_tensor(out=q_[:, :], in0=uu, in1=ww, op=ALU.add), r=[b_uu, b_ww], w=[bq_])
                else:
                    DVE.op(lambda: V.tensor_tensor(out=q_[:, :].rearrange("p (g e) -> p g e", e=64), in0=t3,
                                                   in1=qkg[:, which, :].unsqueeze(1).to_broadcast([128, 8, 64]), op=ALU.mult),
                           r=[b_t1, b_qkg], w=[bq_])

                def fin():
                    for h in range(4):
                        PE.op(lambda: nc.tensor.transpose(out=ptq[:, h * 128:(h + 1) * 128], in_=q_[:, h * 128:(h + 1) * 128], identity=ident[:, :]),
                              r=[bq_, b_ident], w=[b_ptq])
                    ACT.op(lambda: S_.copy(out=dest, in_=ptq[:, 0:512].rearrange("p (h t) -> p h t", h=4)), r=[b_ptq], w=[bdest])
                    if after is not None:
                        after()
                return fin

            cx = ctxt if upd else []
            sched = [(0, lat + cx), (2, lat + ctxt), (3, lat + ctxt), (1, lat + cx), (4, lat + cx), (5, lat + cx)]
            cc_of = {0: [4, 5], 3: [6, 0, 1]}
            cc_pending = []
            load_w(sched[0][0], 0)
            uvn = [0]
            for si, (cb, tl) in enumerate(sched):
                slot = si % 2
                if si + 1 < len(sched):
                    load_w(sched[si + 1][0], (si + 1) % 2)
                if si >= 1 and cc_pending:
                    issue_cc(cc_pending.pop(0))
                if cb in cc_of:
                    cc_pending.append(cc_of[cb])
                if cb in (0, 4, 5):
                    latg = [t for t in tl if t < NT]
                    groups = [latg[i:i + 4] for i in range(0, len(latg), 4)]
                    cg_ = [t for t in tl if t >= NT]
                    if cg_:
                        groups.append(cg_)
                    for grp in groups:
                        c0 = grp[0] * 128
                        ncol = len(grp) * 128
                        for cc in range(4):
                            p_, bp_ = next_pp()
                            for k in range(8):
                                PE.op(lambda: nc.tensor.matmul(p_[:, 0:ncol], lhsT=wst[slot][:, k, cc * 128:(cc + 1) * 128], rhs=hT[:, k, c0:c0 + ncol],
                                                               start=(k == 0), stop=(k == 7)), r=[b_hT[t] for t in grp] + [b_wst[slot]], w=[bp_])
                            if cb >= 4:
                                ch = (cb - 4) * 4 + cc
                                ACT.op(lambda: S_.activation(out=sgT[:, ch, c0:c0 + ncol], in_=p_[:, 0:ncol], func=AF.Silu), r=[bp_], w=[b_sg[ch]])
                            elif cc < 2:
                                ACT.op(lambda: S_.copy(out=aT[:, cc, 0:ncol], in_=p_[:, 0:ncol]), r=[bp_], w=[b_aT])
                            else:
                                if grp[0] < NT:
                                    ACT.op(lambda: S_.copy(out=bhs[:, cc - 2, 8 + c0:8 + c0 + ncol], in_=p_[:, 0:ncol]), r=[bp_], w=[b_bhs])
                                else:
                                    ACT.op(lambda: S_.copy(out=bTc[:, cc - 2, 8:8 + CT], in_=p_[:, 0:ncol]), r=[bp_], w=[b_bTc])
                        if cb == 0:
                            for gi, ti in enumerate(grp):
                                n = uvn[0]
                                uvn[0] += 1
                                isc = ti >= NT
                                for j in range(2):
                                    PE.op(lambda: nc.tensor.matmul(puv[:, j * 256:(j + 1) * 256], lhsT=aT[:, j, gi * 128:(gi + 1) * 128],
                                                                   rhs=(bdc if isc else bd)[:, j, :], start=True, stop=True),
                                          r=[b_aT, b_bd], w=[b_puv])
                                if isc:
                                    ACT.op(lambda: S_.copy(out=uvc[:, ti - NT, :], in_=puv[:, :]), r=[b_puv], w=[b_uvc])
                                else:
                                    u_ = ust[n % 2]
                                    for uv in range(2):
                                        src = puv[:, :].rearrange("p (g uv d) -> p g uv d", g=4, uv=2)[:, :, uv, :]
                                        if uv == 0:
                                            ACT.op(lambda: S_.copy(out=u_[:, uv, :, :], in_=src), r=[b_puv], w=[b_ust[n % 2]])
                                        else:
                                            DVE.op(lambda: V.tensor_copy(out=u_[:, uv, :, :], in_=src), r=[b_puv], w=[b_ust[n % 2]])
                                    for uv, r0 in ((0, R_U), (1, R_VF)):
                                        dst = pay_o[r0:r0 + 256, :].rearrange("(g r) (q d) -> g (r q) d", g=4, d=64)[:, ti * 128:(ti + 1) * 128, :]
                                        SP.dma(dst.rearrange("g t d -> t g d"), u_[:, uv, :, :], r=[b_ust[n % 2]], w=paybuf(r0 // 256))
                    if cb == 0:
                        SP.dma(pay_o[R_B:R_B + 256, :].rearrange("(j p) t -> p j t", p=128), bhs[:, :, 8:8 + T], r=[b_bhs], w=paybuf(R_B // 256))
                else:
                    qfin = None
                    for ti in tl:
                        p_, bp_ = tok_major(cb, ti, slot)
                        if cb == 3:
                            if ti < NT:
                                n = ti
                                ACT.op(lambda: S_.copy(out=vst[n % 2][:, :], in_=p_[:, :]), r=[bp_], w=[b_vst[n % 2]])
                                dst = pay_o[0:1024, :].rearrange("(h r) t -> h r t", h=4)[:, 128:256, :].rearrange("h r (q c) -> h (r q) c", c=128)
                                dst = dst[:, ti * 128:(ti + 1) * 128, :].rearrange("h t c -> t h c")
                                SP.dma(dst, vst[n % 2][:, :].rearrange("p (h c) -> p h c", h=4), r=[b_vst[n % 2]], w=paybuf(0, 1, 2, 3))
                            else:
                                ACT.op(lambda: S_.copy(out=vc[:, ti - NT, :, :], in_=p_[:, :].rearrange("p (h e) -> p h e", h=4)), r=[bp_], w=[b_vc])
                        elif cb == 1:
                            nf = qk_post(p_, bp_, ti, 0, qT[:, :, tcols(ti)], b_qT)
                        else:
                            if ti < NT:
                                def kdma(ti=ti):
                                    SP.dma(pay_o[0:1024, tcols(ti)].rearrange("(h r) t -> r h t", h=4)[0:128, :, :], kst[ti % 2][:, :, :],
                                           r=[b_kst[ti % 2]], w=paybuf(0, 1, 2, 3))
                                nf = qk_post(p_, bp_, ti, 1, kst[ti % 2][:, :, :], b_kst[ti % 2], after=kdma)
                            else:
                                nf = qk_post(p_, bp_, ti, 1, kTc[:, :, (ti - NT) * 128:(ti - NT + 1) * 128], b_kTc)
                        if cb in (1, 2):
                            if qfin is not None:
                                qfin()
                            qfin = nf
                    if qfin is not None:
                        qfin()
            while cc_pending:
                issue_cc(cc_pending.pop(0))
            kb.barrier()

        lam = kb.sb("lam", [128, 8], F32, lst); b_lam = Buf()
        wo = kb.sb("wo", [128, 8, D], BF16, lst); b_wo = Buf()
        subg = kb.sb("subg", [128, 128], F32, lst); b_subg = Buf()

        with ExitStack() as st:
            lv = kb.sb("lv", [128, 4, 64], F32, st); b_lv = Buf()
            lp = kb.sb("lp", [128, 2, 64], F32, st)
            SP.dma(lv[:, :, :], lamv_d[:, :, :], w=[b_lv])
            SP.dma(subg[:, :], subg_d[:, :], w=[b_subg])
            DVE.op(lambda: V.tensor_tensor(out=lp[:, :, :], in0=lv[:, :, :].rearrange("p (a b) e -> p a b e", b=2)[:, :, 0, :],
                                           in1=lv[:, :, :].rearrange("p (a b) e -> p a b e", b=2)[:, :, 1, :], op=ALU.mult), r=[b_lv], w=[b_lv])
            DVE.op(lambda: V.tensor_reduce(out=lam[:, 0:2], in_=lp[:, :, :], axis=AX.X, op=ALU.add), r=[b_lv], w=[b_lam])
            ACT.op(lambda: S_.activation(out=lam[:, 2:4], in_=lam[:, 0:2], func=AF.Exp), r=[b_lam], w=[b_lam])
            DVE.op(lambda: V.tensor_tensor(out=lam[:, 4:5], in0=lam[:, 2:3], in1=lam[:, 3:4], op=ALU.subtract), r=[b_lam], w=[b_lam])
            DVE.op(lambda: V.tensor_scalar(out=lam[:, 5:6], in0=lam[:, 4:5], scalar1=lam_init, scalar2=-1.0, op0=ALU.add, op1=ALU.mult),
                   r=[b_lam], w=[b_lam])
            DVE.op(lambda: V.tensor_scalar(out=subg[:, :], in0=subg[:, :], scalar1=(1.0 - lam_init), scalar2=None, op0=ALU.mult),
                   r=[b_subg], w=[b_subg])
            pA = [kb.ps("pA%d" % i, [128, 512], F32, st) for i in range(2)]; b_pA = bufs(2)
            pY = kb.ps("pY", [128, 2048], F32, st); b_pY = Buf()
            zu = [kb.sb("zu%d" % i, [128, 64, 64], BF16, st) for i in range(2)]; b_zu = bufs(2)
            zv = [kb.sb("zv%d" % i, [128, 64, 64], BF16, st) for i in range(2)]; b_zv = bufs(2)
            sS = kb.sb("sS", [128, 64, 128], BF16, st); b_sS = Buf()
            dft = kb.sb("dft", [128, 3, 128], BF16, st); b_dft = Buf()
            wbt = kb.sb("wbt", [128, 128, 16], BF16, st); b_wbt = Buf()
            ACT.dma(dft[:, :, :], dft_d[:, :, :], w=[b_dft])
            ACT.dma(wbt[:, :, :], wb_d[:, :, :], w=[b_wbt])

            def load_z(g, slot):
                for r_ in range(4):
                    for (z_, bz_, r0) in ((zu[slot], b_zu[slot], R_U), (zv[slot], b_zv[slot], R_VF)):
                        src = gsrc(r_, r0 + g * 64, 64).rearrange("r (q d) -> (r q) d", d=64)
                        SP.dma(z_[32 * r_:32 * (r_ + 1), :, :], src.rearrange("(a b) d -> a b d", b=64), r=gb(r0), w=[bz_])

            load_z(0, 0)
            print("Fourier/pool phase sbuf remaining", nc.sbuf_bytes_remaining)
            ppl = [kb.ps("ppl%d" % i, [128, 512], F32, st) for i in range(2)]; b_ppl = bufs(2)
            s2 = kb.sb("s2", [128, T + 16], F32, st); b_s2 = Buf()
            s4 = kb.sb("s4", [128, T + 16], F32, st); b_s4 = Buf()
            s8 = kb.sb("s8", [128, T + 16], F32, st); b_s8 = Buf()
            pmTs = kb.sb("pmT", [128, 2, T + CT], BF16, st); b_pmTs = bufs(4)
            pool_mm = []
            bdp = kb.sb("bdp", [128, 2, 128], BF16, st); b_bdp = Buf()
            pscale = kb.sb("pscale", [128, 2], F32, st); b_psc = Buf()
            invw = kb.sb("invw", [128, 2], F32, st); b_invw = Buf()
            efix = kb.sb("efix", [128, 2, 16], F32, st); b_efix = Buf()
            sel = kb.sb("sel", [128, 8], F32, st); b_sel = Buf()
            cand = kb.sb("cand", [128, 2, 4, 2, 8], BF16, st); b_cand = Buf()
            ACT.dma(pscale[:, :], pscale_d[:, :], w=[b_psc])
            ACT.dma(invw[:, :], invw_d[:, :], w=[b_invw])
            ACT.dma(efix[:, :, :], efix_d[:, :, :], w=[b_efix])
            ACT.dma(sel[:, :], sel_d[:, :], w=[b_sel])
            for r_ in range(4):
                for fl, c0 in ((0, 0), (1, T - 8)):
                    ACT.dma(cand[:, :, r_, fl, :], gsrc(r_, R_B, 256)[:, c0:c0 + 8].rearrange("(j p) t -> p j t", p=128),
                           r=gb(R_B), w=[b_cand])
            for side, fl, dsl in ((0, 1, slice(0, 8)), (1, 0, slice(T + 8, T + 16))):
                for r_ in range(4):
                    sc_ = sel[:, side * 4 + r_:side * 4 + r_ + 1]
                    if r_ == 0:
                        DVE.op(lambda: V.tensor_scalar(out=bhs[:, :, dsl], in0=cand[:, :, r_, fl, :], scalar1=sc_, scalar2=None, op0=ALU.mult),
                               r=[b_cand, b_sel], w=[b_bhs])
                    else:
                        DVE.op(lambda: V.scalar_tensor_tensor(out=bhs[:, :, dsl], in0=cand[:, :, r_, fl, :], scalar=sc_, in1=bhs[:, :, dsl],
                                                              op0=ALU.mult, op1=ALU.add), r=[b_cand, b_sel], w=[b_bhs])
            DVE.op(lambda: V.memset(bdp[:, :, :], 0.0), w=[b_bdp])
            for g in range(4):
                POOL.dma(bdp[(g % 2) * 64:(g % 2 + 1) * 64, g // 2, (g % 2) * 64:(g % 2 + 1) * 64], wp_d[g, :, :], w=[b_bdp])
            if upd:
                efixc = kb.sb("efixc", [128, 2, 16], F32, st); b_efixc = Buf()
                ACT.dma(efixc[:, :, :], efixc_d[:, :, :], w=[b_efixc])
                DVE.op(lambda: V.memset(bTc[:, :, 0:8], 0.0), w=[b_bTc])
                DVE.op(lambda: V.memset(bTc[:, :, 8 + CT:16 + CT], 0.0), w=[b_bTc])

            pln = [0]

            def pool_seq(bsrc, bb, W, fix, bfix, col0):
                E = W + 16
                for j in range(2):
                    b_ = bsrc[:, j, :]
                    pmT = pmTs[:, j, col0:col0 + W]
                    b_pmT = b_pmTs[j * 2 + (1 if col0 else 0)]
                    POOL.op(lambda: G.tensor_tensor(out=s2[:, 1:E], in0=b_[:, 0:E - 1], in1=b_[:, 1:E], op=ALU.add), r=[bb], w=[b_s2])
                    POOL.op(lambda: G.tensor_tensor(out=s4[:, 2:E - 1], in0=s2[:, 1:E - 2], in1=s2[:, 3:E], op=ALU.add), r=[b_s2], w=[b_s4])
                    if j == 0:
                        lv0, lv1 = s2, s4
                        bl0, bl1 = b_s2, b_s4
                    else:
                        POOL.op(lambda: G.tensor_tensor(out=s8[:, 4:E - 3], in0=s4[:, 2:E - 5], in1=s4[:, 6:E - 1], op=ALU.add), r=[b_s4], w=[b_s8])
                        POOL.op(lambda: G.tensor_tensor(out=s2[64:128, 8:E - 7], in0=s8[64:128, 4:E - 11], in1=s8[64:128, 12:E - 3], op=ALU.add),
                                r=[b_s8], w=[b_s2])
                        lv0, lv1 = s8, s2
                        bl0, bl1 = b_s8, b_s2
                    for half, lv_, bl_ in ((0, lv0, bl0), (1, lv1, bl1)):
                        ps_ = slice(half * 64, (half + 1) * 64)
                        POOL.op(lambda: G.tensor_tensor(out=lv_[ps_, 8:16], in0=lv_[ps_, 8:16], in1=fix[ps_, j, 0:8], op=ALU.mult), r=[bfix], w=[bl_])
                        POOL.op(lambda: G.tensor_tensor(out=lv_[ps_, W:W + 8], in0=lv_[ps_, W:W + 8], in1=fix[ps_, j, 8:16], op=ALU.mult), r=[bfix], w=[bl_])
                        DVE.op(lambda: V.scalar_tensor_tensor(out=pmT[ps_, 0:W], in0=lv_[ps_, 8:8 + W], scalar=invw[ps_, j:j + 1], in1=b_[ps_, 8:8 + W],
                                                              op0=ALU.mult, op1=ALU.subtract), r=[bl_, b_invw, bb], w=[b_pmT])
                    def mm(j=j, pmT=pmT, b_pmT=b_pmT):
                        for c0 in range(0, W, 512):
                            nn = min(512, W - c0)
                            i = pln[0] % 2
                            pln[0] += 1
                            PE.op(lambda: nc.tensor.matmul(ppl[i][:, 0:nn], lhsT=bdp[:, j, :], rhs=pmT[:, c0:c0 + nn], start=True, stop=True),
                                  r=[b_bdp, b_pmT], w=[b_ppl[i]])
                            dv = sgT[:, 2 + j, col0 + c0:col0 + c0 + nn]
                            DVE.op(lambda: V.scalar_tensor_tensor(out=dv, in0=ppl[i][:, 0:nn], scalar=pscale[:, j:j + 1], in1=dv, op0=ALU.mult, op1=ALU.mult),
                                   r=[b_ppl[i], b_psc], w=[b_sg[2 + j]])
                    pool_mm.append(mm)

            pool_seq(bhs, b_bhs, T, efix, b_efix, 0)
            if upd:
                pool_seq(bTc, b_bTc, CT, efixc, b_efixc, T)


            an = 0
            for g in range(4):
                slot = g % 2
                if g + 1 < 4:
                    load_z(g + 1, (g + 1) % 2)
                for c4 in range(16):
                    pa_, bpa_ = pA[an % 2], b_pA[an % 2]
                    an += 1
                    for ci in range(4):
                        ch = c4 * 4 + ci
                        osl = slice(ci * 128, (ci + 1) * 128)
                        PE.op(lambda: nc.tensor.matmul(pa_[0:64, osl], lhsT=zu[slot][:, :, ch], rhs=dft[:, 0, :], start=True, stop=False),
                              r=[b_zu[slot], b_dft], w=[bpa_])
                        PE.op(lambda: nc.tensor.matmul(pa_[0:64, osl], lhsT=zv[slot][:, :, ch], rhs=dft[:, 2, :], start=False, stop=True),
                              r=[b_zv[slot], b_dft], w=[bpa_])
                        PE.op(lambda: nc.tensor.matmul(pa_[64:128, osl], lhsT=zu[slot][:, :, ch], rhs=dft[:, 1, :], start=True, stop=False),
                              r=[b_zu[slot], b_dft], w=[bpa_])
                        PE.op(lambda: nc.tensor.matmul(pa_[64:128, osl], lhsT=zv[slot][:, :, ch], rhs=dft[:, 0, :], start=False, stop=True),
                              r=[b_zv[slot], b_dft], w=[bpa_])
                    dst = sS[:, c4 * 4:(c4 + 1) * 4, :]
                    src = pa_[:, :].rearrange("p (c k) -> p c k", c=4)
                    ACT.op(lambda: S_.copy(out=dst, in_=src), r=[bpa_], w=[b_sS])
                g2 = g % 2
                for k2 in range(128):
                    PE.op(lambda: nc.tensor.matmul(pY[g2 * 64:(g2 + 1) * 64, k2 * 16:(k2 + 1) * 16], lhsT=sS[:, :, k2], rhs=wbt[:, k2, :],
                                                   start=True, stop=True), r=[b_sS, b_wbt], w=[b_pY])
                if g == 2:
                    while pool_mm:
                        pool_mm.pop(0)()
                if g2 == 1:
                    j = g // 2
                    dstv = sgT[:, j, 0:T].rearrange("p (a b) -> p a b", b=128)
                    DVE.op(lambda: V.tensor_tensor(out=dstv, in0=pY[:, :].rearrange("p (b a) -> p a b", a=16), in1=dstv, op=ALU.mult),
                           r=[b_pY], w=[b_sg[j]])
            kb.barrier()

        if upd:
            with ExitStack() as st:
                pYc = kb.ps("pYc", [128, 512], F32, st); b_pYc = Buf()
                dftc = kb.sb("dftc", [128, 2, 2, CT], BF16, st); b_dftc = Buf()
                SP.dma(dftc[:, :, :, :], dftc_d[:, :, :, :], w=[b_dftc])
                for j in range(2):
                    for g2 in range(2):
                        n = 0
                        for lt in range(2):
                            for uv in range(2):
                                c0 = j * 256 + g2 * 128 + uv * 64
                                PE.op(lambda: nc.tensor.matmul(pYc[g2 * 64:(g2 + 1) * 64, 0:CT], lhsT=uvc[:, lt, c0:c0 + 64], rhs=dftc[:, uv, lt, :],
                                                               start=(n == 0), stop=(n == 3)), r=[b_uvc, b_dftc], w=[b_pYc])
                                n += 1
                    DVE.op(lambda: V.tensor_tensor(out=sgT[:, j, T:TT], in0=pYc[:, 0:CT], in1=sgT[:, j, T:TT], op=ALU.mult), r=[b_pYc], w=[b_sg[j]])
                kb.barrier()

        with ExitStack() as st:
            pS = [kb.ps("pS%d" % i, [128, 1024], F32, st) for i in range(2)]; b_pS = bufs(2)
            pO = kb.ps("pO", [128, 1536], F32, st); b_pO = Buf()
            pTq = kb.ps("pTq", [128, 1024], BF16, st); b_pTq = Buf()
            NKC = 2 + 64
            kTh = [kb.sb("kTh%d" % i, [128, NKC * 128], BF16, st) for i in range(2)]; b_kTh = bufs(2)
            vh = [kb.sb("vh%d" % i, [128, NKC, 130], BF16, st) for i in range(2)]; b_vh = bufs(2)
            pT = [kb.sb("pT%d" % i, [128, 1024], BF16, st) for i in range(3)]; b_pT = bufs(3)
            oacc = kb.sb("oacc", [128, 8, 130], F32, st); b_oacc = Buf()
            rz = kb.sb("rz", [128, 16], F32, st); b_rz = Buf()
            od = kb.sb("od", [128, 4, 128], F32, st); b_od = Buf()
            osq = kb.sb("osq", [128, 4, 128], F32, st); b_osq = Buf()
            t2 = kb.sb("t2", [128, 128], F32, st); b_t2 = Buf()
            yb = kb.sb("yb", [128, 4, 128], BF16, st); b_yb = Buf()
            for i in range(2):
                DVE.op(lambda: V.memset(vh[i][:, :, 128:130], 1.0), w=[b_vh[i]])

            def load_kv(h, slot):
                POOL.op(lambda: G.tensor_copy(out=kTh[slot][:, 0:CT], in_=kTc[:, h, :]), r=[b_kTc], w=[b_kTh[slot]])
                POOL.op(lambda: G.tensor_copy(out=vh[slot][:, 0:2, 0:128], in_=vc[:, :, h, :]), r=[b_vc], w=[b_vh[slot]])
                for r_ in range(4):
                    SP.dma(kTh[slot][:, CT + r_ * T:CT + (r_ + 1) * T], gsrc(r_, h * 256, 128),
                           r=gb(h * 256), w=[b_kTh[slot]])
                    src = gsrc(r_, h * 256 + 128, 128).rearrange("r (q c) -> (r q) c", c=128)
                    SP.dma(vh[slot][:, 2 + 16 * r_:2 + 16 * (r_ + 1), 0:128], src.rearrange("(a p) e -> p a e", p=128),
                           r=gb(h * 256), w=[b_vh[slot]])

            deferred = []

            def attend(h, slot, q0, nq, chunks):
                nqs = nq // 128
                nch = len(chunks)
                first_in_bank = set()
                seen_banks = set()
                for m in range(2):
                    for qs in range(nqs):
                        a = m * 4 + qs
                        if a // 3 not in seen_banks:
                            seen_banks.add(a // 3)
                            first_in_bank.add(a)

                def qk(i):
                    c = chunks[i]
                    for m in range(2):
                        PE.op(lambda: nc.tensor.matmul(pS[i % 2][:, m * 512:m * 512 + nq], lhsT=kTh[slot][m * 64:(m + 1) * 64, c * 128:(c + 1) * 128],
                                                       rhs=qT[m * 64:(m + 1) * 64, h, q0:q0 + nq], start=True, stop=True),
                              r=[b_kTh[slot], b_qT], w=[b_pS[i % 2]])

                def ex(i):
                    if nq == 512:
                        ACT.op(lambda: S_.activation(out=pT[i % 3][:, :], in_=pS[i % 2][:, :], func=AF.Exp, scale=0.125), r=[b_pS[i % 2]], w=[b_pT[i % 3]])
                    else:
                        ACT.op(lambda: S_.activation(out=pT[i % 3][:, :].rearrange("p (m q) -> p m q", m=2)[:, :, 0:nq],
                                                     in_=pS[i % 2][:, :].rearrange("p (m q) -> p m q", m=2)[:, :, 0:nq], func=AF.Exp, scale=0.125),
                               r=[b_pS[i % 2]], w=[b_pT[i % 3]])

                def pv(i):
                    c = chunks[i]
                    for m in range(2):
                        for qs in range(nqs):
                            a = m * 4 + qs
                            co = (a // 3) * 512 + (a % 3) * 130
                            PE.op(lambda: nc.tensor.matmul(pO[:, co:co + 130], lhsT=pT[i % 3][:, m * 512 + qs * 128:m * 512 + (qs + 1) * 128],
                                                           rhs=vh[slot][:, c, :], start=(i == 0 and a in first_in_bank),
                                                           stop=(i == nch - 1), skip_group_check=True),
                                  r=[b_pT[i % 3], b_vh[slot]], w=[b_pO])

                qk(0)
                if nch > 1:
                    qk(1)
                for i in range(nch):
                    ex(i)
                    if i + 2 < nch:
                        qk(i + 2)
                    pv(i)
                    if i == 8 and deferred:
                        deferred.pop(0)()
                while deferred:
                    deferred.pop(0)()

                DVE.op(lambda: V.tensor_copy(out=oacc[:, 0:3, :], in_=pO[:, 0:390].rearrange("p (a e) -> p a e", e=130)), r=[b_pO], w=[b_oacc])
                DVE.op(lambda: V.tensor_copy(out=oacc[:, 3:6, :], in_=pO[:, 512:902].rearrange("p (a e) -> p a e", e=130)), r=[b_pO], w=[b_oacc])
                DVE.op(lambda: V.tensor_copy(out=oacc[:, 6:8, :], in_=pO[:, 1024:1284].rearrange("p (a e) -> p a e", e=130)), r=[b_pO], w=[b_oacc])
                DVE.op(lambda: V.reciprocal(out=rz[:, 0:8], in_=oacc[:, :, 128]), r=[b_oacc], w=[b_rz])
                DVE.op(lambda: V.tensor_scalar(out=rz[:, 8:12], in0=rz[:, 4:8], scalar1=lam[:, 5:6], scalar2=None, op0=ALU.mult), r=[b_rz, b_lam], w=[b_rz])
                for qs in range(nqs):
                    DVE.op(lambda: V.tensor_scalar(out=t2[:, :], in0=oacc[:, 4 + qs, 0:128], scalar1=rz[:, 8 + qs:9 + qs], scalar2=None, op0=ALU.mult),
                           r=[b_oacc, b_rz], w=[b_t2])
                    DVE.op(lambda: V.scalar_tensor_tensor(out=od[:, qs, :], in0=oacc[:, qs, 0:128], scalar=rz[:, qs:qs + 1], in1=t2[:, :],
                                                          op0=ALU.mult, op1=ALU.add), r=[b_oacc, b_rz, b_t2], w=[b_od])
                DVE.op(lambda: V.tensor_tensor(out=osq[:, 0:nqs, :], in0=od[:, 0:nqs, :], in1=od[:, 0:nqs, :], op=ALU.mult), r=[b_od], w=[b_osq])
                DVE.op(lambda: V.tensor_reduce(out=rz[:, 12:12 + nqs], in_=osq[:, 0:nqs, :], axis=AX.X, op=ALU.add), r=[b_osq], w=[b_rz])
                DVE.op(lambda: V.tensor_scalar(out=rz[:, 12:12 + nqs], in0=rz[:, 12:12 + nqs], scalar1=1.0 / 128, scalar2=EPS, op0=ALU.mult, op1=ALU.add),
                       r=[b_rz], w=[b_rz])
                POOL.op(lambda: G.tensor_tensor(out=rz[:, 12:12 + nqs], in0=rz[:, 12:12 + nqs], in1=negh[:, 0:nqs], op=ALU.pow), r=[b_rz, b_negh], w=[b_rz])
                for qs in range(nqs):
                    DVE.op(lambda: V.scalar_tensor_tensor(out=yb[:, qs, :], in0=od[:, qs, :], scalar=rz[:, 12 + qs:13 + qs], in1=subg[:, :],
                                                          op0=ALU.mult, op1=ALU.mult), r=[b_od, b_rz, b_subg], w=[b_yb])

                def fin():
                    for qs in range(nqs):
                        PE.op(lambda: nc.tensor.transpose(out=pTq[:, qs * 128:(qs + 1) * 128], in_=yb[:, qs, :], identity=ident[:, :]),
                              r=[b_yb, b_ident], w=[b_pTq])
                    dv = sgT[:, 4 + h, q0:q0 + nq]
                    DVE.op(lambda: V.tensor_tensor(out=dv, in0=pTq[:, 0:nq], in1=dv, op=ALU.mult), r=[b_pTq], w=[b_sg[4 + h]])
                deferred.append(fin)

            load_kv(0, 0)
            for h in range(4):
                slot = h % 2
                if h + 1 < 4:
                    load_kv(h + 1, (h + 1) % 2)
                if h == 2:
                    POOL.dma(wo[:, :, :], wout_d[:, :].rearrange("(k p) c -> p k c", p=128), w=[b_wo])
                for qb_ in range(4):
                    attend(h, slot, qb_ * 512, 512, list(range(NKC)))
                    if h == 0 and qb_ == 1:
                        issue_cc([2, 3], extra=[b_kTh[0], b_vh[0], b_kTh[1], b_vh[1]])
                if upd:
                    attend(h, slot, T, CT, [0, 1])
            while deferred:
                deferred.pop(0)()
            kb.barrier()

        if DEBUG and layer == DEBUG - 1:
            SP.dma(dbg_o[:, :, :], sgT[:, :, :], r=b_sg)
            kb.barrier()
        with ExitStack() as st:
            NB_ = 3
            po = [kb.ps("po%d" % i, [128, 1024], F32, st) for i in range(NB_)]; b_po = bufs(NB_)
            xr = [kb.sb("xr%d" % i, [128, D], F32, st) for i in range(NB_)]; b_xr = bufs(NB_)
            to = [kb.sb("to%d" % i, [128, D], F32, st) for i in range(NB_)]; b_to = bufs(NB_)
            otiles = list(range(NT)) + ([NT, NT + 1] if upd else [])
            for n, ti in enumerate(otiles):
                i = n % NB_
                SP.dma(xr[i][:, :], xsrc(ti), r=xbuf(ti), w=[b_xr[i]])
                for hf in range(2):
                    for k in range(8):
                        PE.op(lambda: nc.tensor.matmul(po[i][:, hf * 512:(hf + 1) * 512], lhsT=sgT[:, k, tcols(ti)], rhs=wo[:, k, hf * 512:(hf + 1) * 512],
                                                       start=(k == 0), stop=(k == 7)), r=b_sg + [b_wo], w=[b_po[i]])
                m_ = modl if ti < NT else modc
                bm_ = b_modl if ti < NT else b_modc
                DVE.op(lambda: V.tensor_tensor(out=to[i][:, :], in0=po[i][:, :], in1=m_[:, 2 * D:3 * D], op=ALU.mult), r=[b_po[i], bm_], w=[b_to[i]])
                DVE.op(lambda: V.tensor_tensor(out=to[i][:, :], in0=to[i][:, :], in1=xr[i][:, :], op=ALU.add), r=[b_xr[i]], w=[b_to[i]])
                if ti < NT:
                    SP.dma(x_o[ti * 128:(ti + 1) * 128, :], to[i][:, :], r=[b_to[i]], w=([] if last else [b_x1[ti]]))
                else:
                    SP.dma(ctx1_d[(ti - NT) * 128:(ti - NT + 1) * 128, :], to[i][:, :], r=[b_to[i]], w=[b_ctx1[ti - NT]])
            kb.barrier()
        lst.close()
    return kb


def _rope_tables(s):
    pos = np.arange(T) + T * s
    row = (pos // 64).astype(np.float32)
    col = (pos % 64).astype(np.float32)
    half = 32
    inv = (np.float32(10000.0) ** (-np.arange(0, half, 2, dtype=np.float32) / np.float32(half))).astype(np.float32)
    ar = row[:, None] * inv[None, :]
    ac = col[:, None] * inv[None, :]
    ang = np.concatenate([ar, ar, ac, ac], -1).astype(np.float32)
    cos, sin = np.cos(ang), np.sin(ang)
    sgn = np.tile(np.concatenate([-np.ones(16), np.ones(16)]), 2).astype(np.float32)
    out = np.stack([cos, sin * sgn], 0).reshape(2, NT, 128, 64).transpose(2, 0, 1, 3)
    return np.ascontiguousarray(out.astype(np.float32))


def _consts(s):
    c = {}
    c["ident"] = np.eye(128, dtype=np.float32).astype(NPBF)
    c["rope"] = _rope_tables(s)
    k = np.arange(64)
    ang = 2 * np.pi * np.outer(k, k) / 64.0
    c["ccs"] = np.ascontiguousarray(np.stack([np.cos(ang), np.sin(ang)], 1).astype(np.float32))
    k = np.arange(128)
    ang = 2 * np.pi * np.outer(k, k) / 128.0
    c["dft128"] = np.ascontiguousarray(np.stack([np.cos(ang), np.sin(ang), -np.sin(ang)], 1).astype(np.float32).astype(NPBF))
    l1 = np.arange(64)[:, None, None]
    k2 = np.arange(128)[None, :, None]
    k1 = np.arange(16)[None, None, :]
    lp = 128 * (16 * s + k1) + k2
    ang = 2 * np.pi * ((l1 * lp) % L) / L
    c["wbt"] = np.ascontiguousarray(np.concatenate([np.cos(ang), -np.sin(ang)], 0).astype(np.float32).astype(NPBF))
    l = np.arange(CT)
    ang = 2 * np.pi * (np.outer(l, l) % CT) / CT
    cs = np.stack([np.cos(ang), -np.sin(ang)], 0).reshape(2, 2, 128, CT).transpose(2, 0, 1, 3)
    c["dftc"] = np.ascontiguousarray(cs.astype(np.float32).astype(NPBF))
    wins = [2, 4, 8, 16]
    invw = np.zeros((128, 2), np.float32)
    for g, w in enumerate(wins):
        invw[(g % 2) * 64:(g % 2 + 1) * 64, g // 2] = 1.0 / w

    def edgefix(first, last, Lseq):
        f = np.ones((128, 2, 16), np.float32)
        for g, w in enumerate(wins):
            lo, hi = w // 2, w - w // 2 - 1
            for i in range(8):
                if first:
                    t = i
                    cnt = min(t + hi, Lseq - 1) - max(t - lo, 0) + 1
                    f[(g % 2) * 64:(g % 2 + 1) * 64, g // 2, i] = w / cnt
                if last:
                    t = Lseq - 8 + i
                    cnt = min(t + hi, Lseq - 1) - max(t - lo, 0) + 1
                    f[(g % 2) * 64:(g % 2 + 1) * 64, g // 2, 8 + i] = w / cnt
        return f
    c["invw"] = invw
    c["efix"] = edgefix(s == 0, s == 3, L)
    c["efixc"] = edgefix(True, True, CT)
    sel = np.zeros((128, 8), np.float32)
    if s > 0:
        sel[:, s - 1] = 1.0
    if s < 3:
        sel[:, 4 + s + 1] = 1.0
    c["sel"] = sel
    return c


_PROGS = {}


def _prog(nlayers=2):
    if nlayers not in _PROGS:
        _PROGS[nlayers] = build_fused(nlayers)
    return _PROGS[nlayers]


def _run(kb, maps):
    need = kb.inputs
    ins = []
    for m in maps:
        d = {}
        for name, (shape, dt) in need.items():
            a = np.ascontiguousarray(m[name])
            assert tuple(a.shape) == tuple(shape), (name, a.shape, shape)
            d[name] = a
        ins.append(d)
    res = run_bass_kernel_spmd(kb.nc, ins, core_ids=list(range(8)))
    return res.results


def _maps(x, c, ctx, c_ctx, norm_g, w_mod, b_mod, w_in, w_fourier, w_pool, pool_scale, qk_norm_g, lam_vecs, subln_g, w_out):
    f = lambda a: np.ascontiguousarray(np.asarray(a, dtype=np.float32))
    x, c, ctx, c_ctx = f(x), f(c), f(ctx), f(c_ctx)
    rep = lambda v: np.ascontiguousarray(np.broadcast_to(np.asarray(v, np.float32)[:, None], (v.shape[0], 128) + v.shape[1:]))
    shared = {
        "w_mod": f(w_mod), "bmod_rep": rep(f(b_mod)), "normg_rep": rep(f(norm_g)), "w_in": f(w_in),
        "qkg_rep": rep(f(qk_norm_g)), "wf": np.ascontiguousarray(f(w_fourier).transpose(0, 2, 1, 3)),
        "w_out": f(w_out), "w_pool": f(w_pool),
        "pscale": np.ascontiguousarray(f(pool_scale).reshape(2, 2, 128).transpose(0, 2, 1)),
        "lamv_rep": rep(f(lam_vecs)), "subg_rep": rep(f(subln_g)),
    }
    consts = [_consts(s) for s in range(4)]
    maps = []
    for b in range(2):
        for s in range(4):
            m = dict(consts[s])
            m.update(shared)
            m["x"] = x[b, s * T:(s + 1) * T]
            m["ctx"] = ctx[b]
            m["cT"] = np.ascontiguousarray(np.stack([c[b], c_ctx], 0).reshape(2, 8, 128).transpose(2, 1, 0))
            maps.append(m)
    return maps


def kernel(x, c, ctx, c_ctx, norm_g, w_mod, b_mod, w_in, w_fourier, w_pool, pool_scale, qk_norm_g, lam_vecs, subln_g, w_out):
    maps = _maps(x, c, ctx, c_ctx, norm_g, w_mod, b_mod, w_in, w_fourier, w_pool, pool_scale, qk_norm_g, lam_vecs, subln_g, w_out)
    res = _run(_prog(2), maps)
    out = np.stack([np.concatenate([res[b * 4 + s]["xo"] for s in range(4)], axis=0) for b in range(2)], 0)
    return out.astype(np.float32)
```
